# Optimizing a Trainium2 kernel written in Bass

```python
import math
import jax, jax.numpy as jnp
from jax import lax
import numpy as np

D_MODEL = 1024
BATCH = 8
SEQ = 4096
DEPTH = 1

HEAD_DIM = 64
RWKV_HEADS = 12
RWKV_WIDTH = RWKV_HEADS * HEAD_DIM
DECAY_LORA = 64
ICLR_LORA = 64
GATE_LORA = 128
ATTN_GROUPS = ((128, 1), (512, 4), (2048, 16))
HEADS_PER_GROUP = 4
ATTN_HEADS = HEADS_PER_GROUP * len(ATTN_GROUPS)
ATTN_WIDTH = ATTN_HEADS * HEAD_DIM
ATTN_OUT_WIDTH = HEADS_PER_GROUP * HEAD_DIM
ATTN_BLOCK = 128
ROPE_THETA = 10000.0
D_FF = -(-8 * D_MODEL // (3 * 256)) * 256
SHIFT_WIDTH = 3 * RWKV_WIDTH + DECAY_LORA + ICLR_LORA + GATE_LORA
IN_WIDTH = SHIFT_WIDTH + 3 * ATTN_WIDTH + 2 * D_MODEL
NORM_EPS = 1e-6
GN_EPS = 64e-5

kernel_name = "hybrid_rwkv7_dilated_attn_gated_block"


def _split(t, sizes):
    idx = np.cumsum(sizes)[:-1].tolist()
    return jnp.split(t, idx, axis=-1)


def rms_norm(x, g):
    xf = x.astype(jnp.float32)
    y = xf * lax.rsqrt(jnp.mean(xf * xf, axis=-1, keepdims=True) + NORM_EPS)
    return (y * g.astype(jnp.float32)).astype(x.dtype)


def modulate(h, shift, scale):
    return h * (1.0 + scale[:, None, :]) + shift[:, None, :]


def token_shift(p):
    return jnp.pad(p, ((0, 0), (1, 0), (0, 0)))[:, :-1]


def apply_rope(t, pos):
    half = t.shape[-1] // 2
    inv_freq = ROPE_THETA ** (-jnp.arange(half, dtype=jnp.float32) / half)
    ang = pos.astype(jnp.float32)[:, None] * inv_freq[None, :]
    cos, sin = jnp.cos(ang), jnp.sin(ang)
    tf = t.astype(jnp.float32)
    t1, t2 = tf[..., :half], tf[..., half:]
    return jnp.concatenate([t1 * cos - t2 * sin, t2 * cos + t1 * sin], axis=-1).astype(t.dtype)


def rwkv7_recurrence(r, w, k, v, kk, a):
    B, S, H, N = r.shape

    def step(state, inp):
        r_t, w_t, k_t, v_t, kk_t, a_t = inp
        sa = -jnp.einsum('bhvk,bhk->bhv', state, kk_t)
        state = (state * w_t[:, :, None, :]
                 + sa[..., None] * (kk_t * a_t)[:, :, None, :]
                 + v_t[..., None] * k_t[:, :, None, :])
        y = jnp.einsum('bhvk,bhk->bhv', state, r_t)
        return state, y

    xs = tuple(jnp.moveaxis(t.astype(jnp.float32), 1, 0) for t in (r, w, k, v, kk, a))
    init = jnp.zeros((B, H, N, N), jnp.float32)
    _, ys = lax.scan(step, init, xs)
    return jnp.moveaxis(ys, 0, 1)


def rwkv7_mix(pr, pk, pv, pw, pa, pg, w0, w2, a0, a2, g2, k_k, k_a, r_k, lnx_w, lnx_b, w_a):
    B, S, _ = pr.shape
    f32 = jnp.float32
    heads = lambda t: t.reshape(B, S, RWKV_HEADS, HEAD_DIM)
    wf = (w0 + jnp.tanh(pw) @ w2).astype(f32)
    decay = jnp.exp(-jnp.exp(-jax.nn.softplus(-wf) - 0.5))
    a = jax.nn.sigmoid((a0 + pa @ a2).astype(f32))
    gate = (jax.nn.sigmoid(pg) @ g2).astype(f32)
    kk = heads((pk * k_k).astype(f32))
    kk = kk / jnp.maximum(jnp.linalg.norm(kk, axis=-1, keepdims=True), 1e-12)
    k = pk.astype(f32) * (1.0 + (a - 1.0) * k_a.astype(f32))
    rf, vf = heads(pr.astype(f32)), heads(pv.astype(f32))
    y = rwkv7_recurrence(rf, heads(decay), heads(k), vf, kk, heads(a))
    mean = jnp.mean(y, axis=-1, keepdims=True)
    var = jnp.mean(jnp.square(y - mean), axis=-1, keepdims=True)
    yn = ((y - mean) * lax.rsqrt(var + GN_EPS)).reshape(B, S, RWKV_WIDTH)
    yn = yn * lnx_w.astype(f32) + lnx_b.astype(f32)
    bonus = (jnp.sum(rf * heads(k) * r_k.astype(f32), axis=-1, keepdims=True) * vf).reshape(B, S, RWKV_WIDTH)
    return ((yn + bonus) * gate).astype(pr.dtype) @ w_a


def dilated_window_attn(q, k, v, window, dilation):
    B, H, S, hd = q.shape
    span = window // dilation
    n_prev = -(-span // ATTN_BLOCK)
    chunk = dilation * ATTN_BLOCK
    s_pad = -(-S // chunk) * chunk
    L = s_pad // dilation
    nb = L // ATTN_BLOCK

    def to_blocks(t):
        t = jnp.pad(t, ((0, 0), (0, 0), (0, s_pad - S), (0, 0)))
        t = t.reshape(B, H, L, dilation, hd).transpose(0, 1, 3, 2, 4)
        return t.reshape(B, H, dilation, nb, ATTN_BLOCK, hd)

    def with_prev(t):
        padded = jnp.pad(t, ((0, 0), (0, 0), (0, 0), (n_prev, 0), (0, 0), (0, 0)))
        return jnp.concatenate([padded[:, :, :, i:i + nb] for i in range(n_prev + 1)], axis=4)

    qb, kb, vb = to_blocks(q), to_blocks(k), to_blocks(v)
    kc, vc = with_prev(kb), with_prev(vb)
    s = jnp.einsum('bhrnqd,bhrnkd->bhrnqk', qb, kc, preferred_element_type=jnp.float32)
    s = s * (HEAD_DIM ** -0.5)
    n_keys = (n_prev + 1) * ATTN_BLOCK
    qi = jnp.arange(ATTN_BLOCK)[:, None]
    kj = jnp.arange(n_keys)[None, :]
    dist = qi + n_prev * ATTN_BLOCK - kj
    key_pos = jnp.arange(nb)[:, None, None] * ATTN_BLOCK + kj[None] - n_prev * ATTN_BLOCK
    valid = (dist >= 0) & (dist <= span) & (key_pos >= 0)
    s = jnp.where(valid, s, -jnp.inf)
    lse = jax.nn.logsumexp(s, axis=-1)
    p = jnp.exp(s - lse[..., None])
    o = jnp.einsum('bhrnqk,bhrnkd->bhrnqd', p, vc.astype(jnp.float32))
    o = o.reshape(B, H, dilation, L, hd).transpose(0, 1, 3, 2, 4).reshape(B, H, s_pad, hd)[:, :, :S]
    lse = lse.reshape(B, H, dilation, L).transpose(0, 1, 3, 2).reshape(B, H, s_pad)[:, :, :S]
    return o, lse


def dilated_attention_mix(pq, pk, pv, w_b):
    B, S, _ = pq.shape
    heads = lambda t: t.reshape(B, S, ATTN_HEADS, HEAD_DIM).transpose(0, 2, 1, 3)
    pos = jnp.arange(S)
    q, k, v = apply_rope(heads(pq), pos), apply_rope(heads(pk), pos), heads(pv)
    outs, lses = [], []
    for g, (window, dilation) in enumerate(ATTN_GROUPS):
        sl = slice(g * HEADS_PER_GROUP, (g + 1) * HEADS_PER_GROUP)
        o, l = dilated_window_attn(q[:, sl], k[:, sl], v[:, sl], window, dilation)
        outs.append(o)
        lses.append(l)
    outs = jnp.stack(outs)
    wts = jax.nn.softmax(jnp.stack(lses), axis=0)
    o = jnp.sum(wts[..., None] * outs, axis=0)
    o = o.transpose(0, 2, 1, 3).reshape(B, S, ATTN_OUT_WIDTH).astype(pq.dtype)
    return o @ w_b


def setup_inputs(seed: int = 0) -> dict:
    key = jax.random.key(seed)
    ks = iter(jax.random.split(key, 32))
    f32 = jnp.float32
    nrm = lambda shape, sc: jax.random.normal(next(ks), shape, f32) * sc
    L, D, RW = DEPTH, D_MODEL, RWKV_WIDTH
    return {
        "x": nrm((BATCH, SEQ, D), 1.0),
        "c": nrm((BATCH, D), 1.0),
        "w_ada": nrm((L, D, 6 * D), 0.5 * D ** -0.5),
        "b_ada": nrm((L, 6 * D), 0.02),
        "g_pre_mix": 1.0 + nrm((L, D), 0.05),
        "g_post_mix": 1.0 + nrm((L, D), 0.05),
        "w_in": nrm((L, D, IN_WIDTH), D ** -0.5),
        "mu_shift": jax.random.uniform(next(ks), (L, SHIFT_WIDTH), f32, 0.0, 1.0),
        "w0": jax.random.uniform(next(ks), (L, RW), f32, -5.0, -0.5),
        "w2": nrm((L, DECAY_LORA, RW), 0.1 * DECAY_LORA ** -0.5),
        "a0": nrm((L, RW), 0.5),
        "a2": nrm((L, ICLR_LORA, RW), 0.5 * ICLR_LORA ** -0.5),
        "g2": nrm((L, GATE_LORA, RW), GATE_LORA ** -0.5),
        "k_k": 0.85 + nrm((L, RW), 0.05),
        "k_a": 1.0 + nrm((L, RW), 0.05),
        "r_k": nrm((L, RWKV_HEADS, HEAD_DIM), 0.1),
        "lnx_w": 1.0 + nrm((L, RW), 0.05),
        "lnx_b": nrm((L, RW), 0.02),
        "w_a": nrm((L, RW, D), RW ** -0.5),
        "w_b": nrm((L, ATTN_OUT_WIDTH, D), ATTN_OUT_WIDTH ** -0.5),
        "w_out": nrm((L, D, D), D ** -0.5),
        "g_pre_ffn": 1.0 + nrm((L, D), 0.05),
        "g_post_ffn": 1.0 + nrm((L, D), 0.05),
        "w_ffn_in": nrm((L, D, 2 * D_FF), D ** -0.5),
        "w_ffn_out": nrm((L, D_FF, D), D_FF ** -0.5),
    }


def reference(x, c, w_ada, b_ada, g_pre_mix, g_post_mix, w_in, mu_shift, w0, w2, a0, a2, g2,
              k_k, k_a, r_k, lnx_w, lnx_b, w_a, w_b, w_out, g_pre_ffn, g_post_ffn,
              w_ffn_in, w_ffn_out):
    for i in range(DEPTH):
        mod = jax.nn.silu(c) @ w_ada[i] + b_ada[i]
        shift_m, scale_m, gate_m, shift_f, scale_f, gate_f = jnp.split(mod, 6, axis=-1)

        h = modulate(rms_norm(x, g_pre_mix[i]), shift_m, scale_m)
        p = h @ w_in[i]
        p_shift, pq, pk_att, pv_att, p_ga, p_gb = _split(
            p, [SHIFT_WIDTH, ATTN_WIDTH, ATTN_WIDTH, ATTN_WIDTH, D_MODEL, D_MODEL])
        p_shift = p_shift + mu_shift[i] * (token_shift(p_shift) - p_shift)
        pr, pk, pv, pw, pa, pg = _split(
            p_shift, [RWKV_WIDTH, RWKV_WIDTH, RWKV_WIDTH, DECAY_LORA, ICLR_LORA, GATE_LORA])
        y_a = rwkv7_mix(pr, pk, pv, pw, pa, pg, w0[i], w2[i], a0[i], a2[i], g2[i],
                        k_k[i], k_a[i], r_k[i], lnx_w[i], lnx_b[i], w_a[i])
        y_b = dilated_attention_mix(pq, pk_att, pv_att, w_b[i])
        merged = jax.nn.sigmoid(p_ga) * y_a + jax.nn.sigmoid(p_gb) * y_b
        x = x + gate_m[:, None, :] * rms_norm(merged @ w_out[i], g_post_mix[i])

        h = modulate(rms_norm(x, g_pre_ffn[i]), shift_f, scale_f)
        u, gt = jnp.split(h @ w_ffn_in[i], 2, axis=-1)
        f = (jax.nn.silu(u) * gt) @ w_ffn_out[i]
        x = x + gate_f[:, None, :] * rms_norm(f, g_post_ffn[i])
    return x
```

```python
import math
from contextlib import ExitStack
import numpy as np
import ml_dtypes
import concourse.bass as bass
import concourse.mybir as mybir
from concourse.bass_utils import run_bass_kernel_spmd

F32 = mybir.dt.float32
BF16 = mybir.dt.bfloat16
AF = mybir.ActivationFunctionType
ALU = mybir.AluOpType

D = 1024
S = 4096
NC8 = 8
INW = 6912
DFF = 2816
TT = 512
NT = S // TT

PC = {}
_off = 0
for _n, _w in [("c", 8), ("b_ada", 48), ("g_pre_mix", 8), ("g_post_mix", 8), ("g_pre_ffn", 8),
               ("g_post_ffn", 8), ("mu", 20), ("w0", 6), ("a0", 6), ("k_k", 6), ("k_a", 6),
               ("r_k", 6), ("lnx_w", 6), ("lnx_b", 6), ("inv_freq", 1)]:
    PC[_n] = (_off, _w)
    _off += _w
NPAR = _off


class _Rec:
    def __init__(self):
        self.name = None
        self.args = ()
        self.kwargs = {}

    def __getattr__(self, name):
        def f(*a, **k):
            self.name, self.args, self.kwargs = name, a, k
            return self
        return f


class Buf:
    __slots__ = ("name", "w", "rs", "sem", "semval", "const", "excl")

    def __init__(self, name, const=False):
        self.excl = False
        self.name = name
        self.w = None
        self.rs = {}
        self.sem = None
        self.semval = 0
        self.const = const


class Prog:
    ENGS = ["pe", "act", "dve", "pool", "sp"]

    def __init__(self, nc):
        self.nc = nc
        self.streams = {e: [] for e in self.ENGS}
        self.cnt = {e: 0 for e in self.ENGS}
        self.sems = {e: nc.alloc_semaphore("s_" + e) for e in self.ENGS}
        self.seen = {e: {} for e in self.ENGS}
        self.dma_owners = []
        self.nbuf = 0

    def buf(self, name="b", const=False):
        self.nbuf += 1
        return Buf(f"{name}{self.nbuf}", const)

    def _deps(self, eng, reads, writes):
        evs = []
        for b in reads:
            if b.w is not None:
                evs.append(b.w)
        for b in writes:
            if b.w is not None:
                evs.append(b.w)
            evs.extend(b.rs.values())
        waits = {}
        seen = self.seen[eng]
        for (key, semh, val) in evs:
            if key == "pe" and eng == "pe":
                continue
            if seen.get(key, 0) >= val:
                continue
            if key in waits and waits[key][1] >= val:
                continue
            waits[key] = (semh, val)
        for key, (semh, val) in waits.items():
            seen[key] = val
        return list(waits.values())

    def _record(self, ev, reads, writes):
        for b in reads:
            if not b.const:
                b.rs[ev[0]] = ev
        for b in writes:
            b.w = ev
            b.rs = {}

    def op(self, eng, fn, reads=(), writes=()):
        if any(b.excl for b in reads):
            writes = list(writes) + [b for b in reads if b.excl]
            reads = [b for b in reads if not b.excl]
        waits = self._deps(eng, reads, writes)
        self.cnt[eng] += 1
        ev = (eng, self.sems[eng], self.cnt[eng])
        self.streams[eng].append((waits, fn, self.sems[eng], 1))
        self._record(ev, reads, writes)

    def dma(self, q, out, in_, owner, reads=(), writes=(), **kw):
        waits = self._deps(q, reads, writes)
        if owner.sem is None:
            owner.sem = self.nc.alloc_semaphore("d_" + owner.name)
            self.dma_owners.append(owner)
        owner.semval += 16
        ev = ("dma_" + owner.name, owner.sem, owner.semval)
        self.streams[q].append((waits, lambda e: e.dma_start(out=out, in_=in_, **kw), owner.sem, 16))
        self._record(ev, reads, writes)

    def barrier(self):
        for e in self.ENGS:
            waits = []
            for f in self.ENGS:
                if f != e and self.cnt[f] > self.seen[e].get(f, 0):
                    waits.append((self.sems[f], self.cnt[f]))
                    self.seen[e][f] = self.cnt[f]
            for o in self.dma_owners:
                key = "dma_" + o.name
                if o.semval > self.seen[e].get(key, 0):
                    waits.append((o.sem, o.semval))
                    self.seen[e][key] = o.semval
            if waits:
                self.streams[e].append((waits, None, None, 0))

    def emit(self):
        nc = self.nc
        P = self

        def run(name, e):
            for waits, fn, semh, inc in P.streams[name]:
                for (s, v) in waits:
                    e.wait_ge(s, v)
                if fn is not None:
                    ins = fn(e)
                    ins.then_inc(semh, inc)

        with nc.Block() as block:
            @block.tensor
            def _(e):
                run("pe", e)

            @block.scalar
            def _(e):
                run("act", e)

            @block.vector
            def _(e):
                run("dve", e)

            @block.gpsimd
            def _(e):
                run("pool", e)

            @block.sync
            def _(e):
                run("sp", e)


def build_nc(stage=99, dbg=False, inject=False):
    nc = bass.Bass("TRN2", target_bir_lowering=False)
    P = Prog(nc)
    dram_in = lambda name, shape: nc.dram_tensor(name, shape, F32, kind="ExternalInput").ap()
    xT = dram_in("xT", [D, S])
    params = dram_in("params", [128, NPAR])
    w_ada = dram_in("w_ada", [D, 6 * D])
    w_in = dram_in("w_in", [D, INW])
    w2 = dram_in("w2", [64, 768])
    a2 = dram_in("a2", [64, 768])
    g2 = dram_in("g2", [128, 768])
    w0row = dram_in("w0row", [1, 768])
    tmasks = dram_in("tmasks", [128, 7 * 512])
    w_a = dram_in("w_a", [768, D])
    w_b = dram_in("w_b", [256, D])
    w_out = dram_in("w_out", [D, D])
    w_ffn_in = dram_in("w_ffn_in", [D, 2 * DFF])
    w_ffn_out = dram_in("w_ffn_out", [DFF, D])
    outT = nc.dram_tensor("outT", [D, S], F32, kind="ExternalOutput").ap()
    okind = "ExternalOutput" if dbg else "Internal"
    PT = nc.dram_tensor("PT", [INW, S], BF16, kind=okind).ap()
    YA = nc.dram_tensor("YA", [768, S], BF16, kind=("ExternalInput" if inject else okind)).ap()
    OT = nc.dram_tensor("OT", [256, S], BF16, kind=okind).ap()
    X1 = nc.dram_tensor("X1", [D, S], F32, kind=okind).ap()
    WFI = nc.dram_tensor("WFI_bf", [D, 2 * DFF], BF16, kind="Internal").ap()
    WFO = nc.dram_tensor("WFO_bf", [DFF, D], BF16, kind="Internal").ap()

    es_all = ExitStack()
    sb = lambda es, name, shape, dt: es.enter_context(nc.sbuf_tensor(name, shape, dt))
    ps = [es_all.enter_context(nc.psum_tensor(f"ps{i}", [128, 512], F32)) for i in range(8)]
    psb = [P.buf(f"ps{i}") for i in range(8)]
    for b in psb:
        b.excl = True

    par = sb(es_all, "par", [128, NPAR], F32)
    modv = sb(es_all, "modv", [128, 48], F32)
    coef = sb(es_all, "coef", [128, 48], F32)
    omm = sb(es_all, "omm", [128, 20], F32)
    ones_bf = sb(es_all, "ones_bf", [128, 128], BF16)
    eps_t = sb(es_all, "eps_t", [128, 1], F32)
    b_par, b_modv, b_coef, b_const = P.buf("par"), P.buf("modv"), P.buf("coef"), P.buf("const")
    pcol = lambda n, i=None: (par[:, PC[n][0]:PC[n][0] + PC[n][1]] if i is None
                              else par[:, PC[n][0] + i:PC[n][0] + i + 1])

    P.dma("sp", par[:], params[:, :], b_par, writes=[b_par])
    P.op("pool", lambda e: e.memset(ones_bf[:], 1.0), writes=[b_const])
    P.op("pool", lambda e: e.memset(eps_t[:], 1e-6), writes=[b_const])

    with ExitStack() as es:
        sc = sb(es, "sc", [128, 8], F32)
        b_sc = P.buf("sc")
        P.op("act", lambda e: e.activation(out=sc[:], in_=pcol("c"), func=AF.Silu), reads=[b_par], writes=[b_sc])
        NB = 768
        wa_t = [sb(es, f"wa_t{i}", [128, 8, NB], F32) for i in range(2)]
        b_wa = [P.buf("wa") for _ in range(2)]
        w_ada_v = w_ada.rearrange("(kc p) n -> p kc n", p=128)
        for jb in range(8):
            t, bt = wa_t[jb % 2], b_wa[jb % 2]
            P.dma("sp", t[:], w_ada_v[:, :, jb * NB:(jb + 1) * NB], bt, writes=[bt])
            for j in range(6):
                jj = jb * 6 + j
                for kc in range(8):
                    P.op("pe", lambda e, t=t, j=j, kc=kc, jj=jj: e.matmul(
                        ps[6][:, jj:jj + 1], lhsT=t[:, kc, j * 128:(j + 1) * 128], rhs=sc[:, kc:kc + 1],
                        start=(kc == 0), stop=(kc == 7)), reads=[bt, b_sc], writes=[psb[6]])
        P.op("dve", lambda e: e.tensor_tensor(out=modv[:], in0=ps[6][:, 0:48], in1=pcol("b_ada"), op=ALU.add),
             reads=[psb[6], b_par], writes=[b_modv])
        P.op("dve", lambda e: e.scalar_tensor_tensor(out=coef[:, 0:8], in0=modv[:, 8:16], scalar=1.0, in1=pcol("g_pre_mix"),
                                                     op0=ALU.add, op1=ALU.mult), reads=[b_modv, b_par], writes=[b_coef])
        P.op("dve", lambda e: e.tensor_tensor(out=coef[:, 8:16], in0=modv[:, 16:24], in1=pcol("g_post_mix"), op=ALU.mult),
             reads=[b_modv, b_par], writes=[b_coef])
        P.op("dve", lambda e: e.scalar_tensor_tensor(out=coef[:, 16:24], in0=modv[:, 32:40], scalar=1.0, in1=pcol("g_pre_ffn"),
                                                     op0=ALU.add, op1=ALU.mult), reads=[b_modv, b_par], writes=[b_coef])
        P.op("dve", lambda e: e.tensor_tensor(out=coef[:, 24:32], in0=modv[:, 40:48], in1=pcol("g_post_ffn"), op=ALU.mult),
             reads=[b_modv, b_par], writes=[b_coef])
        P.op("dve", lambda e: e.tensor_scalar(out=omm[:], in0=pcol("mu"), scalar1=-1.0, scalar2=1.0, op0=ALU.mult, op1=ALU.add),
             reads=[b_par], writes=[b_coef])
        P.barrier()
    A_m = lambda kc: coef[:, kc:kc + 1]
    B_m = lambda kc: modv[:, kc:kc + 1]
    GM = lambda kc: coef[:, 8 + kc:9 + kc]
    A_f = lambda kc: coef[:, 16 + kc:17 + kc]
    B_f = lambda kc: modv[:, 24 + kc:25 + kc]
    GF = lambda kc: coef[:, 24 + kc:25 + kc]

    def rms_modulate(es, src_tile, b_src, dst, b_dst, dst_sl, A, Bc, tmpbufs, b_tmp, sqt, b_sq, rs, b_rs, psi):
        P.op("act", lambda e: e.activation(out=sqt[:], in_=src_tile[:], func=AF.Square), reads=[b_src], writes=[b_sq])
        for kc in range(8):
            P.op("pe", lambda e, kc=kc: e.matmul(ps[psi][:], lhsT=ones_bf[:], rhs=sqt[:, kc, :], start=(kc == 0), stop=(kc == 7)),
                 reads=[b_sq, b_const], writes=[psb[psi]])
        P.op("act", lambda e: e.activation(out=rs[:], in_=ps[psi][:], func=AF.Ln, bias=eps_t[:], scale=1.0 / D),
             reads=[psb[psi], b_const], writes=[b_rs])
        P.op("act", lambda e: e.activation(out=rs[:], in_=rs[:], func=AF.Exp, scale=-0.5), reads=[b_rs], writes=[b_rs])
        for kc in range(8):
            tb, btb = tmpbufs[kc % 2], b_tmp[kc % 2]
            P.op("dve", lambda e, kc=kc, tb=tb: e.scalar_tensor_tensor(out=tb[:], in0=src_tile[:, kc, :], scalar=A(kc), in1=rs[:],
                                                                     op0=ALU.mult, op1=ALU.mult),
                 reads=[b_src, b_rs, b_coef], writes=[btb])
            P.op("act", lambda e, kc=kc, tb=tb: e.activation(out=dst[:, kc, dst_sl], in_=tb[:], func=AF.Identity, bias=Bc(kc), scale=1.0),
                 reads=[btb, b_modv], writes=[b_dst])

    with ExitStack() as esA:
        hT = sb(esA, "hT", [128, 8, S], BF16)
        b_hT = [P.buf("hT") for _ in range(NT)]
        xT_v = xT.rearrange("(kc p) t -> p kc t", p=128)
        with ExitStack() as es:
            xt = [sb(es, f"xt{i}", [128, 8, TT], F32) for i in range(2)]
            b_xt = [P.buf("xt") for _ in range(2)]
            sqt = sb(es, "sqt", [128, 8, TT], BF16)
            b_sq = P.buf("sq")
            rs = sb(es, "rs", [128, TT], F32)
            b_rs = P.buf("rs")
            tmpb = [sb(es, f"tmpb{i}", [128, TT], F32) for i in range(2)]
            b_tmp = [P.buf("tmp") for _ in range(2)]
            for tt in range(NT):
                P.dma("sp", xt[tt % 2][:], xT_v[:, :, tt * TT:(tt + 1) * TT], b_xt[tt % 2], writes=[b_xt[tt % 2]])
                rms_modulate(es, xt[tt % 2], b_xt[tt % 2], hT, b_hT[tt], slice(tt * TT, (tt + 1) * TT), A_m, B_m,
                             tmpb, b_tmp, sqt, b_sq, rs, b_rs, 5)
            P.barrier()
        if stage >= 1:
            phaseA_proj(nc, P, esA, sb, ps, psb, hT, b_hT, w_in, PT, par, pcol, omm, b_par, b_coef)
        P.barrier()

    b_out = P.buf("outdma")
    b_wpre = P.buf("wpre")

    def precast_ffn():
        for k in range(8):
            P.dma("pool", WFI[k * 128:(k + 1) * 128, :].rearrange("p (a b) -> p a b", b=512),
                  w_ffn_in[k * 128:(k + 1) * 128, :].rearrange("p (a b) -> p a b", b=512), b_wpre, writes=[b_wpre])
        for k in range(DFF // 128):
            P.dma("pool", WFO[k * 128:(k + 1) * 128, :].rearrange("p (a b) -> p a b", b=512),
                  w_ffn_out[k * 128:(k + 1) * 128, :].rearrange("p (a b) -> p a b", b=512), b_wpre, writes=[b_wpre])

    if stage >= 5 and (stage < 2 or inject):
        precast_ffn()
    if stage >= 2 and not inject:
        phaseB_rwkv(nc, P, sb, ps, psb, PT, YA, par, pcol, b_par, w2, a2, g2, w0row, tmasks, dbg=dbg, after_consts=(precast_ffn if stage >= 5 else None))
    esCD = ExitStack()
    d1w = None
    if stage >= 4:
        stg32 = [sb(esCD, f"stg32_{i}", [128, 1024], F32) for i in range(2)]
        b_stg32 = [P.buf("stg32") for _ in range(2)]
        d1w = (load_weight_bf16(P, sb, esCD, "wa_bf", w_a, 6, D, stg32, b_stg32, eng="act"),
               load_weight_bf16(P, sb, esCD, "wb_bf", w_b, 2, D, stg32, b_stg32, eng="act"),
               load_weight_bf16(P, sb, esCD, "wo_bf", w_out, 8, D, stg32, b_stg32, eng="act"))
    if stage >= 3:
        phaseC_attn(nc, P, sb, ps, psb, PT, OT)
    if stage >= 4:
        phaseD1(nc, P, sb, ps, psb, PT, YA, OT, X1, xT, d1w, GM, ones_bf, eps_t, b_const, b_coef)
    esCD.close()
    if stage >= 5:
        phaseD2(nc, P, sb, ps, psb, X1, outT, WFI, WFO, b_wpre, A_f, B_f, GF, ones_bf, eps_t, b_const, b_coef, b_modv, b_out)
    if stage < 5:
        with ExitStack() as es:
            z = sb(es, "zt", [128, 8, 64], F32)
            bz = P.buf("z")
            P.op("pool", lambda e: e.memset(z[:], 0.0), writes=[bz])
            P.op("dve", lambda e: e.tensor_copy(out=z[:, 0, 0:48], in_=modv[:]), reads=[bz, b_modv], writes=[bz])
            P.dma("sp", outT.rearrange("(kc p) t -> p kc t", p=128)[:, :, 0:64], z[:], b_out, reads=[bz])
    P.barrier()
    P.emit()
    es_all.close()
    return nc


def phaseA_proj(nc, P, esA, sb, ps, psb, hT, b_hT, w_in, PT, par, pcol, omm, b_par, b_coef):
    with ExitStack() as es:
        NW = 3
        wstg = [sb(es, f"wstg{i}", [128, 8, 512], F32) for i in range(2)]
        b_wstg = [P.buf("wstg") for _ in range(2)]
        wbf = [sb(es, f"wbf{i}", [128, 8, 128], BF16) for i in range(NW)]
        b_wbf = [P.buf("wbf") for _ in range(NW)]
        stg = [sb(es, f"stg{i}", [128, S], BF16) for i in range(3)]
        b_stg = [P.buf("stg") for _ in range(3)]
        mup = [sb(es, f"mup{i}", [128, S + 1], F32) for i in range(2)]
        b_mup = [[P.buf("mup") for _ in range(NT + 1)] for _ in range(2)]
        cosT = sb(es, "cosT", [128, S], F32)
        sinT = sb(es, "sinT", [128, S], F32)
        perm = sb(es, "perm", [128, 128], BF16)
        ident = sb(es, "ident", [128, 128], BF16)
        qraw = [sb(es, f"qraw{i}", [128, TT], BF16) for i in range(2)]
        b_qraw = [P.buf("qraw") for _ in range(2)]
        t1 = [sb(es, f"t1_{i}", [128, TT], F32) for i in range(2)]
        b_t1 = [P.buf("t1") for _ in range(2)]
        t2 = [sb(es, f"t2_{i}", [128, TT], F32) for i in range(2)]
        b_t2 = [P.buf("t2") for _ in range(2)]
        sgn = sb(es, "sgn", [128, 1], F32)
        pi_t = sb(es, "pi_t", [128, 1], F32)
        b_tab = P.buf("tab")
        P.op("pool", lambda e: e.memset(ident[:], 1.0), writes=[b_tab])
        P.op("pool", lambda e: e.affine_select(out=ident[:], in_=ident[:], pattern=[[-1, 128]], compare_op=ALU.is_equal,
                                               fill=0.0, base=0, channel_multiplier=1), reads=[b_tab], writes=[b_tab])
        for h0 in (0, 64):
            P.op("pool", lambda e, h0=h0: e.tensor_copy(out=perm[:, h0:h0 + 32], in_=ident[:, h0 + 32:h0 + 64]), reads=[b_tab], writes=[b_tab])
            P.op("pool", lambda e, h0=h0: e.tensor_copy(out=perm[:, h0 + 32:h0 + 64], in_=ident[:, h0:h0 + 32]), reads=[b_tab], writes=[b_tab])
        for q4 in range(4):
            P.op("pool", lambda e, q4=q4: e.memset(sgn[q4 * 32:(q4 + 1) * 32, :], -1.0 if q4 % 2 == 0 else 1.0), writes=[b_tab])
        P.op("pool", lambda e: e.memset(pi_t[:], -math.pi), writes=[b_tab])
        for m in (0, 1):
            P.op("pool", lambda e, m=m: e.memset(mup[m][:, 0:1], 0.0), writes=[b_mup[m][0]])
        with ExitStack() as es2:
            ang_t = wstg[0]
            ki = mup[1][:, 1:S + 1].bitcast(mybir.dt.int32)
            kf = mup[0][:, 1:S + 1]
            P.op("pool", lambda e: e.iota(ang_t[:].rearrange("p a b -> p (a b)"), pattern=[[1, S]], base=0, channel_multiplier=0, allow_small_or_imprecise_dtypes=True),
                 reads=[b_tab], writes=[b_tab])
            P.op("dve", lambda e: e.tensor_scalar(out=ang_t[:].rearrange("p a b -> p (a b)"), in0=ang_t[:].rearrange("p a b -> p (a b)"), scalar1=pcol("inv_freq"), scalar2=1.0 / (2 * math.pi),
                                                  op0=ALU.mult, op1=ALU.mult), reads=[b_tab, b_par], writes=[b_tab])
            for (tab, addc) in ((sinT, 0.0), (cosT, 0.25)):
                P.op("dve", lambda e, tab=tab, addc=addc: e.tensor_scalar(out=tab[:], in0=ang_t[:].rearrange("p a b -> p (a b)"), scalar1=addc, scalar2=None, op0=ALU.add),
                     reads=[b_tab], writes=[b_tab])
                P.op("dve", lambda e, tab=tab: e.tensor_copy(out=ki, in_=tab[:]), reads=[b_tab], writes=[b_tab])
                P.op("dve", lambda e, tab=tab: e.tensor_copy(out=kf, in_=ki), reads=[b_tab], writes=[b_tab])
                P.op("dve", lambda e, tab=tab: e.tensor_tensor(out=tab[:], in0=tab[:], in1=kf, op=ALU.subtract), reads=[b_tab], writes=[b_tab])
                P.op("dve", lambda e, tab=tab: e.tensor_scalar(out=kf, in0=tab[:], scalar1=0.5, scalar2=None, op0=ALU.is_gt), reads=[b_tab], writes=[b_tab])
                P.op("dve", lambda e, tab=tab: e.tensor_tensor(out=tab[:], in0=tab[:], in1=kf, op=ALU.subtract), reads=[b_tab], writes=[b_tab])
                P.op("dve", lambda e, tab=tab: e.tensor_scalar(out=kf, in0=tab[:], scalar1=-0.5, scalar2=None, op0=ALU.is_lt), reads=[b_tab], writes=[b_tab])
                P.op("dve", lambda e, tab=tab: e.tensor_tensor(out=tab[:], in0=tab[:], in1=kf, op=ALU.add), reads=[b_tab], writes=[b_tab])
                P.op("act", lambda e, tab=tab: e.activation(out=tab[:], in_=tab[:], func=AF.Sin, scale=2 * math.pi - 2e-6), reads=[b_tab], writes=[b_tab])
            P.op("dve", lambda e: e.tensor_scalar(out=sinT[:], in0=sinT[:], scalar1=sgn[:], scalar2=None, op0=ALU.mult), reads=[b_tab], writes=[b_tab])
            P.barrier()
        b_tab.const = True

        w_in_v = w_in.rearrange("(kc p) n -> p kc n", p=128)
        NCH = INW // 128
        import os
        order = [int(v) for v in os.environ['KCH'].split(',')] if 'KCH' in os.environ else list(range(NCH))
        NCH = len(order)

        WG = 4

        def load_w(i):
            if i % WG == 0:
                gsl = (i // WG) % 2
                c0 = order[i] * 128
                ncol = 128 * min(WG, NCH - i)
                P.dma("sp", wstg[gsl][:, :, 0:ncol], w_in_v[:, :, c0:c0 + ncol], b_wstg[gsl], writes=[b_wstg[gsl]])
            gsl = (i // WG) % 2
            s_ = i % NW
            j = i % WG
            P.op("dve", lambda e, s_=s_, gsl=gsl, j=j: e.tensor_copy(out=wbf[s_][:], in_=wstg[gsl][:, :, j * 128:(j + 1) * 128]),
                 reads=[b_wstg[gsl]], writes=[b_wbf[s_]])

        load_w(0)
        if NCH > 1:
            load_w(1)
        pidx = 0
        ridx = 0
        for i, cc in enumerate(order):
            if i + 2 < NCH:
                load_w(i + 2)
            s = i % NW
            so = i % 3
            sg, bsg = stg[so], b_stg[so]
            mu_i = cc if cc < 20 else None
            mm = i % 2
            for tt in range(NT):
                pi = pidx % 5
                pidx += 1
                tsl = slice(tt * TT, (tt + 1) * TT)
                for kc in range(8):
                    P.op("pe", lambda e, pi=pi, s=s, kc=kc, tsl=tsl: e.matmul(ps[pi][:], lhsT=wbf[s][:, kc, :], rhs=hT[:, kc, tsl],
                                                                          start=(kc == 0), stop=(kc == 7)),
                         reads=[b_wbf[s], b_hT[tt]], writes=[psb[pi]])
                if cc < 20:
                    mcol = pcol("mu", cc)
                    ocol = omm[:, cc:cc + 1]
                    P.op("act", lambda e, pi=pi, mm=mm, tt=tt, mcol=mcol: e.activation(
                        out=mup[mm][:, 1 + tt * TT:1 + (tt + 1) * TT], in_=ps[pi][:], func=AF.Copy, scale=mcol),
                        reads=[psb[pi], b_par], writes=[b_mup[mm][tt + 1]])
                    if cc < 18:
                        P.op("dve", lambda e, pi=pi, mm=mm, tt=tt, ocol=ocol, sg=sg, tsl=tsl: e.scalar_tensor_tensor(
                            out=sg[:, tsl], in0=ps[pi][:], scalar=ocol, in1=mup[mm][:, tt * TT:(tt + 1) * TT], op0=ALU.mult, op1=ALU.add),
                            reads=[psb[pi], b_coef, b_mup[mm][tt], b_mup[mm][tt + 1]], writes=[bsg])
                    else:
                        tb, btb = t1[tt % 2], b_t1[tt % 2]
                        P.op("dve", lambda e, pi=pi, mm=mm, tt=tt, ocol=ocol, tb=tb: e.scalar_tensor_tensor(
                            out=tb[:], in0=ps[pi][:], scalar=ocol, in1=mup[mm][:, tt * TT:(tt + 1) * TT], op0=ALU.mult, op1=ALU.add),
                            reads=[psb[pi], b_coef, b_mup[mm][tt], b_mup[mm][tt + 1]], writes=[btb])
                        if cc == 18:
                            P.op("act", lambda e, tb=tb, sg=sg, tsl=tsl: e.activation(out=sg[0:64, tsl], in_=tb[0:64, :], func=AF.Tanh),
                                 reads=[btb], writes=[bsg])
                            P.op("act", lambda e, tb=tb, sg=sg, tsl=tsl: e.activation(out=sg[64:128, tsl], in_=tb[64:128, :], func=AF.Copy),
                                 reads=[btb], writes=[bsg])
                        else:
                            P.op("act", lambda e, tb=tb, sg=sg, tsl=tsl: e.activation(out=sg[:, tsl], in_=tb[:], func=AF.Sigmoid),
                                 reads=[btb], writes=[bsg])
                elif cc < 32:
                    ri = ridx % 2
                    ridx += 1
                    qr, bqr = qraw[ri], b_qraw[ri]
                    P.op("act", lambda e, pi=pi, qr=qr: e.activation(out=qr[:], in_=ps[pi][:], func=AF.Copy), reads=[psb[pi]], writes=[bqr])
                    P.op("pe", lambda e, ri=ri, qr=qr: e.matmul(ps[5 + ri][:], lhsT=perm[:], rhs=qr[:], start=True, stop=True),
                         reads=[bqr, b_tab], writes=[psb[5 + ri]])
                    P.op("dve", lambda e, pi=pi, ri=ri, tsl=tsl: e.tensor_tensor(out=t1[ri][:], in0=ps[pi][:], in1=cosT[:, tsl], op=ALU.mult),
                         reads=[psb[pi], b_tab], writes=[b_t1[ri]])
                    P.op("dve", lambda e, ri=ri, tsl=tsl: e.tensor_tensor(out=t2[ri][:], in0=ps[5 + ri][:], in1=sinT[:, tsl], op=ALU.mult),
                         reads=[psb[5 + ri], b_tab], writes=[b_t2[ri]])
                    P.op("dve", lambda e, ri=ri, sg=sg, tsl=tsl: e.tensor_tensor(out=sg[:, tsl], in0=t1[ri][:], in1=t2[ri][:], op=ALU.add),
                         reads=[b_t1[ri], b_t2[ri]], writes=[bsg])
                elif cc < 38:
                    P.op("act", lambda e, pi=pi, sg=sg, tsl=tsl: e.activation(out=sg[:, tsl], in_=ps[pi][:], func=AF.Copy),
                         reads=[psb[pi]], writes=[bsg])
                else:
                    P.op("act", lambda e, pi=pi, sg=sg, tsl=tsl: e.activation(out=sg[:, tsl], in_=ps[pi][:], func=AF.Sigmoid),
                         reads=[psb[pi]], writes=[bsg])
            P.dma("sp", PT[cc * 128:(cc + 1) * 128, :], sg[:], bsg, reads=[bsg])
        P.barrier()


def _pack_params(inp, b):
    cols = np.zeros((128, NPAR), np.float32)

    def put(name, vec):
        o, w = PC[name]
        cols[:, o:o + w] = np.asarray(vec, np.float32).reshape(w, 128).T

    put("c", inp["c"][b])
    put("b_ada", inp["b_ada"][0])
    for n in ("g_pre_mix", "g_post_mix", "g_pre_ffn", "g_post_ffn", "w0", "a0", "k_k", "k_a", "lnx_w", "lnx_b"):
        put(n, inp[n][0])
    put("mu", inp["mu_shift"][0])
    put("r_k", inp["r_k"][0].reshape(-1))
    half = 32
    inv_freq = (10000.0 ** (-np.arange(half, dtype=np.float32) / half)).astype(np.float32)
    cols[:, PC["inv_freq"][0]] = np.tile(inv_freq, 4)
    return cols


def _tri_masks():
    idx = np.arange(128)
    out = np.zeros((128, 7, 4, 128), np.float32)
    for lvl in range(7):
        b = 1 << lvl
        mU = ((idx[:, None] // b) % 2 == 0) & ((idx[None, :] // b) == (idx[:, None] // b) + 1)
        mL = mU.T
        out[:, lvl, 0], out[:, lvl, 1], out[:, lvl, 2], out[:, lvl, 3] = mU, mL, mU, mL
    return np.ascontiguousarray(out.reshape(128, 7 * 512))


def make_in_maps(inp):
    shared = {
        "tmasks": _tri_masks(),
        "w_ada": np.ascontiguousarray(inp["w_ada"][0]), "w_in": np.ascontiguousarray(inp["w_in"][0]),
        "w2": np.ascontiguousarray(inp["w2"][0]), "a2": np.ascontiguousarray(inp["a2"][0]),
        "g2": np.ascontiguousarray(inp["g2"][0]), "w_a": np.ascontiguousarray(inp["w_a"][0]),
        "w_b": np.ascontiguousarray(inp["w_b"][0]), "w_out": np.ascontiguousarray(inp["w_out"][0]),
        "w_ffn_in": np.ascontiguousarray(inp["w_ffn_in"][0]), "w_ffn_out": np.ascontiguousarray(inp["w_ffn_out"][0]),
    }
    maps = []
    for b in range(NC8):
        m = dict(shared)
        m["xT"] = np.ascontiguousarray(np.asarray(inp["x"][b], np.float32).T)
        m["params"] = _pack_params(inp, b)
        m["w0row"] = np.ascontiguousarray(inp["w0"][0].reshape(1, 768))
        maps.append(m)
    return maps


def kernel(**inputs):
    inp = {k: np.asarray(v) for k, v in inputs.items()}
    nc = build_nc()
    in_maps = make_in_maps(inp)
    res = run_bass_kernel_spmd(nc, in_maps, core_ids=list(range(NC8)))
    out = np.stack([np.ascontiguousarray(r["outT"].T) for r in res.results], axis=0)
    return out.astype(np.float32)


def phaseC_attn(nc, P, sb, ps, psb, PT, OT):
    with ExitStack() as es:
        ident = sb(es, "identC", [128, 128], BF16)
        maskT = sb(es, "maskT", [128, 256], BF16)
        ones64 = sb(es, "ones64", [128, 64], BF16)
        b_c = P.buf("constC")
        P.op("pool", lambda e: e.memset(ident[:], 1.0), writes=[b_c])
        P.op("pool", lambda e: e.affine_select(out=ident[:], in_=ident[:], pattern=[[-1, 128]], compare_op=ALU.is_equal,
                                               fill=0.0, base=0, channel_multiplier=1), reads=[b_c], writes=[b_c])
        P.op("pool", lambda e: e.memset(maskT[:], 1.0), reads=[b_c], writes=[b_c])
        P.op("pool", lambda e: e.affine_select(out=maskT[:, 0:128], in_=maskT[:, 0:128], pattern=[[-1, 128]], compare_op=ALU.is_ge,
                                               fill=0.0, base=0, channel_multiplier=1), reads=[b_c], writes=[b_c])
        P.op("pool", lambda e: e.affine_select(out=maskT[:, 128:256], in_=maskT[:, 128:256], pattern=[[1, 128]], compare_op=ALU.is_ge,
                                               fill=0.0, base=0, channel_multiplier=-1), reads=[b_c], writes=[b_c])
        P.op("pool", lambda e: e.memset(ones64[:], 1.0), reads=[b_c], writes=[b_c])
        P.barrier()
        b_c.const = True
        qkv = [[sb(es, f"qkv{i}_{j}", [128, S], BF16) for j in range(3)] for i in range(2)]
        b_qkv = [[P.buf("qkv") for j in range(3)] for i in range(2)]
        vtok = sb(es, "vtok", [128, 32, 128], BF16)
        b_vtok = [P.buf("vtok") for _ in range(8)]
        acc = sb(es, "acc", [128, 2, S], F32)
        b_acc = P.buf("acc")
        NPT = 3
        pT = [sb(es, f"pT{i}", [128, 2, 256], BF16) for i in range(NPT)]
        b_pT = [P.buf("pT") for _ in range(NPT)]
        o_bf = sb(es, "o_bf", [128, S], BF16)
        b_obf = P.buf("obf")
        rec = sb(es, "rec", [128, S], F32)
        b_rec = P.buf("rec")
        ps6b = ps[6][:].bitcast(BF16)

        def load_pair(idx, pp):
            st = idx % 2
            for j, base in enumerate((2560, 3328, 4096)):
                r0 = base + pp * 128
                P.dma("sp", qkv[st][j][:], PT[r0:r0 + 128, :], b_qkv[st][j], writes=[b_qkv[st][j]])

        seq = [(spn, g) for spn in (0, 1) for g in (0, 1, 2)]
        load_pair(0, seq[0][1] * 2 + seq[0][0])
        cnt = 0
        for idx, (spn, g) in enumerate(seq):
            pp = g * 2 + spn
            if idx + 1 < len(seq):
                load_pair(idx + 1, seq[idx + 1][1] * 2 + seq[idx + 1][0])
            st = idx % 2
            d = (1, 4, 16)[g]
            nb = S // d // 128
            qT, kT, vT = qkv[st]
            bq, bk, bv = b_qkv[st]
            view = lambda t: t[:].rearrange("p (n i r) -> p r n i", i=128, r=d)
            qv, kv, vv = view(qT), view(kT), view(vT)
            accv = acc[:].rearrange("p c (n i r) -> p c r n i", i=128, r=d)
            for g4 in range(8):
                for j in range(4):
                    b = g4 * 4 + j
                    r, n = b // nb, b % nb
                    P.op("pe", lambda e, j=j, r=r, n=n, vv=vv: e.transpose(ps6b[:, j * 128:(j + 1) * 128], vv[:, r, n, :], ident[:]),
                         reads=[bv, b_c], writes=[psb[6]])
                P.op("act", lambda e, g4=g4: e.activation(out=vtok[:, g4 * 4:(g4 + 1) * 4, :],
                                                          in_=ps6b[:, 0:512].rearrange("p (j c) -> p j c", c=128), func=AF.Copy),
                     reads=[psb[6]], writes=[b_vtok[g4]])
            def part1(b, cnt_, kv=kv, qv=qv, bk=bk, bq=bq, nb=nb):
                r, n = b // nb, b % nb
                np_ = n - 1 if n > 0 else n
                slot = cnt_ % NPT
                sbk = cnt_ % 2
                for h in (0, 1):
                    hs = slice(64 * h, 64 * h + 64)
                    bank = h * 2 + sbk
                    P.op("pe", lambda e, bank=bank, hs=hs, r=r, np_=np_, n=n: e.matmul(
                        ps[bank][:, 0:128], lhsT=kv[hs, r, np_, :], rhs=qv[hs, r, n, :], start=True, stop=True),
                        reads=[bk, bq], writes=[psb[bank]])
                    P.op("pe", lambda e, bank=bank, hs=hs, r=r, n=n: e.matmul(
                        ps[bank][:, 128:256], lhsT=kv[hs, r, n, :], rhs=qv[hs, r, n, :], start=True, stop=True),
                        reads=[bk, bq], writes=[psb[bank]])
                    P.op("act", lambda e, bank=bank, slot=slot, h=h: e.activation(out=pT[slot][:, h, :], in_=ps[bank][:, 0:256],
                                                                                 func=AF.Exp, scale=0.125),
                         reads=[psb[bank]], writes=[b_pT[slot]])
                for h in (0, 1):
                    P.op("dve", lambda e, slot=slot, h=h: e.tensor_tensor(out=pT[slot][:, h, :], in0=pT[slot][:, h, :], in1=maskT[:], op=ALU.mult),
                         reads=[b_pT[slot], b_c], writes=[b_pT[slot]])

            def part2(b, cnt_, accv=accv, g=g, nb=nb):
                r, n = b // nb, b % nb
                bprev = b - 1 if n > 0 else b
                slot = cnt_ % NPT
                ob = 4 + cnt_ % 2
                for h in (0, 1):
                    hs = slice(64 * h, 64 * h + 64)
                    if n > 0:
                        P.op("pe", lambda e, ob=ob, hs=hs, bprev=bprev, slot=slot, h=h: e.matmul(
                            ps[ob][hs, 0:128], lhsT=vtok[:, bprev, hs], rhs=pT[slot][:, h, 0:128], start=True, stop=False),
                            reads=[b_vtok[bprev // 4], b_pT[slot]], writes=[psb[ob]])
                    P.op("pe", lambda e, ob=ob, hs=hs, b=b, slot=slot, h=h, n=n: e.matmul(
                        ps[ob][hs, 0:128], lhsT=vtok[:, b, hs], rhs=pT[slot][:, h, 128:256], start=(n == 0), stop=True),
                        reads=[b_vtok[b // 4], b_pT[slot]], writes=[psb[ob]])
                    if n > 0:
                        P.op("pe", lambda e, ob=ob, hs=hs, slot=slot, h=h: e.matmul(
                            ps[ob][hs, 128:256], lhsT=ones64[:], rhs=pT[slot][:, h, 0:128], start=True, stop=False),
                            reads=[b_c, b_pT[slot]], writes=[psb[ob]])
                    P.op("pe", lambda e, ob=ob, hs=hs, slot=slot, h=h, n=n: e.matmul(
                        ps[ob][hs, 128:256], lhsT=ones64[:], rhs=pT[slot][:, h, 128:256], start=(n == 0), stop=True),
                        reads=[b_c, b_pT[slot]], writes=[psb[ob]])
                src = ps[ob][:, 0:256].rearrange("p (c i) -> p c i", i=128)
                if g == 0:
                    P.op("dve", lambda e, src=src, r=r, n=n: e.tensor_copy(out=accv[:, :, r, n, :], in_=src),
                         reads=[psb[ob]], writes=[b_acc])
                else:
                    P.op("dve", lambda e, src=src, r=r, n=n: e.tensor_tensor(out=accv[:, :, r, n, :], in0=src, in1=accv[:, :, r, n, :], op=ALU.add),
                         reads=[psb[ob], b_acc], writes=[b_acc])

            part1(0, cnt)
            for b in range(32):
                if b + 1 < 32:
                    part1(b + 1, cnt + b + 1)
                part2(b, cnt + b)
            cnt += 32
            if g == 2:
                P.op("dve", lambda e: e.reciprocal(out=rec[:], in_=acc[:, 1, :]), reads=[b_acc], writes=[b_rec])
                P.op("pool", lambda e: e.tensor_tensor(out=o_bf[:], in0=acc[:, 0, :], in1=rec[:], op=ALU.mult), reads=[b_acc, b_rec], writes=[b_obf])
                P.dma("sp", OT[spn * 128:(spn + 1) * 128, :], o_bf[:], b_obf, reads=[b_obf])
        P.barrier()


def load_weight_bf16(P, sb, es, name, w_dram, nk, ncols, stg32, b_stg32, cnt0=0, eng="pool"):
    wt = sb(es, name, [128, nk, ncols], BF16)
    bw = P.buf(name)
    for kc in range(nk):
        si = (cnt0 + kc) % len(stg32)
        for c0 in range(0, ncols, 1024):
            c1 = min(ncols, c0 + 1024)
            P.dma("sp", stg32[si][:, 0:c1 - c0], w_dram[kc * 128:(kc + 1) * 128, c0:c1], b_stg32[si], writes=[b_stg32[si]])
            if eng == "act":
                P.op("act", lambda e, si=si, kc=kc, c0=c0, c1=c1: e.activation(out=wt[:, kc, c0:c1], in_=stg32[si][:, 0:c1 - c0], func=AF.Copy),
                     reads=[b_stg32[si]], writes=[bw])
            else:
                P.op(eng, lambda e, si=si, kc=kc, c0=c0, c1=c1: e.tensor_copy(out=wt[:, kc, c0:c1], in_=stg32[si][:, 0:c1 - c0]),
                     reads=[b_stg32[si]], writes=[bw])
            si = (si + 1) % len(stg32)
    return wt, bw


def phaseD1(nc, P, sb, ps, psb, PT, YA, OT, X1, xT, d1w, GM, ones_bf, eps_t, b_const, b_coef):
    with ExitStack() as es:
        (wa, b_wa), (wb, b_wb), (wo, b_wo) = d1w
        ya_t = [sb(es, f"ya_t{i}", [128, 6, TT], BF16) for i in range(2)]
        o_t = [sb(es, f"o_t{i}", [128, 2, TT], BF16) for i in range(2)]
        sga_t = [sb(es, f"sga_t{i}", [128, 8, TT], BF16) for i in range(2)]
        sgb_t = [sb(es, f"sgb_t{i}", [128, 8, TT], BF16) for i in range(2)]
        x_t = [sb(es, f"x_t{i}", [128, 8, TT], F32) for i in range(2)]
        b_in = [[P.buf("d1in") for _ in range(5)] for _ in range(2)]
        merged = sb(es, "merged", [128, 8, TT], BF16)
        b_merged = P.buf("merged")
        m3 = sb(es, "m3", [128, 8, TT], F32)
        b_m3 = P.buf("m3")
        sq = sb(es, "sqD", [128, 8, TT], BF16)
        b_sq = P.buf("sqD")
        rs = sb(es, "rsD", [128, TT], F32)
        b_rs = P.buf("rsD")
        m1 = [sb(es, f"m1_{i}", [128, TT], F32) for i in range(2)]
        m2 = [sb(es, f"m2_{i}", [128, TT], F32) for i in range(2)]
        b_m1 = [P.buf("m1") for _ in range(2)]
        b_m2 = [P.buf("m2") for _ in range(2)]
        x1_t = [sb(es, f"x1_t{i}", [128, 8, TT], F32) for i in range(2)]
        b_x1 = [P.buf("x1t") for _ in range(2)]
        YAv = YA.rearrange("(kc p) t -> p kc t", p=128)
        OTv = OT.rearrange("(kc p) t -> p kc t", p=128)
        GAv = PT[4864:5888, :].rearrange("(kc p) t -> p kc t", p=128)
        GBv = PT[5888:6912, :].rearrange("(kc p) t -> p kc t", p=128)
        xTv = xT.rearrange("(kc p) t -> p kc t", p=128)
        X1v = X1.rearrange("(kc p) t -> p kc t", p=128)

        def loads(tt):
            s2 = tt % 2
            tsl = slice(tt * TT, (tt + 1) * TT)
            for j, (dst, src) in enumerate(((ya_t, YAv), (o_t, OTv), (sga_t, GAv), (sgb_t, GBv), (x_t, xTv))):
                P.dma("sp", dst[s2][:], src[:, :, tsl], b_in[s2][j], writes=[b_in[s2][j]])

        merged2 = [merged, sb(es, "merged_b", [128, 8, TT], BF16)]
        b_merged2 = [b_merged, P.buf("merged_b")]
        cnt = [0]

        def s1(tt):
            s2 = tt % 2
            mg, bmg = merged2[s2], b_merged2[s2]
            for jc in range(8):
                c = cnt[0]
                cnt[0] += 1
                pa, pb = c % 2, 2 + c % 2
                mi = c % 2
                js = slice(jc * 128, (jc + 1) * 128)
                for kc in range(6):
                    P.op("pe", lambda e, pa=pa, kc=kc, js=js, s2=s2: e.matmul(ps[pa][:], lhsT=wa[:, kc, js], rhs=ya_t[s2][:, kc, :],
                                                                           start=(kc == 0), stop=(kc == 5)),
                         reads=[b_wa, b_in[s2][0]], writes=[psb[pa]])
                for kc in range(2):
                    P.op("pe", lambda e, pb=pb, kc=kc, js=js, s2=s2: e.matmul(ps[pb][:], lhsT=wb[:, kc, js], rhs=o_t[s2][:, kc, :],
                                                                           start=(kc == 0), stop=(kc == 1)),
                         reads=[b_wb, b_in[s2][1]], writes=[psb[pb]])
                P.op("dve", lambda e, pa=pa, mi=mi, jc=jc, s2=s2: e.tensor_tensor(out=m1[mi][:], in0=ps[pa][:], in1=sga_t[s2][:, jc, :], op=ALU.mult),
                     reads=[psb[pa], b_in[s2][2]], writes=[b_m1[mi]])
                P.op("dve", lambda e, pb=pb, mi=mi, jc=jc, s2=s2: e.tensor_tensor(out=m2[mi][:], in0=ps[pb][:], in1=sgb_t[s2][:, jc, :], op=ALU.mult),
                     reads=[psb[pb], b_in[s2][3]], writes=[b_m2[mi]])
                P.op("dve", lambda e, mi=mi, jc=jc, mg=mg: e.tensor_tensor(out=mg[:, jc, :], in0=m1[mi][:], in1=m2[mi][:], op=ALU.add),
                     reads=[b_m1[mi], b_m2[mi]], writes=[bmg])

        def s2f(tt):
            s2 = tt % 2
            mg, bmg = merged2[s2], b_merged2[s2]
            for jc in range(8):
                po = 4 + jc % 2
                js = slice(jc * 128, (jc + 1) * 128)
                for kc in range(8):
                    P.op("pe", lambda e, po=po, kc=kc, js=js, mg=mg: e.matmul(ps[po][:], lhsT=wo[:, kc, js], rhs=mg[:, kc, :],
                                                                           start=(kc == 0), stop=(kc == 7)),
                         reads=[b_wo, bmg], writes=[psb[po]])
                P.op("act", lambda e, po=po, jc=jc: e.activation(out=m3[:, jc, :], in_=ps[po][:], func=AF.Copy), reads=[psb[po]], writes=[b_m3])
            P.op("act", lambda e: e.activation(out=sq[:], in_=m3[:], func=AF.Square), reads=[b_m3], writes=[b_sq])
            for kc in range(8):
                P.op("pe", lambda e, kc=kc: e.matmul(ps[6][:], lhsT=ones_bf[:], rhs=sq[:, kc, :], start=(kc == 0), stop=(kc == 7)),
                     reads=[b_sq, b_const], writes=[psb[6]])
            P.op("act", lambda e: e.activation(out=rs[:], in_=ps[6][:], func=AF.Ln, bias=eps_t[:], scale=1.0 / D),
                 reads=[psb[6], b_const], writes=[b_rs])
            P.op("act", lambda e: e.activation(out=rs[:], in_=rs[:], func=AF.Exp, scale=-0.5), reads=[b_rs], writes=[b_rs])

        def s3(tt):
            s2 = tt % 2
            tsl = slice(tt * TT, (tt + 1) * TT)
            for jc in range(8):
                mi = jc % 2
                P.op("dve", lambda e, mi=mi, jc=jc: e.scalar_tensor_tensor(out=m1[mi][:], in0=m3[:, jc, :], scalar=GM(jc), in1=rs[:],
                                                                         op0=ALU.mult, op1=ALU.mult),
                     reads=[b_m3, b_rs, b_coef], writes=[b_m1[mi]])
                P.op("dve", lambda e, mi=mi, jc=jc, s2=s2: e.tensor_tensor(out=x1_t[s2][:, jc, :], in0=m1[mi][:], in1=x_t[s2][:, jc, :], op=ALU.add),
                     reads=[b_m1[mi], b_in[s2][4]], writes=[b_x1[s2]])
            P.dma("sp", X1v[:, :, tsl], x1_t[s2][:], b_x1[s2], reads=[b_x1[s2]])

        loads(0)
        if NT > 1:
            loads(1)
        s1(0)
        for tt in range(NT):
            s2f(tt)
            if tt + 1 < NT:
                s1(tt + 1)
            s3(tt)
            if tt + 2 < NT:
                loads(tt + 2)
        P.barrier()


def phaseD2(nc, P, sb, ps, psb, X1, outT, WFI, WFO, b_wpre, A_f, B_f, GF, ones_bf, eps_t, b_const, b_coef, b_modv, b_out):
    T2 = 256
    NT2 = S // T2
    NH = DFF // 128
    with ExitStack() as es:
        wfi = sb(es, "wfi", [128, 8, 2 * DFF], BF16)
        wfo = sb(es, "wfo", [128, NH, D], BF16)
        b_wfi, b_wfo = P.buf("wfi"), P.buf("wfo")
        WFIv = WFI.rearrange("(kc p) n -> p kc n", p=128)
        WFOv = WFO.rearrange("(kc p) n -> p kc n", p=128)
        for kc in range(8):
            P.dma("sp", wfi[:, kc, :], WFIv[:, kc, :], b_wfi, reads=[b_wpre], writes=[b_wfi])
        for k0 in range(0, NH, 11):
            P.dma("sp", wfo[:, k0:k0 + 11, :], WFOv[:, k0:k0 + 11, :], b_wfo, reads=[b_wpre], writes=[b_wfo])
        x1_t = [sb(es, f"x1f{i}", [128, 8, T2], F32) for i in range(2)]
        b_x1 = [P.buf("x1f") for _ in range(2)]
        sq = sb(es, "sqF", [128, 8, T2], BF16)
        b_sq = P.buf("sqF")
        rs = sb(es, "rsF", [128, T2], F32)
        b_rs = P.buf("rsF")
        tmpb = [sb(es, f"tmpF{i}", [128, T2], F32) for i in range(2)]
        b_tmp = [P.buf("tmpF") for _ in range(2)]
        h2 = sb(es, "h2", [128, 8, T2], BF16)
        b_h2 = P.buf("h2")
        su = [sb(es, f"su{i}", [128, T2], F32) for i in range(2)]
        b_su = [P.buf("su") for _ in range(2)]
        actT = sb(es, "actT", [128, NH, T2], BF16)
        b_act = P.buf("actT")
        f_t = sb(es, "f_t", [128, 8, T2], F32)
        b_f = P.buf("f_t")
        X1v = X1.rearrange("(kc p) t -> p kc t", p=128)
        outv = outT.rearrange("(kc p) t -> p kc t", p=128)
        h2b = [h2, sb(es, "h2b", [128, 8, T2], BF16)]
        b_h2b = [b_h2, P.buf("h2b")]
        sqE = sb(es, "sqE", [128, 8, T2], BF16)
        b_sqE = P.buf("sqE")
        rsE = sb(es, "rsE", [128, T2], F32)
        b_rsE = P.buf("rsE")
        cnt = [0]

        def load_x1(tt):
            P.dma("sp", x1_t[tt % 2][:], X1v[:, :, tt * T2:(tt + 1) * T2], b_x1[tt % 2], writes=[b_x1[tt % 2]])

        def pro(tt):
            s2 = tt % 2
            xt = x1_t[s2]
            hh, bhh = h2b[s2], b_h2b[s2]
            P.op("act", lambda e, xt=xt: e.activation(out=sq[:], in_=xt[:], func=AF.Square), reads=[b_x1[s2]], writes=[b_sq])
            for kc in range(8):
                P.op("pe", lambda e, kc=kc: e.matmul(ps[6][:, 0:T2], lhsT=ones_bf[:], rhs=sq[:, kc, :], start=(kc == 0), stop=(kc == 7)),
                     reads=[b_sq, b_const], writes=[psb[6]])
            P.op("act", lambda e: e.activation(out=rs[:], in_=ps[6][:, 0:T2], func=AF.Ln, bias=eps_t[:], scale=1.0 / D),
                 reads=[psb[6], b_const], writes=[b_rs])
            P.op("act", lambda e: e.activation(out=rs[:], in_=rs[:], func=AF.Exp, scale=-0.5), reads=[b_rs], writes=[b_rs])
            for kc in range(8):
                tb, btb = tmpb[kc % 2], b_tmp[kc % 2]
                P.op("dve", lambda e, kc=kc, tb=tb, xt=xt: e.scalar_tensor_tensor(out=tb[:], in0=xt[:, kc, :], scalar=A_f(kc), in1=rs[:],
                                                                               op0=ALU.mult, op1=ALU.mult),
                     reads=[b_x1[s2], b_rs, b_coef], writes=[btb])
                P.op("act", lambda e, kc=kc, tb=tb, hh=hh: e.activation(out=hh[:, kc, :], in_=tb[:], func=AF.Identity, bias=B_f(kc), scale=1.0),
                     reads=[btb, b_modv], writes=[bhh])

        def ug(tt):
            s2 = tt % 2
            hh, bhh = h2b[s2], b_h2b[s2]
            for hc in range(NH):
                c = cnt[0]
                cnt[0] += 1
                pu, pg = c % 2, 2 + c % 2
                si = c % 2
                for kc in range(8):
                    P.op("pe", lambda e, pu=pu, kc=kc, hc=hc, hh=hh: e.matmul(ps[pu][:, 0:T2], lhsT=wfi[:, kc, hc * 128:(hc + 1) * 128], rhs=hh[:, kc, :],
                                                                           start=(kc == 0), stop=(kc == 7)),
                         reads=[b_wfi, bhh], writes=[psb[pu]])
                for kc in range(8):
                    P.op("pe", lambda e, pg=pg, kc=kc, hc=hc, hh=hh: e.matmul(ps[pg][:, 0:T2], lhsT=wfi[:, kc, DFF + hc * 128:DFF + (hc + 1) * 128], rhs=hh[:, kc, :],
                                                                           start=(kc == 0), stop=(kc == 7)),
                         reads=[b_wfi, bhh], writes=[psb[pg]])
                P.op("act", lambda e, pu=pu, si=si: e.activation(out=su[si][:], in_=ps[pu][:, 0:T2], func=AF.Silu), reads=[psb[pu]], writes=[b_su[si]])
                P.op("dve", lambda e, pg=pg, si=si, hc=hc: e.tensor_tensor(out=actT[:, hc, :], in0=ps[pg][:, 0:T2], in1=su[si][:], op=ALU.mult),
                     reads=[psb[pg], b_su[si]], writes=[b_act])

        def ff(tt):
            for jc in range(8):
                pf = 4 + jc % 2
                for hc in range(NH):
                    P.op("pe", lambda e, pf=pf, hc=hc, jc=jc: e.matmul(ps[pf][:, 0:T2], lhsT=wfo[:, hc, jc * 128:(jc + 1) * 128], rhs=actT[:, hc, :],
                                                                    start=(hc == 0), stop=(hc == NH - 1)),
                         reads=[b_wfo, b_act], writes=[psb[pf]])
                P.op("act", lambda e, pf=pf, jc=jc: e.activation(out=f_t[:, jc, :], in_=ps[pf][:, 0:T2], func=AF.Copy), reads=[psb[pf]], writes=[b_f])

        def epi(tt):
            s2 = tt % 2
            xt = x1_t[s2]
            P.op("act", lambda e: e.activation(out=sqE[:], in_=f_t[:], func=AF.Square), reads=[b_f], writes=[b_sqE])
            for kc in range(8):
                P.op("pe", lambda e, kc=kc: e.matmul(ps[7][:, 0:T2], lhsT=ones_bf[:], rhs=sqE[:, kc, :], start=(kc == 0), stop=(kc == 7)),
                     reads=[b_sqE, b_const], writes=[psb[7]])
            P.op("act", lambda e: e.activation(out=rsE[:], in_=ps[7][:, 0:T2], func=AF.Ln, bias=eps_t[:], scale=1.0 / D),
                 reads=[psb[7], b_const], writes=[b_rsE])
            P.op("act", lambda e: e.activation(out=rsE[:], in_=rsE[:], func=AF.Exp, scale=-0.5), reads=[b_rsE], writes=[b_rsE])
            for jc in range(8):
                tb, btb = tmpb[jc % 2], b_tmp[jc % 2]
                P.op("dve", lambda e, jc=jc, tb=tb: e.scalar_tensor_tensor(out=tb[:], in0=f_t[:, jc, :], scalar=GF(jc), in1=rsE[:],
                                                                        op0=ALU.mult, op1=ALU.mult),
                     reads=[b_f, b_rsE, b_coef], writes=[btb])
                P.op("dve", lambda e, jc=jc, tb=tb, xt=xt: e.tensor_tensor(out=xt[:, jc, :], in0=tb[:], in1=xt[:, jc, :], op=ALU.add),
                     reads=[btb], writes=[b_x1[s2]])
            P.dma("sp", outv[:, :, tt * T2:(tt + 1) * T2], xt[:], b_x1[s2], reads=[b_x1[s2]])

        load_x1(0)
        if NT2 > 1:
            load_x1(1)
        pro(0)
        for tt in range(NT2):
            ug(tt)
            if tt + 1 < NT2:
                pro(tt + 1)
            ff(tt)
            epi(tt)
            if tt + 2 < NT2:
                load_x1(tt + 2)
        P.barrier()


def phaseB_rwkv(nc, P, sb, ps, psb, PT, YA, par, pcol, b_par, w2, a2, g2, w0row, tmasks, dbg=False, after_consts=None):
    CDEC = math.exp(-0.5)
    NP = 6
    with ExitStack() as es:
        identB = sb(es, "identB", [128, 128], BF16)
        TRI = sb(es, "TRI", [128, 256], F32)
        bones = sb(es, "bones", [128, 128], BF16)
        bonesF = sb(es, "bonesF", [128, 128], F32)
        mask4 = sb(es, "mask4", [128, 512], BF16)
        maskLT = sb(es, "maskLT", [128, 128], F32)
        gneps = sb(es, "gneps", [128, 1], F32)
        w0bc = sb(es, "w0bc", [128, 768], F32)
        lw = [sb(es, f"lw{i}", [128, 768], BF16) for i in range(3)]
        Sbd = sb(es, "Sbd", [128, NP, 128], BF16)
        mlev = sb(es, "mlev", [128, 7, 512], BF16)
        ident4 = sb(es, "ident4", [128, 4, 128], BF16)
        b_c = P.buf("constB")
        b_S = [P.buf("S") for _ in range(NP)]
        cw = lambda fn, rd=(): P.op("pool", fn, reads=[b_c] + list(rd), writes=[b_c])
        cw(lambda e: e.memset(identB[:], 1.0))
        cw(lambda e: e.affine_select(out=identB[:], in_=identB[:], pattern=[[-1, 128]], compare_op=ALU.is_equal, fill=0.0, base=0, channel_multiplier=1))
        cw(lambda e: e.memset(TRI[:], 1.0))
        cw(lambda e: e.affine_select(out=TRI[:, 0:128], in_=TRI[:, 0:128], pattern=[[1, 128]], compare_op=ALU.is_ge, fill=0.0, base=0, channel_multiplier=-1))
        cw(lambda e: e.affine_select(out=TRI[:, 128:256], in_=TRI[:, 128:256], pattern=[[1, 128]], compare_op=ALU.is_gt, fill=0.0, base=0, channel_multiplier=-1))
        cw(lambda e: e.memset(mask4[:], 1.0))
        for q4 in range(4):
            op_ = ALU.is_gt if q4 % 2 == 0 else ALU.is_ge
            cw(lambda e, q4=q4, op_=op_: e.affine_select(out=mask4[:, q4 * 128:(q4 + 1) * 128], in_=mask4[:, q4 * 128:(q4 + 1) * 128],
                                                         pattern=[[1, 128]], compare_op=op_, fill=0.0, base=0, channel_multiplier=-1))
        cw(lambda e: e.memset(maskLT[:], 1.0))
        cw(lambda e: e.affine_select(out=maskLT[:], in_=maskLT[:], pattern=[[-1, 128]], compare_op=ALU.is_gt, fill=0.0, base=0, channel_multiplier=1))
        cw(lambda e: e.memset(bones[:], 0.0))
        cw(lambda e: e.memset(bonesF[:], 0.0))
        for h in (0, 1):
            hs = slice(64 * h, 64 * h + 64)
            cw(lambda e, hs=hs: e.memset(bones[hs, hs], 1.0))
            cw(lambda e, hs=hs: e.memset(bonesF[hs, hs], 1.0 / 64))
        cw(lambda e: e.memset(gneps[:], 64e-5))
        cw(lambda e: e.memset(Sbd[:], 0.0))
        for j4 in range(4):
            cw(lambda e, j4=j4: e.tensor_copy(out=ident4[:, j4, :], in_=identB[:]))
        with ExitStack() as es2:
            st32 = sb(es2, "st32B", [128, 768], F32)
            b_st = P.buf("st32B")
            for i, (wd, nr) in enumerate(((w2, 64), (a2, 64), (g2, 128))):
                P.dma("sp", st32[0:nr, :], wd[:, :], b_st, writes=[b_st])
                P.op("dve", lambda e, i=i, nr=nr: e.tensor_copy(out=lw[i][0:nr, :], in_=st32[0:nr, :]), reads=[b_st], writes=[b_c])
            P.dma("sp", w0bc[:], w0row.partition_broadcast(128), b_st, reads=[b_st], writes=[b_c])
            st_b = sb(es2, "st32Bb", [128, 768], F32)
            hi_b = sb(es2, "hi96", [128, 768], BF16)
            b_w0 = P.buf("w0hl")
            P.op("dve", lambda e: e.memset(lw[0][64:128, :], 0.0), reads=[b_c], writes=[b_c])
            P.dma("sp", st_b[64:65, :], w0row[:, :], b_w0, writes=[b_w0])
            P.dma("sp", st_b[96:97, :], w0row[:, :], b_w0, writes=[b_w0])
            P.op("dve", lambda e: e.tensor_copy(out=lw[0][64:65, :], in_=st_b[64:65, :]), reads=[b_w0, b_c], writes=[b_c])
            P.op("dve", lambda e: e.tensor_copy(out=hi_b[96:97, :], in_=st_b[96:97, :]), reads=[b_w0], writes=[b_w0])
            P.op("dve", lambda e: e.tensor_copy(out=st32[96:97, :], in_=hi_b[96:97, :]), reads=[b_w0, b_st], writes=[b_st])
            P.op("dve", lambda e: e.tensor_tensor(out=lw[0][96:97, :], in0=st_b[96:97, :], in1=st32[96:97, :], op=ALU.subtract),
                 reads=[b_w0, b_st, b_c], writes=[b_c])
            with ExitStack() as es3:
                mst = sb(es3, "mst", [128, 7 * 512], F32)
                b_mst = P.buf("mst")
                P.dma("sp", mst[:], tmasks[:, :], b_mst, writes=[b_mst])
                P.op("dve", lambda e: e.tensor_copy(out=mlev[:].rearrange("p a b -> p (a b)"), in_=mst[:]), reads=[b_mst], writes=[b_c])
                P.barrier()
            P.barrier()
        b_c.const = True
        w2bf, a2bf, g2bf = lw
        if after_consts is not None:
            after_consts()

        GT = 256
        NG = S // GT
        rkv_g = [sb(es, f"rkv_g{i}", [128, 18, GT], BF16) for i in range(2)]
        twl_g = [sb(es, f"twl_g{i}", [128, GT], BF16) for i in range(2)]
        b_twl1 = P.buf("twl_ones")
        for i in range(2):
            P.op("pool", lambda e, i=i: e.memset(twl_g[i][64:128, :], 1.0), writes=[b_twl1])
        P.barrier()
        al_g = [sb(es, f"al_g{i}", [64, GT], BF16) for i in range(2)]
        sgl_g = [sb(es, f"sgl_g{i}", [128, GT], BF16) for i in range(2)]
        ya_g = [sb(es, f"ya_g{i}", [128, NP, GT], BF16) for i in range(2)]
        b_g = [[P.buf("grp") for _ in range(4)] for _ in range(2)]
        b_ya = [P.buf("ya_g") for _ in range(2)]
        NF, NH = 16, 70
        Ft = [sb(es, f"Ft{hp}", [128, NF, 128], F32) for hp in range(NP)]
        Ht = [sb(es, f"Ht{hp}", [128, NH, 128], BF16) for hp in range(NP)]
        bF = [[P.buf("F") for _ in range(NF)] for _ in range(NP)]
        bH = [[P.buf("H") for _ in range(NH)] for _ in range(NP)]
        PTv = PT[0:2304, :].rearrange("(c p) t -> p c t", p=128)
        YAv = YA.rearrange("(c p) t -> p c t", p=128)
        ps0b = ps[0][:].bitcast(BF16)

        def load_group(gi):
            s2 = gi % 2
            gsl = slice(gi * GT, (gi + 1) * GT)
            P.dma("sp", rkv_g[s2][:], PTv[:, :, gsl], b_g[s2][0], writes=[b_g[s2][0]])
            P.dma("sp", twl_g[s2][0:64, :], PT[2304:2368, gsl], b_g[s2][1], writes=[b_g[s2][1]])
            P.dma("sp", al_g[s2][:], PT[2368:2432, gsl], b_g[s2][2], writes=[b_g[s2][2]])
            P.dma("sp", sgl_g[s2][:], PT[2432:2560, gsl], b_g[s2][3], writes=[b_g[s2][3]])

        load_group(0)
        import os
        rr = [0]

        def tile_body(n):
            TPG = GT // 128
            gi, s2 = n // TPG, (n // TPG) % 2
            def pre():
                if n % TPG == 0 and gi + 1 < NG:
                    load_group(gi + 1)
            ts = slice((n % TPG) * 128, (n % TPG + 1) * 128)
            bg = b_g[s2]
            steps = []
            segs = []
            curh = [{}]

            def B(hp, k):
                cur = curh[0]
                if k not in cur:
                    cur[k] = rr[0] % 8
                    rr[0] += 1
                return cur[k]
            psbf = [ps[i][:].bitcast(BF16) for i in range(8)]

            segkind = {}

            def seg(kind="ps"):
                segs.append(len(steps))
                segkind[len(steps)] = kind
            F = lambda hp, i: Ft[hp][:, i, :]
            H = lambda hp, i: Ht[hp][:, i, :]
            H4 = lambda hp, i: Ht[hp][:, i:i + 4, :].rearrange("p a b -> p (a b)")
            rT = lambda hp: rkv_g[s2][:, hp, ts]
            kT = lambda hp: rkv_g[s2][:, 6 + hp, ts]
            vT = lambda hp: rkv_g[s2][:, 12 + hp, ts]
            cs = lambda hp: slice(hp * 128, (hp + 1) * 128)

            def add(eng, fn, rd, wr):
                steps.append((eng, fn, rd, wr))

            seg()
            add("pe", lambda hp: (lambda e: e.matmul(ps[B(hp, 0)][:, 0:128], lhsT=twl_g[s2][0:97, ts], rhs=w2bf[0:97, cs(hp)], start=True, stop=True)),
                lambda hp: [bg[1], b_c], lambda hp: [psb[B(hp, 0)]])
            add("pe", lambda hp: (lambda e: e.matmul(ps[B(hp, 0)][:, 128:256], lhsT=a2bf[0:64, cs(hp)], rhs=al_g[s2][:, ts], start=True, stop=True)),
                lambda hp: [bg[2], b_c], lambda hp: [psb[B(hp, 0)]])
            add("act", lambda hp: (lambda e: e.activation(out=F(hp, 1), in_=ps[B(hp, 0)][:, 0:128], func=AF.Sigmoid)),
                lambda hp: [psb[B(hp, 0)]], lambda hp: [bF[hp][1]])
            add("act", lambda hp: (lambda e: e.activation(out=F(hp, 2), in_=ps[B(hp, 0)][:, 128:256], func=AF.Sigmoid, bias=pcol("a0", hp), scale=1.0)),
                lambda hp: [psb[B(hp, 0)], b_par], lambda hp: [bF[hp][2]])
            seg()
            add("pe", lambda hp: (lambda e: e.matmul(ps[B(hp, 1)][:, 0:256], lhsT=F(hp, 1), rhs=TRI[:], start=True, stop=True)),
                lambda hp: [bF[hp][1], b_c], lambda hp: [psb[B(hp, 1)]])
            add("act", lambda hp: (lambda e: e.activation(out=Ft[hp][:, 4:6, :], in_=ps[B(hp, 1)][:, 0:256].rearrange("p (j c) -> p j c", c=128), func=AF.Exp, scale=-CDEC)),
                lambda hp: [psb[B(hp, 1)]], lambda hp: [bF[hp][4], bF[hp][5]])
            add("act", lambda hp: (lambda e: e.activation(out=F(hp, 6), in_=ps[B(hp, 1)][:, 0:128], func=AF.Exp, scale=CDEC)),
                lambda hp: [psb[B(hp, 1)]], lambda hp: [bF[hp][6]])
            add("dve", lambda hp: (lambda e: e.tensor_scalar(out=F(hp, 7), in0=F(hp, 6), scalar1=Ft[hp][:, 4, 127:128], scalar2=None, op0=ALU.mult)),
                lambda hp: [bF[hp][6], bF[hp][4]], lambda hp: [bF[hp][7]])
            seg()
            add("act", lambda hp: (lambda e: e.activation(out=H(hp, 0), in_=kT(hp), func=AF.Square, scale=pcol("k_k", hp))),
                lambda hp: [bg[0], b_par], lambda hp: [bH[hp][0]])
            add("pe", lambda hp: (lambda e: e.matmul(ps[B(hp, 1)][:, 256:384], lhsT=bones[:], rhs=H(hp, 0), start=True, stop=True)),
                lambda hp: [bH[hp][0], b_c], lambda hp: [psb[B(hp, 1)]])
            add("act", lambda hp: (lambda e: e.activation(out=F(hp, 8), in_=ps[B(hp, 1)][:, 256:384], func=AF.Ln)),
                lambda hp: [psb[B(hp, 1)], bF[hp][7]], lambda hp: [bF[hp][8]])
            seg("ew")
            add("act", lambda hp: (lambda e: e.activation(out=F(hp, 8), in_=F(hp, 8), func=AF.Exp, scale=-0.5)),
                lambda hp: [], lambda hp: [bF[hp][8]])
            add("dve", lambda hp: (lambda e: e.scalar_tensor_tensor(out=F(hp, 9), in0=kT(hp), scalar=pcol("k_k", hp), in1=F(hp, 8), op0=ALU.mult, op1=ALU.mult)),
                lambda hp: [bg[0], b_par, bF[hp][8]], lambda hp: [bF[hp][9]])
            add("dve", lambda hp: (lambda e: e.tensor_scalar(out=F(hp, 10), in0=F(hp, 2), scalar1=-1.0, scalar2=pcol("k_a", hp), op0=ALU.add, op1=ALU.mult)),
                lambda hp: [bF[hp][2], b_par], lambda hp: [bF[hp][10]])
            add("dve", lambda hp: (lambda e: e.scalar_tensor_tensor(out=F(hp, 10), in0=F(hp, 10), scalar=1.0, in1=kT(hp), op0=ALU.add, op1=ALU.mult)),
                lambda hp: [bg[0]], lambda hp: [bF[hp][10]])
            add("dve", lambda hp: (lambda e: e.scalar_tensor_tensor(out=F(hp, 11), in0=F(hp, 9), scalar=-1.0, in1=F(hp, 2), op0=ALU.mult, op1=ALU.mult)),
                lambda hp: [bF[hp][9], bF[hp][2]], lambda hp: [bF[hp][11]])
            add("dve", lambda hp: (lambda e: e.tensor_tensor(out=H(hp, 2), in0=F(hp, 9), in1=F(hp, 5), op=ALU.mult)),
                lambda hp: [bF[hp][9], bF[hp][5]], lambda hp: [bH[hp][2]])
            add("dve", lambda hp: (lambda e: e.tensor_tensor(out=H(hp, 3), in0=rT(hp), in1=F(hp, 4), op=ALU.mult)),
                lambda hp: [bg[0], bF[hp][4]], lambda hp: [bH[hp][3]])
            add("dve", lambda hp: (lambda e: e.tensor_tensor(out=Ht[hp][:, 4:6, :], in0=Ft[hp][:, 10:12, :],
                                                           in1=F(hp, 6).unsqueeze(1).to_broadcast([128, 2, 128]), op=ALU.mult)),
                lambda hp: [bF[hp][10], bF[hp][11], bF[hp][6]], lambda hp: [bH[hp][4], bH[hp][5]])
            add("dve", lambda hp: (lambda e: e.tensor_tensor(out=Ht[hp][:, 6:8, :], in0=Ft[hp][:, 10:12, :],
                                                           in1=F(hp, 7).unsqueeze(1).to_broadcast([128, 2, 128]), op=ALU.mult)),
                lambda hp: [bF[hp][10], bF[hp][11], bF[hp][7]], lambda hp: [bH[hp][6], bH[hp][7]])
            seg("ew")
            add("dve", lambda hp: (lambda e: e.scalar_tensor_tensor(out=H(hp, 1), in0=rT(hp), scalar=pcol("r_k", hp), in1=F(hp, 10), op0=ALU.mult, op1=ALU.mult)),
                lambda hp: [bg[0], b_par, bF[hp][10]], lambda hp: [bH[hp][1]])
            seg()
            add("pe", lambda hp: (lambda e: e.matmul(ps[B(hp, 1)][:, 384:512], lhsT=bones[:], rhs=H(hp, 1), start=True, stop=True)),
                lambda hp: [bH[hp][1], b_c], lambda hp: [psb[B(hp, 1)]])
            add("dve", lambda hp: (lambda e: e.tensor_tensor(out=F(hp, 15), in0=ps[B(hp, 1)][:, 384:512], in1=vT(hp), op=ALU.mult)),
                lambda hp: [psb[B(hp, 1)], bg[0]], lambda hp: [bF[hp][15]])
            seg()
            for j, src in enumerate((lambda hp: H(hp, 2), vT, lambda hp: H(hp, 6), lambda hp: H(hp, 7))):
                rdj = [lambda hp: [bH[hp][2]], lambda hp: [bg[0]], lambda hp: [bH[hp][6]], lambda hp: [bH[hp][7]]][j]
                add("pe", lambda hp, j=j, src=src: (lambda e: e.transpose(psbf[B(hp, 0)][:, j * 128:(j + 1) * 128], src(hp), identB[:])),
                    lambda hp, rdj=rdj: rdj(hp) + [b_c], lambda hp: [psb[B(hp, 0)]])
            add("act", lambda hp: (lambda e: e.activation(out=Ht[hp][:, 8:12, :], in_=psbf[B(hp, 0)][:, 0:512].rearrange("p (j c) -> p j c", c=128), func=AF.Copy)),
                lambda hp: [psb[B(hp, 0)]], lambda hp: [bH[hp][8], bH[hp][9], bH[hp][10], bH[hp][11]])
            seg()
            for h in (0, 1):
                hs = slice(64 * h, 64 * h + 64)
                xb = 2 + h
                mb = 4 + h
                add("pe", lambda hp, hs=hs, h=h: (lambda e: e.matmul(ps[B(hp, 2 + h)][:, 0:256], lhsT=Ht[hp][hs, 5, :],
                                                                    rhs=Ht[hp][hs, 2:4, :], start=True, stop=True)),
                    lambda hp: [bH[hp][5], bH[hp][2], bH[hp][3]], lambda hp, h=h: [psb[B(hp, 2 + h)]])
                add("pe", lambda hp, hs=hs, h=h: (lambda e: e.matmul(ps[B(hp, 2 + h)][:, 256:512], lhsT=Ht[hp][hs, 4, :],
                                                                    rhs=Ht[hp][hs, 2:4, :], start=True, stop=True)),
                    lambda hp: [bH[hp][4], bH[hp][2], bH[hp][3]], lambda hp, h=h: [psb[B(hp, 2 + h)]])
                add("pe", lambda hp, hs=hs, h=h: (lambda e: e.matmul(ps[B(hp, h)][:, 0:128], lhsT=Ht[hp][hs, 2, :],
                                                                    rhs=Ht[hp][hs, 5, :], start=True, stop=True)),
                    lambda hp: [bH[hp][5], bH[hp][2]], lambda hp, h=h: [psb[B(hp, h)]])
                xs0 = 12 + 4 * h
                add("dve", lambda hp, h=h, xs0=xs0: (lambda e: e.tensor_tensor(out=H4(hp, xs0), in0=ps[B(hp, 2 + h)][:], in1=mask4[:], op=ALU.mult)),
                    lambda hp, h=h: [psb[B(hp, 2 + h)], b_c], lambda hp, xs0=xs0: [bH[hp][xs0 + i] for i in range(4)])
                add("dve", lambda hp, h=h: (lambda e: e.tensor_tensor(out=H(hp, 28 + h), in0=ps[B(hp, h)][:, 0:128], in1=maskLT[:], op=ALU.mult)),
                    lambda hp, h=h: [psb[B(hp, h)], b_c], lambda hp, h=h: [bH[hp][28 + h]])
            seg("ew")
            TallV = lambda hp: Ht[hp][:, 20:24, :]
            bTall = lambda hp: [bH[hp][20 + i] for i in range(4)]
            XL = lambda hp, h, lvl: H(hp, 42 + 14 * h + lvl)
            bXL = lambda hp, h: [bH[hp][42 + 14 * h + i] for i in range(7)]
            add("dve", lambda hp: (lambda e: e.tensor_tensor(out=Ht[hp][:, 35:63:14, :], in0=Ht[hp][:, 12:20:4, :],
                                                           in1=mlev[:, 0, 0:128].unsqueeze(1).to_broadcast([128, 2, 128]), op=ALU.mult)),
                lambda hp: [bH[hp][12], bH[hp][16], b_c], lambda hp: [bH[hp][35], bH[hp][49]])
            add("dve", lambda hp: (lambda e: e.tensor_tensor(
                out=Ht[hp][:, 42:70, :].rearrange("p (h l) c -> p h l c", l=14)[:, :, 0:7, :],
                in0=Ht[hp][:, 28:30, :].unsqueeze(2).to_broadcast([128, 2, 7, 128]),
                in1=mlev[:, :, 128:256].unsqueeze(1).to_broadcast([128, 2, 7, 128]), op=ALU.mult)),
                lambda hp: [bH[hp][28], bH[hp][29], b_c], lambda hp: bXL(hp, 0) + bXL(hp, 1))
            add("dve", lambda hp: (lambda e: e.tensor_tensor(out=TallV(hp), in0=Ht[hp][:, 35:63:7, :], in1=ident4[:], op=ALU.add)),
                lambda hp: [b_c, bH[hp][35], bH[hp][49]] + bXL(hp, 0) + bXL(hp, 1), lambda hp: bTall(hp))
            for lvl in range(1, 7):
                seg("inv")
                for h in (0, 1):
                    add("pe", lambda hp, h=h, lvl=lvl: (lambda e: e.matmul(ps[B(hp, 2)][:, h * 128:(h + 1) * 128], lhsT=XL(hp, h, lvl), rhs=H(hp, 20 + 2 * h), start=True, stop=True)),
                        lambda hp, h=h: bXL(hp, h) + [bH[hp][20 + 2 * h]], lambda hp: [psb[B(hp, 2)]])
                add("act", lambda hp: (lambda e: e.activation(out=Ht[hp][:, 24:26, :], in_=ps[B(hp, 2)][:, 0:256].rearrange("p (j c) -> p j c", c=128), func=AF.Copy)),
                    lambda hp: [psb[B(hp, 2)]], lambda hp: [bH[hp][24], bH[hp][25]])
                seg("inv")
                add("pe", lambda hp: (lambda e: e.matmul(ps[B(hp, 3)][:], lhsT=identB[:], rhs=Ht[hp][:, 20:24, :], start=True, stop=False)),
                    lambda hp: bTall(hp) + [b_c], lambda hp: [psb[B(hp, 3)]])
                for h in (0, 1):
                    add("pe", lambda hp, h=h: (lambda e: e.matmul(ps[B(hp, 3)][:, (2 * h) * 128:(2 * h + 1) * 128], lhsT=H(hp, 21 + 2 * h), rhs=H(hp, 24 + h), start=False, stop=False)),
                        lambda hp, h=h: [bH[hp][21 + 2 * h], bH[hp][24 + h]], lambda hp: [psb[B(hp, 3)]])
                    add("pe", lambda hp, h=h: (lambda e: e.matmul(ps[B(hp, 3)][:, (2 * h + 1) * 128:(2 * h + 2) * 128], lhsT=H(hp, 24 + h), rhs=H(hp, 21 + 2 * h), start=False, stop=(h == 1))),
                        lambda hp, h=h: [bH[hp][21 + 2 * h], bH[hp][24 + h]], lambda hp: [psb[B(hp, 3)]])
                if lvl % 2 == 1:
                    add("dve", lambda hp: (lambda e: e.tensor_copy(out=TallV(hp), in_=ps[B(hp, 3)][:].rearrange("p (j c) -> p j c", c=128))),
                        lambda hp: [psb[B(hp, 3)]], lambda hp: bTall(hp))
                else:
                    add("act", lambda hp: (lambda e: e.activation(out=TallV(hp), in_=ps[B(hp, 3)][:].rearrange("p (j c) -> p j c", c=128), func=AF.Copy)),
                        lambda hp: [psb[B(hp, 3)]], lambda hp: bTall(hp))
            seg()
            TTs = lambda hp, h: H(hp, 20 + 2 * h)
            bTT = lambda hp, h: bH[hp][20 + 2 * h]
            for h in (0, 1):
                hs = slice(64 * h, 64 * h + 64)
                add("pe", lambda hp, h=h, hs=hs: (lambda e: e.matmul(ps[B(hp, 0)][:, 64 * h:64 * h + 64], lhsT=H(hp, 14 + 4 * h), rhs=Ht[hp][:, 9, hs], start=True, stop=True)),
                    lambda hp, h=h: [bH[hp][14 + 4 * h], bH[hp][9]], lambda hp: [psb[B(hp, 0)]])
            add("act", lambda hp: (lambda e: e.activation(out=H(hp, 32), in_=ps[B(hp, 0)][:, 0:128], func=AF.Copy)),
                lambda hp: [psb[B(hp, 0)]], lambda hp: [bH[hp][32]])
            seg()
            for h in (0, 1):
                hs = slice(64 * h, 64 * h + 64)
                add("pe", lambda hp, h=h, hs=hs: (lambda e: e.matmul(ps[B(hp, 0)][hs, 256:384], lhsT=Ht[hp][:, 8, hs], rhs=TTs(hp, h), start=True, stop=True)),
                    lambda hp, h=h: [bTT(hp, h), bH[hp][8]], lambda hp: [psb[B(hp, 0)]])
            add("act", lambda hp: (lambda e: e.activation(out=H(hp, 33), in_=ps[B(hp, 0)][:, 256:384], func=AF.Copy)),
                lambda hp: [psb[B(hp, 0)]], lambda hp: [bH[hp][33]])
            seg("pm")
            add("pe", lambda hp: (lambda e: e.matmul(ps[B(hp, 1)][:, 0:128], lhsT=H(hp, 33), rhs=Sbd[:, hp, :], start=True, stop=False)),
                lambda hp: [bH[hp][33], b_S[hp]], lambda hp: [psb[B(hp, 1)]])
            for h in (0, 1):
                hs = slice(64 * h, 64 * h + 64)
                add("pe", lambda hp, h=h, hs=hs: (lambda e: e.matmul(ps[B(hp, 1)][:, 64 * h:64 * h + 64], lhsT=TTs(hp, h), rhs=Ht[hp][:, 32, hs], start=False, stop=(h == 1))),
                    lambda hp, h=h: [bTT(hp, h), bH[hp][32]], lambda hp: [psb[B(hp, 1)]])
            add("dve", lambda hp: (lambda e: e.tensor_copy(out=H(hp, 34), in_=ps[B(hp, 1)][:, 0:128])),
                lambda hp: [psb[B(hp, 1)]], lambda hp: [bH[hp][34]])
            add("pe", lambda hp: (lambda e: e.matmul(ps[B(hp, 1)][:, 128:256], lhsT=Sbd[:, hp, :], rhs=H(hp, 3), start=True, stop=False)),
                lambda hp: [b_S[hp], bH[hp][3]], lambda hp: [psb[B(hp, 1)]])
            for h in (0, 1):
                hs = slice(64 * h, 64 * h + 64)
                add("pe", lambda hp, h=h, hs=hs: (lambda e: e.matmul(ps[B(hp, 1)][hs, 128:256], lhsT=Ht[hp][:, 34, hs], rhs=H(hp, 13 + 4 * h), start=False, stop=False)),
                    lambda hp, h=h: [bH[hp][34], bH[hp][13 + 4 * h]], lambda hp: [psb[B(hp, 1)]])
                add("pe", lambda hp, h=h, hs=hs: (lambda e: e.matmul(ps[B(hp, 1)][hs, 128:256], lhsT=Ht[hp][:, 9, hs], rhs=H(hp, 15 + 4 * h), start=False, stop=True)),
                    lambda hp, h=h: [bH[hp][9], bH[hp][15 + 4 * h]], lambda hp: [psb[B(hp, 1)]])
            add("pe", lambda hp: (lambda e: e.matmul(ps[B(hp, 1)][:, 256:384], lhsT=H(hp, 10), rhs=H(hp, 9), start=True, stop=False)),
                lambda hp: [bH[hp][10], bH[hp][9]], lambda hp: [psb[B(hp, 1)]])
            add("pe", lambda hp: (lambda e: e.matmul(ps[B(hp, 1)][:, 256:384], lhsT=H(hp, 11), rhs=H(hp, 34), start=False, stop=True)),
                lambda hp: [bH[hp][11], bH[hp][34]], lambda hp: [psb[B(hp, 1)]])
            add("act", lambda hp: (lambda e: e.activation(out=F(hp, 13), in_=ps[B(hp, 1)][:, 128:256], func=AF.Copy)),
                lambda hp: [psb[B(hp, 1)]], lambda hp: [bF[hp][13]])
            add("act", lambda hp: (lambda e: e.activation(out=F(hp, 14), in_=ps[B(hp, 1)][:, 128:256], func=AF.Square)),
                lambda hp: [psb[B(hp, 1)]], lambda hp: [bF[hp][14]])
            for h in (0, 1):
                hs = slice(64 * h, 64 * h + 64)
                add("dve", lambda hp, h=h, hs=hs: (lambda e: e.scalar_tensor_tensor(out=Sbd[hs, hp, hs], in0=Sbd[hs, hp, hs], scalar=Ft[hp][hs, 4, 127:128],
                                                                                 in1=ps[B(hp, 1)][hs, 256 + 64 * h:256 + 64 * h + 64], op0=ALU.mult, op1=ALU.add)),
                    lambda hp: [psb[B(hp, 1)], bF[hp][4]], lambda hp: [b_S[hp]])
            seg()
            add("pe", lambda hp: (lambda e: e.matmul(ps[B(hp, 1)][:, 0:256], lhsT=bonesF[:], rhs=Ft[hp][:, 13:15, :], start=True, stop=True)),
                lambda hp: [bF[hp][13], bF[hp][14], b_c], lambda hp: [psb[B(hp, 1)]])
            add("act", lambda hp: (lambda e: e.activation(out=F(hp, 1), in_=ps[B(hp, 1)][:, 0:128], func=AF.Square)),
                lambda hp: [psb[B(hp, 1)]], lambda hp: [bF[hp][1]])
            add("dve", lambda hp: (lambda e: e.tensor_tensor(out=F(hp, 5), in0=F(hp, 13), in1=ps[B(hp, 1)][:, 0:128], op=ALU.subtract)),
                lambda hp: [psb[B(hp, 1)], bF[hp][13]], lambda hp: [bF[hp][5]])
            add("dve", lambda hp: (lambda e: e.tensor_tensor(out=F(hp, 2), in0=ps[B(hp, 1)][:, 128:256], in1=F(hp, 1), op=ALU.subtract)),
                lambda hp: [psb[B(hp, 1)], bF[hp][1]], lambda hp: [bF[hp][2]])
            seg("ew")
            add("act", lambda hp: (lambda e: e.activation(out=F(hp, 2), in_=F(hp, 2), func=AF.Ln, bias=gneps[:], scale=1.0)),
                lambda hp: [b_c], lambda hp: [bF[hp][2]])
            add("act", lambda hp: (lambda e: e.activation(out=F(hp, 8), in_=F(hp, 2), func=AF.Exp, scale=-0.5)),
                lambda hp: [bF[hp][2]], lambda hp: [bF[hp][8]])
            add("dve", lambda hp: (lambda e: e.tensor_tensor(out=F(hp, 6), in0=F(hp, 5), in1=F(hp, 8), op=ALU.mult)),
                lambda hp: [bF[hp][5], bF[hp][8]], lambda hp: [bF[hp][6]])
            add("dve", lambda hp: (lambda e: e.scalar_tensor_tensor(out=F(hp, 9), in0=F(hp, 6), scalar=pcol("lnx_w", hp), in1=F(hp, 15), op0=ALU.mult, op1=ALU.add)),
                lambda hp: [bF[hp][6], bF[hp][15], b_par], lambda hp: [bF[hp][9]])
            seg()
            add("pe", lambda hp: (lambda e: e.matmul(ps[B(hp, 0)][:, 0:128], lhsT=g2bf[:, cs(hp)], rhs=sgl_g[s2][:, ts], start=True, stop=True)),
                lambda hp: [bg[3], b_c], lambda hp: [psb[B(hp, 0)]])
            add("dve", lambda hp: (lambda e: e.scalar_tensor_tensor(out=ya_g[s2][:, hp, ts], in0=F(hp, 9), scalar=pcol("lnx_b", hp), in1=ps[B(hp, 0)][:, 0:128], op0=ALU.add, op1=ALU.mult)),
                lambda hp: [psb[B(hp, 0)], bF[hp][9], b_par], lambda hp: [b_ya[s2]])

            bounds = sorted(set(segs + [0, len(steps)]))
            nseg = len(bounds) - 1

            def emit_one(eng, fn, rd, wr, hp):
                rec = _Rec()
                fn(hp)(rec)
                P.op(eng, lambda e, c=rec: getattr(e, c.name)(*c.args, **c.kwargs), reads=rd(hp), writes=wr(hp))

            def emit_seg(si, pairs):
                kind = segkind.get(bounds[si], "ps")
                ops = steps[bounds[si]:bounds[si + 1]]
                if kind in ("ew", "pm"):
                    dicts = {hp: {} for hp in pairs}
                    npp = len(pairs)
                    for dwave in range(len(ops) + npp - 1):
                        for pi, hp in enumerate(pairs):
                            st = dwave - pi
                            if 0 <= st < len(ops):
                                (eng, fn, rd, wr) = ops[st]
                                curh[0] = dicts[hp] if kind == "pm" else {}
                                emit_one(eng, fn, rd, wr, hp)
                else:
                    for hp in pairs:
                        curh[0] = {}
                        for (eng, fn, rd, wr) in ops:
                            emit_one(eng, fn, rd, wr, hp)

            def post():
                if n % TPG == TPG - 1:
                    P.dma("sp", YAv[:, :, gi * GT:(gi + 1) * GT], ya_g[s2][:], b_ya[s2], reads=[b_ya[s2]])
                if dbg and n == NTL - 1:
                    P.barrier()
                    DBGF = nc.dram_tensor("DBGF", [128, NF, 128], F32, kind="ExternalOutput").ap()
                    DBGH = nc.dram_tensor("DBGH", [128, NH, 128], BF16, kind="ExternalOutput").ap()
                    DBGS = nc.dram_tensor("DBGS", [128, NP, 128], BF16, kind="ExternalOutput").ap()
                    bd = P.buf("dbgd")
                    P.dma("sp", DBGF[:, :, :], Ft[0][:], bd)
                    P.dma("sp", DBGH[:, :, :], Ht[0][:], bd)
                    P.dma("sp", DBGS[:, :, :], Sbd[:], bd)

            kinds = [segkind.get(bounds[k], 'ps') for k in range(nseg)]
            return dict(pre=pre, post=post, nseg=nseg, emit_seg=emit_seg, kinds=kinds)

        NTL = int(os.environ.get('KNT', S // 128))
        descs = {}

        def get_desc(n):
            if n not in descs:
                descs[n] = tile_body(n)
            return descs[n]

        nseg0 = get_desc(0)["nseg"]
        LAG = 0
        total = NTL * nseg0
        G0, G1 = (0, 1, 2, 3, 4, 5), ()
        for i in range(total + LAG):
            if i < total:
                n, si = divmod(i, nseg0)
                dsc = get_desc(n)
                kinds = dsc["kinds"]
                if kinds[si] == "inv":
                    if si == 0 or kinds[si - 1] != "inv":
                        sj = si
                        while sj < nseg0 and kinds[sj] == "inv":
                            sj += 1
                        ninv = sj - si
                        for dwave in range(ninv + len(G0) - 1):
                            for pi, hp in enumerate(G0):
                                st = dwave - pi
                                if 0 <= st < ninv:
                                    dsc["emit_seg"](si + st, (hp,))
                else:
                    dsc["emit_seg"](si, G0)
            j = i - LAG
            if j >= 0:
                n, si = divmod(j, nseg0)
                dsc = get_desc(n)
                if si == 0:
                    dsc["pre"]()
                if G1:
                    dsc["emit_seg"](si, G1)
                if si == nseg0 - 1:
                    dsc["post"]()
                    if n - 1 in descs:
                        del descs[n - 1]

        P.barrier()
```

```python
import math
from contextlib import ExitStack
import numpy as np
import ml_dtypes
import concourse.bass as bass
import concourse.mybir as mybir
from concourse.bass_utils import run_bass_kernel_spmd

F32 = mybir.dt.float32
BF16 = mybir.dt.bfloat16
AF = mybir.ActivationFunctionType
ALU = mybir.AluOpType

D = 1024
S = 4096
NC8 = 8
INW = 6912
DFF = 2816
TT = 512
NT = S // TT

PC = {}
_off = 0
for _n, _w in [("c", 8), ("b_ada", 48), ("g_pre_mix", 8), ("g_post_mix", 8), ("g_pre_ffn", 8),
               ("g_post_ffn", 8), ("mu", 20), ("w0", 6), ("a0", 6), ("k_k", 6), ("k_a", 6),
               ("r_k", 6), ("lnx_w", 6), ("lnx_b", 6), ("inv_freq", 1)]:
    PC[_n] = (_off, _w)
    _off += _w
NPAR = _off


class _Rec:
    def __init__(self):
        self.name = None
        self.args = ()
        self.kwargs = {}

    def __getattr__(self, name):
        def f(*a, **k):
            self.name, self.args, self.kwargs = name, a, k
            return self
        return f


class Buf:
    __slots__ = ("name", "w", "rs", "sem", "semval", "const", "excl")

    def __init__(self, name, const=False):
        self.excl = False
        self.name = name
        self.w = None
        self.rs = {}
        self.sem = None
        self.semval = 0
        self.const = const


class Prog:
    ENGS = ["pe", "act", "dve", "pool", "sp"]

    def __init__(self, nc):
        self.nc = nc
        self.streams = {e: [] for e in self.ENGS}
        self.cnt = {e: 0 for e in self.ENGS}
        self.sems = {e: nc.alloc_semaphore("s_" + e) for e in self.ENGS}
        self.seen = {e: {} for e in self.ENGS}
        self.dma_owners = []
        self.nbuf = 0

    def buf(self, name="b", const=False):
        self.nbuf += 1
        return Buf(f"{name}{self.nbuf}", const)

    def _deps(self, eng, reads, writes):
        evs = []
        for b in reads:
            if b.w is not None:
                evs.append(b.w)
        for b in writes:
            if b.w is not None:
                evs.append(b.w)
            evs.extend(b.rs.values())
        waits = {}
        seen = self.seen[eng]
        for (key, semh, val) in evs:
            if key == "pe" and eng == "pe":
                continue
            if seen.get(key, 0) >= val:
                continue
            if key in waits and waits[key][1] >= val:
                continue
            waits[key] = (semh, val)
        for key, (semh, val) in waits.items():
            seen[key] = val
        return list(waits.values())

    def _record(self, ev, reads, writes):
        for b in reads:
            if not b.const:
                b.rs[ev[0]] = ev
        for b in writes:
            b.w = ev
            b.rs = {}

    def op(self, eng, fn, reads=(), writes=()):
        if any(b.excl for b in reads):
            writes = list(writes) + [b for b in reads if b.excl]
            reads = [b for b in reads if not b.excl]
        waits = self._deps(eng, reads, writes)
        self.cnt[eng] += 1
        ev = (eng, self.sems[eng], self.cnt[eng])
        self.streams[eng].append((waits, fn, self.sems[eng], 1))
        self._record(ev, reads, writes)

    def dma(self, q, out, in_, owner, reads=(), writes=(), **kw):
        waits = self._deps(q, reads, writes)
        if owner.sem is None:
            owner.sem = self.nc.alloc_semaphore("d_" + owner.name)
            self.dma_owners.append(owner)
        owner.semval += 16
        ev = ("dma_" + owner.name, owner.sem, owner.semval)
        self.streams[q].append((waits, lambda e: e.dma_start(out=out, in_=in_, **kw), owner.sem, 16))
        self._record(ev, reads, writes)

    def barrier(self):
        for e in self.ENGS:
            waits = []
            for f in self.ENGS:
                if f != e and self.cnt[f] > self.seen[e].get(f, 0):
                    waits.append((self.sems[f], self.cnt[f]))
                    self.seen[e][f] = self.cnt[f]
            for o in self.dma_owners:
                key = "dma_" + o.name
                if o.semval > self.seen[e].get(key, 0):
                    waits.append((o.sem, o.semval))
                    self.seen[e][key] = o.semval
            if waits:
                self.streams[e].append((waits, None, None, 0))

    def emit(self):
        nc = self.nc
        P = self

        def run(name, e):
            for waits, fn, semh, inc in P.streams[name]:
                for (s, v) in waits:
                    e.wait_ge(s, v)
                if fn is not None:
                    ins = fn(e)
                    ins.then_inc(semh, inc)

        with nc.Block() as block:
            @block.tensor
            def _(e):
                run("pe", e)

            @block.scalar
            def _(e):
                run("act", e)

            @block.vector
            def _(e):
                run("dve", e)

            @block.gpsimd
            def _(e):
                run("pool", e)

            @block.sync
            def _(e):
                run("sp", e)


def build_nc(stage=99, dbg=False, inject=False):
    nc = bass.Bass("TRN2", target_bir_lowering=False)
    P = Prog(nc)
    dram_in = lambda name, shape: nc.dram_tensor(name, shape, F32, kind="ExternalInput").ap()
    xT = dram_in("xT", [D, S])
    params = dram_in("params", [128, NPAR])
    w_ada = dram_in("w_ada", [D, 6 * D])
    w_in = dram_in("w_in", [D, INW])
    w2 = dram_in("w2", [64, 768])
    a2 = dram_in("a2", [64, 768])
    g2 = dram_in("g2", [128, 768])
    w0row = dram_in("w0row", [1, 768])
    tmasks = dram_in("tmasks", [128, 7 * 512])
    w_a = dram_in("w_a", [768, D])
    w_b = dram_in("w_b", [256, D])
    w_out = dram_in("w_out", [D, D])
    w_ffn_in = dram_in("w_ffn_in", [D, 2 * DFF])
    w_ffn_out = dram_in("w_ffn_out", [DFF, D])
    outT = nc.dram_tensor("outT", [D, S], F32, kind="ExternalOutput").ap()
    okind = "ExternalOutput" if dbg else "Internal"
    PT = nc.dram_tensor("PT", [INW, S], BF16, kind=okind).ap()
    YA = nc.dram_tensor("YA", [768, S], BF16, kind=("ExternalInput" if inject else okind)).ap()
    OT = nc.dram_tensor("OT", [256, S], BF16, kind=okind).ap()
    X1 = nc.dram_tensor("X1", [D, S], F32, kind=okind).ap()
    WFI = nc.dram_tensor("WFI_bf", [D, 2 * DFF], BF16, kind="Internal").ap()
    WFO = nc.dram_tensor("WFO_bf", [DFF, D], BF16, kind="Internal").ap()

    es_all = ExitStack()
    sb = lambda es, name, shape, dt: es.enter_context(nc.sbuf_tensor(name, shape, dt))
    ps = [es_all.enter_context(nc.psum_tensor(f"ps{i}", [128, 512], F32)) for i in range(8)]
    psb = [P.buf(f"ps{i}") for i in range(8)]
    for b in psb:
        b.excl = True

    par = sb(es_all, "par", [128, NPAR], F32)
    modv = sb(es_all, "modv", [128, 48], F32)
    coef = sb(es_all, "coef", [128, 48], F32)
    omm = sb(es_all, "omm", [128, 20], F32)
    ones_bf = sb(es_all, "ones_bf", [128, 128], BF16)
    eps_t = sb(es_all, "eps_t", [128, 1], F32)
    b_par, b_modv, b_coef, b_const = P.buf("par"), P.buf("modv"), P.buf("coef"), P.buf("const")
    pcol = lambda n, i=None: (par[:, PC[n][0]:PC[n][0] + PC[n][1]] if i is None
                              else par[:, PC[n][0] + i:PC[n][0] + i + 1])

    P.dma("sp", par[:], params[:, :], b_par, writes=[b_par])
    P.op("pool", lambda e: e.memset(ones_bf[:], 1.0), writes=[b_const])
    P.op("pool", lambda e: e.memset(eps_t[:], 1e-6), writes=[b_const])

    with ExitStack() as es:
        sc = sb(es, "sc", [128, 8], F32)
        b_sc = P.buf("sc")
        P.op("act", lambda e: e.activation(out=sc[:], in_=pcol("c"), func=AF.Silu), reads=[b_par], writes=[b_sc])
        NB = 768
        wa_t = [sb(es, f"wa_t{i}", [128, 8, NB], F32) for i in range(2)]
        b_wa = [P.buf("wa") for _ in range(2)]
        w_ada_v = w_ada.rearrange("(kc p) n -> p kc n", p=128)
        for jb in range(8):
            t, bt = wa_t[jb % 2], b_wa[jb % 2]
            P.dma("sp", t[:], w_ada_v[:, :, jb * NB:(jb + 1) * NB], bt, writes=[bt])
            for j in range(6):
                jj = jb * 6 + j
                for kc in range(8):
                    P.op("pe", lambda e, t=t, j=j, kc=kc, jj=jj: e.matmul(
                        ps[6][:, jj:jj + 1], lhsT=t[:, kc, j * 128:(j + 1) * 128], rhs=sc[:, kc:kc + 1],
                        start=(kc == 0), stop=(kc == 7)), reads=[bt, b_sc], writes=[psb[6]])
        P.op("dve", lambda e: e.tensor_tensor(out=modv[:], in0=ps[6][:, 0:48], in1=pcol("b_ada"), op=ALU.add),
             reads=[psb[6], b_par], writes=[b_modv])
        P.op("dve", lambda e: e.scalar_tensor_tensor(out=coef[:, 0:8], in0=modv[:, 8:16], scalar=1.0, in1=pcol("g_pre_mix"),
                                                     op0=ALU.add, op1=ALU.mult), reads=[b_modv, b_par], writes=[b_coef])
        P.op("dve", lambda e: e.tensor_tensor(out=coef[:, 8:16], in0=modv[:, 16:24], in1=pcol("g_post_mix"), op=ALU.mult),
             reads=[b_modv, b_par], writes=[b_coef])
        P.op("dve", lambda e: e.scalar_tensor_tensor(out=coef[:, 16:24], in0=modv[:, 32:40], scalar=1.0, in1=pcol("g_pre_ffn"),
                                                     op0=ALU.add, op1=ALU.mult), reads=[b_modv, b_par], writes=[b_coef])
        P.op("dve", lambda e: e.tensor_tensor(out=coef[:, 24:32], in0=modv[:, 40:48], in1=pcol("g_post_ffn"), op=ALU.mult),
             reads=[b_modv, b_par], writes=[b_coef])
        P.op("dve", lambda e: e.tensor_scalar(out=omm[:], in0=pcol("mu"), scalar1=-1.0, scalar2=1.0, op0=ALU.mult, op1=ALU.add),
             reads=[b_par], writes=[b_coef])
        P.barrier()
    A_m = lambda kc: coef[:, kc:kc + 1]
    B_m = lambda kc: modv[:, kc:kc + 1]
    GM = lambda kc: coef[:, 8 + kc:9 + kc]
    A_f = lambda kc: coef[:, 16 + kc:17 + kc]
    B_f = lambda kc: modv[:, 24 + kc:25 + kc]
    GF = lambda kc: coef[:, 24 + kc:25 + kc]

    def rms_modulate(es, src_tile, b_src, dst, b_dst, dst_sl, A, Bc, tmpbufs, b_tmp, sqt, b_sq, rs, b_rs, psi):
        P.op("act", lambda e: e.activation(out=sqt[:], in_=src_tile[:], func=AF.Square), reads=[b_src], writes=[b_sq])
        for kc in range(8):
            P.op("pe", lambda e, kc=kc: e.matmul(ps[psi][:], lhsT=ones_bf[:], rhs=sqt[:, kc, :], start=(kc == 0), stop=(kc == 7)),
                 reads=[b_sq, b_const], writes=[psb[psi]])
        P.op("act", lambda e: e.activation(out=rs[:], in_=ps[psi][:], func=AF.Ln, bias=eps_t[:], scale=1.0 / D),
             reads=[psb[psi], b_const], writes=[b_rs])
        P.op("act", lambda e: e.activation(out=rs[:], in_=rs[:], func=AF.Exp, scale=-0.5), reads=[b_rs], writes=[b_rs])
        for kc in range(8):
            tb, btb = tmpbufs[kc % 2], b_tmp[kc % 2]
            P.op("dve", lambda e, kc=kc, tb=tb: e.scalar_tensor_tensor(out=tb[:], in0=src_tile[:, kc, :], scalar=A(kc), in1=rs[:],
                                                                     op0=ALU.mult, op1=ALU.mult),
                 reads=[b_src, b_rs, b_coef], writes=[btb])
            P.op("act", lambda e, kc=kc, tb=tb: e.activation(out=dst[:, kc, dst_sl], in_=tb[:], func=AF.Identity, bias=Bc(kc), scale=1.0),
                 reads=[btb, b_modv], writes=[b_dst])

    with ExitStack() as esA:
        hT = sb(esA, "hT", [128, 8, S], BF16)
        b_hT = [P.buf("hT") for _ in range(NT)]
        xT_v = xT.rearrange("(kc p) t -> p kc t", p=128)
        with ExitStack() as es:
            xt = [sb(es, f"xt{i}", [128, 8, TT], F32) for i in range(2)]
            b_xt = [P.buf("xt") for _ in range(2)]
            sqt = sb(es, "sqt", [128, 8, TT], BF16)
            b_sq = P.buf("sq")
            rs = sb(es, "rs", [128, TT], F32)
            b_rs = P.buf("rs")
            tmpb = [sb(es, f"tmpb{i}", [128, TT], F32) for i in range(2)]
            b_tmp = [P.buf("tmp") for _ in range(2)]
            for tt in range(NT):
                P.dma("sp", xt[tt % 2][:], xT_v[:, :, tt * TT:(tt + 1) * TT], b_xt[tt % 2], writes=[b_xt[tt % 2]])
                rms_modulate(es, xt[tt % 2], b_xt[tt % 2], hT, b_hT[tt], slice(tt * TT, (tt + 1) * TT), A_m, B_m,
                             tmpb, b_tmp, sqt, b_sq, rs, b_rs, 5)
            P.barrier()
        if stage >= 1:
            phaseA_proj(nc, P, esA, sb, ps, psb, hT, b_hT, w_in, PT, par, pcol, omm, b_par, b_coef)
        P.barrier()

    b_out = P.buf("outdma")
    b_wpre = P.buf("wpre")

    def precast_ffn():
        for k in range(8):
            P.dma("pool", WFI[k * 128:(k + 1) * 128, :].rearrange("p (a b) -> p a b", b=512),
                  w_ffn_in[k * 128:(k + 1) * 128, :].rearrange("p (a b) -> p a b", b=512), b_wpre, writes=[b_wpre])
        for k in range(DFF // 128):
            P.dma("pool", WFO[k * 128:(k + 1) * 128, :].rearrange("p (a b) -> p a b", b=512),
                  w_ffn_out[k * 128:(k + 1) * 128, :].rearrange("p (a b) -> p a b", b=512), b_wpre, writes=[b_wpre])

    if stage >= 5 and (stage < 2 or inject):
        precast_ffn()
    if stage >= 2 and not inject:
        phaseB_rwkv(nc, P, sb, ps, psb, PT, YA, par, pcol, b_par, w2, a2, g2, w0row, tmasks, dbg=dbg, after_consts=(precast_ffn if stage >= 5 else None))
    esCD = ExitStack()
    d1w = None
    if stage >= 4:
        stg32 = [sb(esCD, f"stg32_{i}", [128, 1024], F32) for i in range(2)]
        b_stg32 = [P.buf("stg32") for _ in range(2)]
        d1w = (load_weight_bf16(P, sb, esCD, "wa_bf", w_a, 6, D, stg32, b_stg32, eng="act"),
               load_weight_bf16(P, sb, esCD, "wb_bf", w_b, 2, D, stg32, b_stg32, eng="act"),
               load_weight_bf16(P, sb, esCD, "wo_bf", w_out, 8, D, stg32, b_stg32, eng="act"))
    if stage >= 3:
        phaseC_attn(nc, P, sb, ps, psb, PT, OT)
    if stage >= 4:
        phaseD1(nc, P, sb, ps, psb, PT, YA, OT, X1, xT, d1w, GM, ones_bf, eps_t, b_const, b_coef)
    esCD.close()
    if stage >= 5:
        phaseD2(nc, P, sb, ps, psb, X1, outT, WFI, WFO, b_wpre, A_f, B_f, GF, ones_bf, eps_t, b_const, b_coef, b_modv, b_out)
    if stage < 5:
        with ExitStack() as es:
            z = sb(es, "zt", [128, 8, 64], F32)
            bz = P.buf("z")
            P.op("pool", lambda e: e.memset(z[:], 0.0), writes=[bz])
            P.op("dve", lambda e: e.tensor_copy(out=z[:, 0, 0:48], in_=modv[:]), reads=[bz, b_modv], writes=[bz])
            P.dma("sp", outT.rearrange("(kc p) t -> p kc t", p=128)[:, :, 0:64], z[:], b_out, reads=[bz])
    P.barrier()
    P.emit()
    es_all.close()
    return nc


def phaseA_proj(nc, P, esA, sb, ps, psb, hT, b_hT, w_in, PT, par, pcol, omm, b_par, b_coef):
    with ExitStack() as es:
        NW = 3
        wstg = [sb(es, f"wstg{i}", [128, 8, 512], F32) for i in range(2)]
        b_wstg = [P.buf("wstg") for _ in range(2)]
        wbf = [sb(es, f"wbf{i}", [128, 8, 128], BF16) for i in range(NW)]
        b_wbf = [P.buf("wbf") for _ in range(NW)]
        stg = [sb(es, f"stg{i}", [128, S], BF16) for i in range(3)]
        b_stg = [P.buf("stg") for _ in range(3)]
        mup = [sb(es, f"mup{i}", [128, S + 1], F32) for i in range(2)]
        b_mup = [[P.buf("mup") for _ in range(NT + 1)] for _ in range(2)]
        cosT = sb(es, "cosT", [128, S], F32)
        sinT = sb(es, "sinT", [128, S], F32)
        perm = sb(es, "perm", [128, 128], BF16)
        ident = sb(es, "ident", [128, 128], BF16)
        qraw = [sb(es, f"qraw{i}", [128, TT], BF16) for i in range(2)]
        b_qraw = [P.buf("qraw") for _ in range(2)]
        t1 = [sb(es, f"t1_{i}", [128, TT], F32) for i in range(2)]
        b_t1 = [P.buf("t1") for _ in range(2)]
        t2 = [sb(es, f"t2_{i}", [128, TT], F32) for i in range(2)]
        b_t2 = [P.buf("t2") for _ in range(2)]
        sgn = sb(es, "sgn", [128, 1], F32)
        pi_t = sb(es, "pi_t", [128, 1], F32)
        b_tab = P.buf("tab")
        P.op("pool", lambda e: e.memset(ident[:], 1.0), writes=[b_tab])
        P.op("pool", lambda e: e.affine_select(out=ident[:], in_=ident[:], pattern=[[-1, 128]], compare_op=ALU.is_equal,
                                               fill=0.0, base=0, channel_multiplier=1), reads=[b_tab], writes=[b_tab])
        for h0 in (0, 64):
            P.op("pool", lambda e, h0=h0: e.tensor_copy(out=perm[:, h0:h0 + 32], in_=ident[:, h0 + 32:h0 + 64]), reads=[b_tab], writes=[b_tab])
            P.op("pool", lambda e, h0=h0: e.tensor_copy(out=perm[:, h0 + 32:h0 + 64], in_=ident[:, h0:h0 + 32]), reads=[b_tab], writes=[b_tab])
        for q4 in range(4):
            P.op("pool", lambda e, q4=q4: e.memset(sgn[q4 * 32:(q4 + 1) * 32, :], -1.0 if q4 % 2 == 0 else 1.0), writes=[b_tab])
        P.op("pool", lambda e: e.memset(pi_t[:], -math.pi), writes=[b_tab])
        for m in (0, 1):
            P.op("pool", lambda e, m=m: e.memset(mup[m][:, 0:1], 0.0), writes=[b_mup[m][0]])
        with ExitStack() as es2:
            ang_t = wstg[0]
            ki = mup[1][:, 1:S + 1].bitcast(mybir.dt.int32)
            kf = mup[0][:, 1:S + 1]
            P.op("pool", lambda e: e.iota(ang_t[:].rearrange("p a b -> p (a b)"), pattern=[[1, S]], base=0, channel_multiplier=0, allow_small_or_imprecise_dtypes=True),
                 reads=[b_tab], writes=[b_tab])
            P.op("dve", lambda e: e.tensor_scalar(out=ang_t[:].rearrange("p a b -> p (a b)"), in0=ang_t[:].rearrange("p a b -> p (a b)"), scalar1=pcol("inv_freq"), scalar2=1.0 / (2 * math.pi),
                                                  op0=ALU.mult, op1=ALU.mult), reads=[b_tab, b_par], writes=[b_tab])
            for (tab, addc) in ((sinT, 0.0), (cosT, 0.25)):
                P.op("dve", lambda e, tab=tab, addc=addc: e.tensor_scalar(out=tab[:], in0=ang_t[:].rearrange("p a b -> p (a b)"), scalar1=addc, scalar2=None, op0=ALU.add),
                     reads=[b_tab], writes=[b_tab])
                P.op("dve", lambda e, tab=tab: e.tensor_copy(out=ki, in_=tab[:]), reads=[b_tab], writes=[b_tab])
                P.op("dve", lambda e, tab=tab: e.tensor_copy(out=kf, in_=ki), reads=[b_tab], writes=[b_tab])
                P.op("dve", lambda e, tab=tab: e.tensor_tensor(out=tab[:], in0=tab[:], in1=kf, op=ALU.subtract), reads=[b_tab], writes=[b_tab])
                P.op("dve", lambda e, tab=tab: e.tensor_scalar(out=kf, in0=tab[:], scalar1=0.5, scalar2=None, op0=ALU.is_gt), reads=[b_tab], writes=[b_tab])
                P.op("dve", lambda e, tab=tab: e.tensor_tensor(out=tab[:], in0=tab[:], in1=kf, op=ALU.subtract), reads=[b_tab], writes=[b_tab])
                P.op("dve", lambda e, tab=tab: e.tensor_scalar(out=kf, in0=tab[:], scalar1=-0.5, scalar2=None, op0=ALU.is_lt), reads=[b_tab], writes=[b_tab])
                P.op("dve", lambda e, tab=tab: e.tensor_tensor(out=tab[:], in0=tab[:], in1=kf, op=ALU.add), reads=[b_tab], writes=[b_tab])
                P.op("act", lambda e, tab=tab: e.activation(out=tab[:], in_=tab[:], func=AF.Sin, scale=2 * math.pi - 2e-6), reads=[b_tab], writes=[b_tab])
            P.op("dve", lambda e: e.tensor_scalar(out=sinT[:], in0=sinT[:], scalar1=sgn[:], scalar2=None, op0=ALU.mult), reads=[b_tab], writes=[b_tab])
            P.barrier()
        b_tab.const = True

        w_in_v = w_in.rearrange("(kc p) n -> p kc n", p=128)
        NCH = INW // 128
        import os
        order = [int(v) for v in os.environ['KCH'].split(',')] if 'KCH' in os.environ else list(range(NCH))
        NCH = len(order)

        WG = 4

        def load_w(i):
            if i % WG == 0:
                gsl = (i // WG) % 2
                c0 = order[i] * 128
                ncol = 128 * min(WG, NCH - i)
                P.dma("sp", wstg[gsl][:, :, 0:ncol], w_in_v[:, :, c0:c0 + ncol], b_wstg[gsl], writes=[b_wstg[gsl]])
            gsl = (i // WG) % 2
            s_ = i % NW
            j = i % WG
            P.op("pool", lambda e, s_=s_, gsl=gsl, j=j: e.tensor_copy(out=wbf[s_][:], in_=wstg[gsl][:, :, j * 128:(j + 1) * 128]),
                 reads=[b_wstg[gsl]], writes=[b_wbf[s_]])

        load_w(0)
        if NCH > 1:
            load_w(1)
        pidx = 0
        ridx = 0
        for i, cc in enumerate(order):
            if i + 2 < NCH:
                load_w(i + 2)
            s = i % NW
            so = i % 3
            sg, bsg = stg[so], b_stg[so]
            mu_i = cc if cc < 20 else None
            mm = i % 2
            for tt in range(NT):
                pi = pidx % 5
                pidx += 1
                tsl = slice(tt * TT, (tt + 1) * TT)
                for kc in range(8):
                    P.op("pe", lambda e, pi=pi, s=s, kc=kc, tsl=tsl: e.matmul(ps[pi][:], lhsT=wbf[s][:, kc, :], rhs=hT[:, kc, tsl],
                                                                          start=(kc == 0), stop=(kc == 7)),
                         reads=[b_wbf[s], b_hT[tt]], writes=[psb[pi]])
                if cc < 20:
                    mcol = pcol("mu", cc)
                    ocol = omm[:, cc:cc + 1]
                    P.op("act", lambda e, pi=pi, mm=mm, tt=tt, mcol=mcol: e.activation(
                        out=mup[mm][:, 1 + tt * TT:1 + (tt + 1) * TT], in_=ps[pi][:], func=AF.Copy, scale=mcol),
                        reads=[psb[pi], b_par], writes=[b_mup[mm][tt + 1]])
                    if cc < 18:
                        P.op("dve", lambda e, pi=pi, mm=mm, tt=tt, ocol=ocol, sg=sg, tsl=tsl: e.scalar_tensor_tensor(
                            out=sg[:, tsl], in0=ps[pi][:], scalar=ocol, in1=mup[mm][:, tt * TT:(tt + 1) * TT], op0=ALU.mult, op1=ALU.add),
                            reads=[psb[pi], b_coef, b_mup[mm][tt], b_mup[mm][tt + 1]], writes=[bsg])
                    else:
                        tb, btb = t1[tt % 2], b_t1[tt % 2]
                        P.op("dve", lambda e, pi=pi, mm=mm, tt=tt, ocol=ocol, tb=tb: e.scalar_tensor_tensor(
                            out=tb[:], in0=ps[pi][:], scalar=ocol, in1=mup[mm][:, tt * TT:(tt + 1) * TT], op0=ALU.mult, op1=ALU.add),
                            reads=[psb[pi], b_coef, b_mup[mm][tt], b_mup[mm][tt + 1]], writes=[btb])
                        if cc == 18:
                            P.op("act", lambda e, tb=tb, sg=sg, tsl=tsl: e.activation(out=sg[0:64, tsl], in_=tb[0:64, :], func=AF.Tanh),
                                 reads=[btb], writes=[bsg])
                            P.op("act", lambda e, tb=tb, sg=sg, tsl=tsl: e.activation(out=sg[64:128, tsl], in_=tb[64:128, :], func=AF.Copy),
                                 reads=[btb], writes=[bsg])
                        else:
                            P.op("act", lambda e, tb=tb, sg=sg, tsl=tsl: e.activation(out=sg[:, tsl], in_=tb[:], func=AF.Sigmoid),
                                 reads=[btb], writes=[bsg])
                elif cc < 32:
                    ri = ridx % 2
                    ridx += 1
                    qr, bqr = qraw[ri], b_qraw[ri]
                    P.op("act", lambda e, pi=pi, qr=qr: e.activation(out=qr[:], in_=ps[pi][:], func=AF.Copy), reads=[psb[pi]], writes=[bqr])
                    P.op("pe", lambda e, ri=ri, qr=qr: e.matmul(ps[5 + ri][:], lhsT=perm[:], rhs=qr[:], start=True, stop=True),
                         reads=[bqr, b_tab], writes=[psb[5 + ri]])
                    P.op("dve", lambda e, pi=pi, ri=ri, tsl=tsl: e.tensor_tensor(out=t1[ri][:], in0=ps[pi][:], in1=cosT[:, tsl], op=ALU.mult),
                         reads=[psb[pi], b_tab], writes=[b_t1[ri]])
                    P.op("dve", lambda e, ri=ri, tsl=tsl: e.tensor_tensor(out=t2[ri][:], in0=ps[5 + ri][:], in1=sinT[:, tsl], op=ALU.mult),
                         reads=[psb[5 + ri], b_tab], writes=[b_t2[ri]])
                    P.op("dve", lambda e, ri=ri, sg=sg, tsl=tsl: e.tensor_tensor(out=sg[:, tsl], in0=t1[ri][:], in1=t2[ri][:], op=ALU.add),
                         reads=[b_t1[ri], b_t2[ri]], writes=[bsg])
                elif cc < 38:
                    P.op("act", lambda e, pi=pi, sg=sg, tsl=tsl: e.activation(out=sg[:, tsl], in_=ps[pi][:], func=AF.Copy),
                         reads=[psb[pi]], writes=[bsg])
                else:
                    P.op("act", lambda e, pi=pi, sg=sg, tsl=tsl: e.activation(out=sg[:, tsl], in_=ps[pi][:], func=AF.Sigmoid),
                         reads=[psb[pi]], writes=[bsg])
            P.dma("sp", PT[cc * 128:(cc + 1) * 128, :], sg[:], bsg, reads=[bsg])
        P.barrier()


def _pack_params(inp, b):
    cols = np.zeros((128, NPAR), np.float32)

    def put(name, vec):
        o, w = PC[name]
        cols[:, o:o + w] = np.asarray(vec, np.float32).reshape(w, 128).T

    put("c", inp["c"][b])
    put("b_ada", inp["b_ada"][0])
    for n in ("g_pre_mix", "g_post_mix", "g_pre_ffn", "g_post_ffn", "w0", "a0", "k_k", "k_a", "lnx_w", "lnx_b"):
        put(n, inp[n][0])
    put("mu", inp["mu_shift"][0])
    put("r_k", inp["r_k"][0].reshape(-1))
    half = 32
    inv_freq = (10000.0 ** (-np.arange(half, dtype=np.float32) / half)).astype(np.float32)
    cols[:, PC["inv_freq"][0]] = np.tile(inv_freq, 4)
    return cols


def _tri_masks():
    idx = np.arange(128)
    out = np.zeros((128, 7, 4, 128), np.float32)
    for lvl in range(7):
        b = 1 << lvl
        mU = ((idx[:, None] // b) % 2 == 0) & ((idx[None, :] // b) == (idx[:, None] // b) + 1)
        mL = mU.T
        out[:, lvl, 0], out[:, lvl, 1], out[:, lvl, 2], out[:, lvl, 3] = mU, mL, mU, mL
    return np.ascontiguousarray(out.reshape(128, 7 * 512))


def make_in_maps(inp):
    shared = {
        "tmasks": _tri_masks(),
        "w_ada": np.ascontiguousarray(inp["w_ada"][0]), "w_in": np.ascontiguousarray(inp["w_in"][0]),
        "w2": np.ascontiguousarray(inp["w2"][0]), "a2": np.ascontiguousarray(inp["a2"][0]),
        "g2": np.ascontiguousarray(inp["g2"][0]), "w_a": np.ascontiguousarray(inp["w_a"][0]),
        "w_b": np.ascontiguousarray(inp["w_b"][0]), "w_out": np.ascontiguousarray(inp["w_out"][0]),
        "w_ffn_in": np.ascontiguousarray(inp["w_ffn_in"][0]), "w_ffn_out": np.ascontiguousarray(inp["w_ffn_out"][0]),
    }
    maps = []
    for b in range(NC8):
        m = dict(shared)
        m["xT"] = np.ascontiguousarray(np.asarray(inp["x"][b], np.float32).T)
        m["params"] = _pack_params(inp, b)
        m["w0row"] = np.ascontiguousarray(inp["w0"][0].reshape(1, 768))
        maps.append(m)
    return maps


def kernel(**inputs):
    inp = {k: np.asarray(v) for k, v in inputs.items()}
    nc = build_nc()
    in_maps = make_in_maps(inp)
    res = run_bass_kernel_spmd(nc, in_maps, core_ids=list(range(NC8)))
    out = np.stack([np.ascontiguousarray(r["outT"].T) for r in res.results], axis=0)
    return out.astype(np.float32)


def phaseC_attn(nc, P, sb, ps, psb, PT, OT):
    with ExitStack() as es:
        ident = sb(es, "identC", [128, 128], BF16)
        maskT = sb(es, "maskT", [128, 256], BF16)
        ones64 = sb(es, "ones64", [128, 64], BF16)
        b_c = P.buf("constC")
        P.op("pool", lambda e: e.memset(ident[:], 1.0), writes=[b_c])
        P.op("pool", lambda e: e.affine_select(out=ident[:], in_=ident[:], pattern=[[-1, 128]], compare_op=ALU.is_equal,
                                               fill=0.0, base=0, channel_multiplier=1), reads=[b_c], writes=[b_c])
        P.op("pool", lambda e: e.memset(maskT[:], 1.0), reads=[b_c], writes=[b_c])
        P.op("pool", lambda e: e.affine_select(out=maskT[:, 0:128], in_=maskT[:, 0:128], pattern=[[-1, 128]], compare_op=ALU.is_ge,
                                               fill=0.0, base=0, channel_multiplier=1), reads=[b_c], writes=[b_c])
        P.op("pool", lambda e: e.affine_select(out=maskT[:, 128:256], in_=maskT[:, 128:256], pattern=[[1, 128]], compare_op=ALU.is_ge,
                                               fill=0.0, base=0, channel_multiplier=-1), reads=[b_c], writes=[b_c])
        P.op("pool", lambda e: e.memset(ones64[:], 1.0), reads=[b_c], writes=[b_c])
        P.barrier()
        b_c.const = True
        qkv = [[sb(es, f"qkv{i}_{j}", [128, S], BF16) for j in range(3)] for i in range(2)]
        b_qkv = [[P.buf("qkv") for j in range(3)] for i in range(2)]
        vtok = sb(es, "vtok", [128, 32, 128], BF16)
        b_vtok = [P.buf("vtok") for _ in range(8)]
        acc = sb(es, "acc", [128, 2, S], F32)
        b_acc = P.buf("acc")
        NPT = 4
        pT = [sb(es, f"pT{i}", [128, 2, 256], BF16) for i in range(NPT)]
        b_pT = [P.buf("pT") for _ in range(NPT)]
        o_bf = sb(es, "o_bf", [128, S], BF16)
        b_obf = P.buf("obf")
        rec = sb(es, "rec", [128, S], F32)
        b_rec = P.buf("rec")
        ps6b = ps[6][:].bitcast(BF16)

        def load_pair(idx, pp):
            st = idx % 2
            for j, base in enumerate((2560, 3328, 4096)):
                r0 = base + pp * 128
                P.dma("sp", qkv[st][j][:], PT[r0:r0 + 128, :], b_qkv[st][j], writes=[b_qkv[st][j]])

        seq = [(spn, g) for spn in (0, 1) for g in (0, 1, 2)]
        load_pair(0, seq[0][1] * 2 + seq[0][0])
        cnt = 0
        for idx, (spn, g) in enumerate(seq):
            pp = g * 2 + spn
            if idx + 1 < len(seq):
                load_pair(idx + 1, seq[idx + 1][1] * 2 + seq[idx + 1][0])
            st = idx % 2
            d = (1, 4, 16)[g]
            nb = S // d // 128
            qT, kT, vT = qkv[st]
            bq, bk, bv = b_qkv[st]
            view = lambda t: t[:].rearrange("p (n i r) -> p r n i", i=128, r=d)
            qv, kv, vv = view(qT), view(kT), view(vT)
            accv = acc[:].rearrange("p c (n i r) -> p c r n i", i=128, r=d)
            for g4 in range(8):
                for j in range(4):
                    b = g4 * 4 + j
                    r, n = b // nb, b % nb
                    P.op("pe", lambda e, j=j, r=r, n=n, vv=vv: e.transpose(ps6b[:, j * 128:(j + 1) * 128], vv[:, r, n, :], ident[:]),
                         reads=[bv, b_c], writes=[psb[6]])
                P.op("act", lambda e, g4=g4: e.activation(out=vtok[:, g4 * 4:(g4 + 1) * 4, :],
                                                          in_=ps6b[:, 0:512].rearrange("p (j c) -> p j c", c=128), func=AF.Copy),
                     reads=[psb[6]], writes=[b_vtok[g4]])
            def part1(b, cnt_, kv=kv, qv=qv, bk=bk, bq=bq, nb=nb):
                r, n = b // nb, b % nb
                np_ = n - 1 if n > 0 else n
                slot = cnt_ % NPT
                sbk = cnt_ % 3
                for h in (0, 1):
                    hs = slice(64 * h, 64 * h + 64)
                    bank = ((0, 1, 6), (2, 3, 7))[h][sbk]
                    P.op("pe", lambda e, bank=bank, hs=hs, r=r, np_=np_, n=n: e.matmul(
                        ps[bank][:, 0:128], lhsT=kv[hs, r, np_, :], rhs=qv[hs, r, n, :], start=True, stop=True),
                        reads=[bk, bq], writes=[psb[bank]])
                    P.op("pe", lambda e, bank=bank, hs=hs, r=r, n=n: e.matmul(
                        ps[bank][:, 128:256], lhsT=kv[hs, r, n, :], rhs=qv[hs, r, n, :], start=True, stop=True),
                        reads=[bk, bq], writes=[psb[bank]])
                    P.op("act", lambda e, bank=bank, slot=slot, h=h: e.activation(out=pT[slot][:, h, :], in_=ps[bank][:, 0:256],
                                                                                 func=AF.Exp, scale=0.125),
                         reads=[psb[bank]], writes=[b_pT[slot]])
                for h in (0, 1):
                    P.op("dve", lambda e, slot=slot, h=h: e.tensor_tensor(out=pT[slot][:, h, :], in0=pT[slot][:, h, :], in1=maskT[:], op=ALU.mult),
                         reads=[b_pT[slot], b_c], writes=[b_pT[slot]])

            def part2(b, cnt_, accv=accv, g=g, nb=nb):
                r, n = b // nb, b % nb
                bprev = b - 1 if n > 0 else b
                slot = cnt_ % NPT
                ob = 4 + cnt_ % 2
                for h in (0, 1):
                    hs = slice(64 * h, 64 * h + 64)
                    if n > 0:
                        P.op("pe", lambda e, ob=ob, hs=hs, bprev=bprev, slot=slot, h=h: e.matmul(
                            ps[ob][hs, 0:128], lhsT=vtok[:, bprev, hs], rhs=pT[slot][:, h, 0:128], start=True, stop=False),
                            reads=[b_vtok[bprev // 4], b_pT[slot]], writes=[psb[ob]])
                    P.op("pe", lambda e, ob=ob, hs=hs, b=b, slot=slot, h=h, n=n: e.matmul(
                        ps[ob][hs, 0:128], lhsT=vtok[:, b, hs], rhs=pT[slot][:, h, 128:256], start=(n == 0), stop=True),
                        reads=[b_vtok[b // 4], b_pT[slot]], writes=[psb[ob]])
                    if n > 0:
                        P.op("pe", lambda e, ob=ob, hs=hs, slot=slot, h=h: e.matmul(
                            ps[ob][hs, 128:256], lhsT=ones64[:], rhs=pT[slot][:, h, 0:128], start=True, stop=False),
                            reads=[b_c, b_pT[slot]], writes=[psb[ob]])
                    P.op("pe", lambda e, ob=ob, hs=hs, slot=slot, h=h, n=n: e.matmul(
                        ps[ob][hs, 128:256], lhsT=ones64[:], rhs=pT[slot][:, h, 128:256], start=(n == 0), stop=True),
                        reads=[b_c, b_pT[slot]], writes=[psb[ob]])
                src = ps[ob][:, 0:256].rearrange("p (c i) -> p c i", i=128)
                if g == 0:
                    P.op("dve", lambda e, src=src, r=r, n=n: e.tensor_copy(out=accv[:, :, r, n, :], in_=src),
                         reads=[psb[ob]], writes=[b_acc])
                else:
                    P.op("dve", lambda e, src=src, r=r, n=n: e.tensor_tensor(out=accv[:, :, r, n, :], in0=src, in1=accv[:, :, r, n, :], op=ALU.add),
                         reads=[psb[ob], b_acc], writes=[b_acc])

            part1(0, cnt)
            part1(1, cnt + 1)
            for b in range(32):
                if b + 2 < 32:
                    part1(b + 2, cnt + b + 2)
                part2(b, cnt + b)
            cnt += 32
            if g == 2:
                P.op("dve", lambda e: e.reciprocal(out=rec[:], in_=acc[:, 1, :]), reads=[b_acc], writes=[b_rec])
                P.op("pool", lambda e: e.tensor_tensor(out=o_bf[:], in0=acc[:, 0, :], in1=rec[:], op=ALU.mult), reads=[b_acc, b_rec], writes=[b_obf])
                P.dma("sp", OT[spn * 128:(spn + 1) * 128, :], o_bf[:], b_obf, reads=[b_obf])
        P.barrier()


def load_weight_bf16(P, sb, es, name, w_dram, nk, ncols, stg32, b_stg32, cnt0=0, eng="pool"):
    wt = sb(es, name, [128, nk, ncols], BF16)
    bw = P.buf(name)
    for kc in range(nk):
        si = (cnt0 + kc) % len(stg32)
        for c0 in range(0, ncols, 1024):
            c1 = min(ncols, c0 + 1024)
            P.dma("sp", stg32[si][:, 0:c1 - c0], w_dram[kc * 128:(kc + 1) * 128, c0:c1], b_stg32[si], writes=[b_stg32[si]])
            if eng == "act":
                P.op("act", lambda e, si=si, kc=kc, c0=c0, c1=c1: e.activation(out=wt[:, kc, c0:c1], in_=stg32[si][:, 0:c1 - c0], func=AF.Copy),
                     reads=[b_stg32[si]], writes=[bw])
            else:
                P.op(eng, lambda e, si=si, kc=kc, c0=c0, c1=c1: e.tensor_copy(out=wt[:, kc, c0:c1], in_=stg32[si][:, 0:c1 - c0]),
                     reads=[b_stg32[si]], writes=[bw])
            si = (si + 1) % len(stg32)
    return wt, bw


def phaseD1(nc, P, sb, ps, psb, PT, YA, OT, X1, xT, d1w, GM, ones_bf, eps_t, b_const, b_coef):
    with ExitStack() as es:
        (wa, b_wa), (wb, b_wb), (wo, b_wo) = d1w
        ya_t = [sb(es, f"ya_t{i}", [128, 6, TT], BF16) for i in range(2)]
        o_t = [sb(es, f"o_t{i}", [128, 2, TT], BF16) for i in range(2)]
        sga_t = [sb(es, f"sga_t{i}", [128, 8, TT], BF16) for i in range(2)]
        sgb_t = [sb(es, f"sgb_t{i}", [128, 8, TT], BF16) for i in range(2)]
        x_t = [sb(es, f"x_t{i}", [128, 8, TT], F32) for i in range(2)]
        b_in = [[P.buf("d1in") for _ in range(5)] for _ in range(2)]
        merged = sb(es, "merged", [128, 8, TT], BF16)
        b_merged = P.buf("merged")
        m3 = sb(es, "m3", [128, 8, TT], F32)
        b_m3 = P.buf("m3")
        sq = sb(es, "sqD", [128, 8, TT], BF16)
        b_sq = P.buf("sqD")
        rs = sb(es, "rsD", [128, TT], F32)
        b_rs = P.buf("rsD")
        m1 = [sb(es, f"m1_{i}", [128, TT], F32) for i in range(2)]
        m2 = [sb(es, f"m2_{i}", [128, TT], F32) for i in range(2)]
        b_m1 = [P.buf("m1") for _ in range(2)]
        b_m2 = [P.buf("m2") for _ in range(2)]
        x1_t = [sb(es, f"x1_t{i}", [128, 8, TT], F32) for i in range(2)]
        b_x1 = [P.buf("x1t") for _ in range(2)]
        YAv = YA.rearrange("(kc p) t -> p kc t", p=128)
        OTv = OT.rearrange("(kc p) t -> p kc t", p=128)
        GAv = PT[4864:5888, :].rearrange("(kc p) t -> p kc t", p=128)
        GBv = PT[5888:6912, :].rearrange("(kc p) t -> p kc t", p=128)
        xTv = xT.rearrange("(kc p) t -> p kc t", p=128)
        X1v = X1.rearrange("(kc p) t -> p kc t", p=128)

        def loads(tt):
            s2 = tt % 2
            tsl = slice(tt * TT, (tt + 1) * TT)
            for j, (dst, src) in enumerate(((ya_t, YAv), (o_t, OTv), (sga_t, GAv), (sgb_t, GBv), (x_t, xTv))):
                P.dma("sp", dst[s2][:], src[:, :, tsl], b_in[s2][j], writes=[b_in[s2][j]])

        merged2 = [merged, sb(es, "merged_b", [128, 8, TT], BF16)]
        b_merged2 = [b_merged, P.buf("merged_b")]
        cnt = [0]

        def s1(tt):
            s2 = tt % 2
            mg, bmg = merged2[s2], b_merged2[s2]
            for jc in range(8):
                c = cnt[0]
                cnt[0] += 1
                pa, pb = c % 2, 2 + c % 2
                mi = c % 2
                js = slice(jc * 128, (jc + 1) * 128)
                for kc in range(6):
                    P.op("pe", lambda e, pa=pa, kc=kc, js=js, s2=s2: e.matmul(ps[pa][:], lhsT=wa[:, kc, js], rhs=ya_t[s2][:, kc, :],
                                                                           start=(kc == 0), stop=(kc == 5)),
                         reads=[b_wa, b_in[s2][0]], writes=[psb[pa]])
                for kc in range(2):
                    P.op("pe", lambda e, pb=pb, kc=kc, js=js, s2=s2: e.matmul(ps[pb][:], lhsT=wb[:, kc, js], rhs=o_t[s2][:, kc, :],
                                                                           start=(kc == 0), stop=(kc == 1)),
                         reads=[b_wb, b_in[s2][1]], writes=[psb[pb]])
                P.op("dve", lambda e, pa=pa, mi=mi, jc=jc, s2=s2: e.tensor_tensor(out=m1[mi][:], in0=ps[pa][:], in1=sga_t[s2][:, jc, :], op=ALU.mult),
                     reads=[psb[pa], b_in[s2][2]], writes=[b_m1[mi]])
                P.op("dve", lambda e, pb=pb, mi=mi, jc=jc, s2=s2: e.tensor_tensor(out=m2[mi][:], in0=ps[pb][:], in1=sgb_t[s2][:, jc, :], op=ALU.mult),
                     reads=[psb[pb], b_in[s2][3]], writes=[b_m2[mi]])
                P.op("dve", lambda e, mi=mi, jc=jc, mg=mg: e.tensor_tensor(out=mg[:, jc, :], in0=m1[mi][:], in1=m2[mi][:], op=ALU.add),
                     reads=[b_m1[mi], b_m2[mi]], writes=[bmg])

        def s2f(tt):
            s2 = tt % 2
            mg, bmg = merged2[s2], b_merged2[s2]
            for jc in range(8):
                po = 4 + jc % 2
                js = slice(jc * 128, (jc + 1) * 128)
                for kc in range(8):
                    P.op("pe", lambda e, po=po, kc=kc, js=js, mg=mg: e.matmul(ps[po][:], lhsT=wo[:, kc, js], rhs=mg[:, kc, :],
                                                                           start=(kc == 0), stop=(kc == 7)),
                         reads=[b_wo, bmg], writes=[psb[po]])
                P.op("act", lambda e, po=po, jc=jc: e.activation(out=m3[:, jc, :], in_=ps[po][:], func=AF.Copy), reads=[psb[po]], writes=[b_m3])
            P.op("act", lambda e: e.activation(out=sq[:], in_=m3[:], func=AF.Square), reads=[b_m3], writes=[b_sq])
            for kc in range(8):
                P.op("pe", lambda e, kc=kc: e.matmul(ps[6][:], lhsT=ones_bf[:], rhs=sq[:, kc, :], start=(kc == 0), stop=(kc == 7)),
                     reads=[b_sq, b_const], writes=[psb[6]])
            P.op("act", lambda e: e.activation(out=rs[:], in_=ps[6][:], func=AF.Ln, bias=eps_t[:], scale=1.0 / D),
                 reads=[psb[6], b_const], writes=[b_rs])
            P.op("act", lambda e: e.activation(out=rs[:], in_=rs[:], func=AF.Exp, scale=-0.5), reads=[b_rs], writes=[b_rs])

        def s3(tt):
            s2 = tt % 2
            tsl = slice(tt * TT, (tt + 1) * TT)
            for jc in range(8):
                mi = jc % 2
                P.op("dve", lambda e, mi=mi, jc=jc: e.scalar_tensor_tensor(out=m1[mi][:], in0=m3[:, jc, :], scalar=GM(jc), in1=rs[:],
                                                                         op0=ALU.mult, op1=ALU.mult),
                     reads=[b_m3, b_rs, b_coef], writes=[b_m1[mi]])
                P.op("dve", lambda e, mi=mi, jc=jc, s2=s2: e.tensor_tensor(out=x1_t[s2][:, jc, :], in0=m1[mi][:], in1=x_t[s2][:, jc, :], op=ALU.add),
                     reads=[b_m1[mi], b_in[s2][4]], writes=[b_x1[s2]])
            P.dma("sp", X1v[:, :, tsl], x1_t[s2][:], b_x1[s2], reads=[b_x1[s2]])

        loads(0)
        if NT > 1:
            loads(1)
        s1(0)
        for tt in range(NT):
            s2f(tt)
            if tt + 1 < NT:
                s1(tt + 1)
            s3(tt)
            if tt + 2 < NT:
                loads(tt + 2)
        P.barrier()


def phaseD2(nc, P, sb, ps, psb, X1, outT, WFI, WFO, b_wpre, A_f, B_f, GF, ones_bf, eps_t, b_const, b_coef, b_modv, b_out):
    T2 = 256
    NT2 = S // T2
    NH = DFF // 128
    with ExitStack() as es:
        wfi = sb(es, "wfi", [128, 8, 2 * DFF], BF16)
        wfo = sb(es, "wfo", [128, NH, D], BF16)
        b_wfi, b_wfo = P.buf("wfi"), P.buf("wfo")
        WFIv = WFI.rearrange("(kc p) n -> p kc n", p=128)
        WFOv = WFO.rearrange("(kc p) n -> p kc n", p=128)
        for kc in range(8):
            P.dma("sp", wfi[:, kc, :], WFIv[:, kc, :], b_wfi, reads=[b_wpre], writes=[b_wfi])
        for k0 in range(0, NH, 11):
            P.dma("sp", wfo[:, k0:k0 + 11, :], WFOv[:, k0:k0 + 11, :], b_wfo, reads=[b_wpre], writes=[b_wfo])
        x1_t = [sb(es, f"x1f{i}", [128, 8, T2], F32) for i in range(2)]
        b_x1 = [P.buf("x1f") for _ in range(2)]
        sq = sb(es, "sqF", [128, 8, T2], BF16)
        b_sq = P.buf("sqF")
        rs = sb(es, "rsF", [128, T2], F32)
        b_rs = P.buf("rsF")
        tmpb = [sb(es, f"tmpF{i}", [128, T2], F32) for i in range(2)]
        b_tmp = [P.buf("tmpF") for _ in range(2)]
        h2 = sb(es, "h2", [128, 8, T2], BF16)
        b_h2 = P.buf("h2")
        su = [sb(es, f"su{i}", [128, T2], F32) for i in range(2)]
        b_su = [P.buf("su") for _ in range(2)]
        actT = sb(es, "actT", [128, NH, T2], BF16)
        b_act = P.buf("actT")
        f_t = sb(es, "f_t", [128, 8, T2], F32)
        b_f = P.buf("f_t")
        X1v = X1.rearrange("(kc p) t -> p kc t", p=128)
        outv = outT.rearrange("(kc p) t -> p kc t", p=128)
        h2b = [h2, sb(es, "h2b", [128, 8, T2], BF16)]
        b_h2b = [b_h2, P.buf("h2b")]
        sqE = sb(es, "sqE", [128, 8, T2], BF16)
        b_sqE = P.buf("sqE")
        rsE = sb(es, "rsE", [128, T2], F32)
        b_rsE = P.buf("rsE")
        cnt = [0]

        def load_x1(tt):
            P.dma("sp", x1_t[tt % 2][:], X1v[:, :, tt * T2:(tt + 1) * T2], b_x1[tt % 2], writes=[b_x1[tt % 2]])

        def pro(tt):
            s2 = tt % 2
            xt = x1_t[s2]
            hh, bhh = h2b[s2], b_h2b[s2]
            P.op("act", lambda e, xt=xt: e.activation(out=sq[:], in_=xt[:], func=AF.Square), reads=[b_x1[s2]], writes=[b_sq])
            for kc in range(8):
                P.op("pe", lambda e, kc=kc: e.matmul(ps[6][:, 0:T2], lhsT=ones_bf[:], rhs=sq[:, kc, :], start=(kc == 0), stop=(kc == 7)),
                     reads=[b_sq, b_const], writes=[psb[6]])
            P.op("act", lambda e: e.activation(out=rs[:], in_=ps[6][:, 0:T2], func=AF.Ln, bias=eps_t[:], scale=1.0 / D),
                 reads=[psb[6], b_const], writes=[b_rs])
            P.op("act", lambda e: e.activation(out=rs[:], in_=rs[:], func=AF.Exp, scale=-0.5), reads=[b_rs], writes=[b_rs])
            for kc in range(8):
                tb, btb = tmpb[kc % 2], b_tmp[kc % 2]
                P.op("dve", lambda e, kc=kc, tb=tb, xt=xt: e.scalar_tensor_tensor(out=tb[:], in0=xt[:, kc, :], scalar=A_f(kc), in1=rs[:],
                                                                               op0=ALU.mult, op1=ALU.mult),
                     reads=[b_x1[s2], b_rs, b_coef], writes=[btb])
                P.op("act", lambda e, kc=kc, tb=tb, hh=hh: e.activation(out=hh[:, kc, :], in_=tb[:], func=AF.Identity, bias=B_f(kc), scale=1.0),
                     reads=[btb, b_modv], writes=[bhh])

        def ug(tt):
            s2 = tt % 2
            hh, bhh = h2b[s2], b_h2b[s2]
            for hc in range(NH):
                c = cnt[0]
                cnt[0] += 1
                pu, pg = c % 2, 2 + c % 2
                si = c % 2
                for kc in range(8):
                    P.op("pe", lambda e, pu=pu, kc=kc, hc=hc, hh=hh: e.matmul(ps[pu][:, 0:T2], lhsT=wfi[:, kc, hc * 128:(hc + 1) * 128], rhs=hh[:, kc, :],
                                                                           start=(kc == 0), stop=(kc == 7)),
                         reads=[b_wfi, bhh], writes=[psb[pu]])
                for kc in range(8):
                    P.op("pe", lambda e, pg=pg, kc=kc, hc=hc, hh=hh: e.matmul(ps[pg][:, 0:T2], lhsT=wfi[:, kc, DFF + hc * 128:DFF + (hc + 1) * 128], rhs=hh[:, kc, :],
                                                                           start=(kc == 0), stop=(kc == 7)),
                         reads=[b_wfi, bhh], writes=[psb[pg]])
                P.op("act", lambda e, pu=pu, si=si: e.activation(out=su[si][:], in_=ps[pu][:, 0:T2], func=AF.Silu), reads=[psb[pu]], writes=[b_su[si]])
                P.op("dve", lambda e, pg=pg, si=si, hc=hc: e.tensor_tensor(out=actT[:, hc, :], in0=ps[pg][:, 0:T2], in1=su[si][:], op=ALU.mult),
                     reads=[psb[pg], b_su[si]], writes=[b_act])

        def ff(tt):
            for jc in range(8):
                pf = 4 + jc % 2
                for hc in range(NH):
                    P.op("pe", lambda e, pf=pf, hc=hc, jc=jc: e.matmul(ps[pf][:, 0:T2], lhsT=wfo[:, hc, jc * 128:(jc + 1) * 128], rhs=actT[:, hc, :],
                                                                    start=(hc == 0), stop=(hc == NH - 1)),
                         reads=[b_wfo, b_act], writes=[psb[pf]])
                P.op("act", lambda e, pf=pf, jc=jc: e.activation(out=f_t[:, jc, :], in_=ps[pf][:, 0:T2], func=AF.Copy), reads=[psb[pf]], writes=[b_f])

        def epi(tt):
            s2 = tt % 2
            xt = x1_t[s2]
            P.op("act", lambda e: e.activation(out=sqE[:], in_=f_t[:], func=AF.Square), reads=[b_f], writes=[b_sqE])
            for kc in range(8):
                P.op("pe", lambda e, kc=kc: e.matmul(ps[7][:, 0:T2], lhsT=ones_bf[:], rhs=sqE[:, kc, :], start=(kc == 0), stop=(kc == 7)),
                     reads=[b_sqE, b_const], writes=[psb[7]])
            P.op("act", lambda e: e.activation(out=rsE[:], in_=ps[7][:, 0:T2], func=AF.Ln, bias=eps_t[:], scale=1.0 / D),
                 reads=[psb[7], b_const], writes=[b_rsE])
            P.op("act", lambda e: e.activation(out=rsE[:], in_=rsE[:], func=AF.Exp, scale=-0.5), reads=[b_rsE], writes=[b_rsE])
            for jc in range(8):
                tb, btb = tmpb[jc % 2], b_tmp[jc % 2]
                P.op("dve", lambda e, jc=jc, tb=tb: e.scalar_tensor_tensor(out=tb[:], in0=f_t[:, jc, :], scalar=GF(jc), in1=rsE[:],
                                                                        op0=ALU.mult, op1=ALU.mult),
                     reads=[b_f, b_rsE, b_coef], writes=[btb])
                P.op("dve", lambda e, jc=jc, tb=tb, xt=xt: e.tensor_tensor(out=xt[:, jc, :], in0=tb[:], in1=xt[:, jc, :], op=ALU.add),
                     reads=[btb], writes=[b_x1[s2]])
            P.dma("sp", outv[:, :, tt * T2:(tt + 1) * T2], xt[:], b_x1[s2], reads=[b_x1[s2]])

        load_x1(0)
        if NT2 > 1:
            load_x1(1)
        pro(0)
        for tt in range(NT2):
            ug(tt)
            if tt + 1 < NT2:
                pro(tt + 1)
            ff(tt)
            epi(tt)
            if tt + 2 < NT2:
                load_x1(tt + 2)
        P.barrier()


def phaseB_rwkv(nc, P, sb, ps, psb, PT, YA, par, pcol, b_par, w2, a2, g2, w0row, tmasks, dbg=False, after_consts=None):
    CDEC = math.exp(-0.5)
    NP = 6
    with ExitStack() as es:
        identB = sb(es, "identB", [128, 128], BF16)
        TRI = sb(es, "TRI", [128, 256], F32)
        bones = sb(es, "bones", [128, 128], BF16)
        bonesF = sb(es, "bonesF", [128, 128], F32)
        mask4 = sb(es, "mask4", [128, 512], BF16)
        maskLT = sb(es, "maskLT", [128, 128], F32)
        gneps = sb(es, "gneps", [128, 1], F32)
        w0bc = sb(es, "w0bc", [128, 768], F32)
        lw = [sb(es, f"lw{i}", [128, 768], BF16) for i in range(3)]
        Sbd = sb(es, "Sbd", [128, NP, 128], BF16)
        mlev = sb(es, "mlev", [128, 7, 512], BF16)
        ident4 = sb(es, "ident4", [128, 4, 128], BF16)
        b_c = P.buf("constB")
        b_S = [P.buf("S") for _ in range(NP)]
        cw = lambda fn, rd=(): P.op("pool", fn, reads=[b_c] + list(rd), writes=[b_c])
        cw(lambda e: e.memset(identB[:], 1.0))
        cw(lambda e: e.affine_select(out=identB[:], in_=identB[:], pattern=[[-1, 128]], compare_op=ALU.is_equal, fill=0.0, base=0, channel_multiplier=1))
        cw(lambda e: e.memset(TRI[:], 1.0))
        cw(lambda e: e.affine_select(out=TRI[:, 0:128], in_=TRI[:, 0:128], pattern=[[1, 128]], compare_op=ALU.is_ge, fill=0.0, base=0, channel_multiplier=-1))
        cw(lambda e: e.affine_select(out=TRI[:, 128:256], in_=TRI[:, 128:256], pattern=[[1, 128]], compare_op=ALU.is_gt, fill=0.0, base=0, channel_multiplier=-1))
        cw(lambda e: e.memset(mask4[:], 1.0))
        for q4 in range(4):
            op_ = ALU.is_gt if q4 % 2 == 0 else ALU.is_ge
            cw(lambda e, q4=q4, op_=op_: e.affine_select(out=mask4[:, q4 * 128:(q4 + 1) * 128], in_=mask4[:, q4 * 128:(q4 + 1) * 128],
                                                         pattern=[[1, 128]], compare_op=op_, fill=0.0, base=0, channel_multiplier=-1))
        cw(lambda e: e.memset(maskLT[:], 1.0))
        cw(lambda e: e.affine_select(out=maskLT[:], in_=maskLT[:], pattern=[[-1, 128]], compare_op=ALU.is_gt, fill=0.0, base=0, channel_multiplier=1))
        cw(lambda e: e.memset(bones[:], 0.0))
        cw(lambda e: e.memset(bonesF[:], 0.0))
        for h in (0, 1):
            hs = slice(64 * h, 64 * h + 64)
            cw(lambda e, hs=hs: e.memset(bones[hs, hs], 1.0))
            cw(lambda e, hs=hs: e.memset(bonesF[hs, hs], 1.0 / 64))
        cw(lambda e: e.memset(gneps[:], 64e-5))
        cw(lambda e: e.memset(Sbd[:], 0.0))
        for j4 in range(4):
            cw(lambda e, j4=j4: e.tensor_copy(out=ident4[:, j4, :], in_=identB[:]))
        with ExitStack() as es2:
            st32 = sb(es2, "st32B", [128, 768], F32)
            b_st = P.buf("st32B")
            for i, (wd, nr) in enumerate(((w2, 64), (a2, 64), (g2, 128))):
                P.dma("sp", st32[0:nr, :], wd[:, :], b_st, writes=[b_st])
                P.op("dve", lambda e, i=i, nr=nr: e.tensor_copy(out=lw[i][0:nr, :], in_=st32[0:nr, :]), reads=[b_st], writes=[b_c])
            P.dma("sp", w0bc[:], w0row.partition_broadcast(128), b_st, reads=[b_st], writes=[b_c])
            st_b = sb(es2, "st32Bb", [128, 768], F32)
            hi_b = sb(es2, "hi96", [128, 768], BF16)
            b_w0 = P.buf("w0hl")
            P.op("dve", lambda e: e.memset(lw[0][64:128, :], 0.0), reads=[b_c], writes=[b_c])
            P.dma("sp", st_b[64:65, :], w0row[:, :], b_w0, writes=[b_w0])
            P.dma("sp", st_b[96:97, :], w0row[:, :], b_w0, writes=[b_w0])
            P.op("dve", lambda e: e.tensor_copy(out=lw[0][64:65, :], in_=st_b[64:65, :]), reads=[b_w0, b_c], writes=[b_c])
            P.op("dve", lambda e: e.tensor_copy(out=hi_b[96:97, :], in_=st_b[96:97, :]), reads=[b_w0], writes=[b_w0])
            P.op("dve", lambda e: e.tensor_copy(out=st32[96:97, :], in_=hi_b[96:97, :]), reads=[b_w0, b_st], writes=[b_st])
            P.op("dve", lambda e: e.tensor_tensor(out=lw[0][96:97, :], in0=st_b[96:97, :], in1=st32[96:97, :], op=ALU.subtract),
                 reads=[b_w0, b_st, b_c], writes=[b_c])
            with ExitStack() as es3:
                mst = sb(es3, "mst", [128, 7 * 512], F32)
                b_mst = P.buf("mst")
                P.dma("sp", mst[:], tmasks[:, :], b_mst, writes=[b_mst])
                P.op("dve", lambda e: e.tensor_copy(out=mlev[:].rearrange("p a b -> p (a b)"), in_=mst[:]), reads=[b_mst], writes=[b_c])
                P.barrier()
            P.barrier()
        b_c.const = True
        w2bf, a2bf, g2bf = lw
        if after_consts is not None:
            after_consts()

        GT = 256
        NG = S // GT
        rkv_g = [sb(es, f"rkv_g{i}", [128, 18, GT], BF16) for i in range(2)]
        twl_g = [sb(es, f"twl_g{i}", [128, GT], BF16) for i in range(2)]
        b_twl1 = P.buf("twl_ones")
        for i in range(2):
            P.op("pool", lambda e, i=i: e.memset(twl_g[i][64:128, :], 1.0), writes=[b_twl1])
        P.barrier()
        al_g = [sb(es, f"al_g{i}", [64, GT], BF16) for i in range(2)]
        sgl_g = [sb(es, f"sgl_g{i}", [128, GT], BF16) for i in range(2)]
        ya_g = [sb(es, f"ya_g{i}", [128, NP, GT], BF16) for i in range(2)]
        b_g = [[P.buf("grp") for _ in range(4)] for _ in range(2)]
        b_ya = [P.buf("ya_g") for _ in range(2)]
        NF, NH = 16, 70
        Ft = [sb(es, f"Ft{hp}", [128, NF, 128], F32) for hp in range(NP)]
        Ht = [sb(es, f"Ht{hp}", [128, NH, 128], BF16) for hp in range(NP)]
        bF = [[P.buf("F") for _ in range(NF)] for _ in range(NP)]
        bH = [[P.buf("H") for _ in range(NH)] for _ in range(NP)]
        PTv = PT[0:2304, :].rearrange("(c p) t -> p c t", p=128)
        YAv = YA.rearrange("(c p) t -> p c t", p=128)
        ps0b = ps[0][:].bitcast(BF16)

        def load_group(gi):
            s2 = gi % 2
            gsl = slice(gi * GT, (gi + 1) * GT)
            P.dma("sp", rkv_g[s2][:], PTv[:, :, gsl], b_g[s2][0], writes=[b_g[s2][0]])
            P.dma("sp", twl_g[s2][0:64, :], PT[2304:2368, gsl], b_g[s2][1], writes=[b_g[s2][1]])
            P.dma("sp", al_g[s2][:], PT[2368:2432, gsl], b_g[s2][2], writes=[b_g[s2][2]])
            P.dma("sp", sgl_g[s2][:], PT[2432:2560, gsl], b_g[s2][3], writes=[b_g[s2][3]])

        load_group(0)
        import os
        rr = [0]

        def tile_body(n):
            TPG = GT // 128
            gi, s2 = n // TPG, (n // TPG) % 2
            def pre():
                if n % TPG == 0 and gi + 1 < NG:
                    load_group(gi + 1)
            ts = slice((n % TPG) * 128, (n % TPG + 1) * 128)
            bg = b_g[s2]
            steps = []
            segs = []
            curh = [{}]

            def B(hp, k):
                cur = curh[0]
                if k not in cur:
                    cur[k] = rr[0] % 8
                    rr[0] += 1
                return cur[k]
            psbf = [ps[i][:].bitcast(BF16) for i in range(8)]

            segkind = {}

            def seg(kind="ps"):
                segs.append(len(steps))
                segkind[len(steps)] = kind
            F = lambda hp, i: Ft[hp][:, i, :]
            H = lambda hp, i: Ht[hp][:, i, :]
            H4 = lambda hp, i: Ht[hp][:, i:i + 4, :].rearrange("p a b -> p (a b)")
            rT = lambda hp: rkv_g[s2][:, hp, ts]
            kT = lambda hp: rkv_g[s2][:, 6 + hp, ts]
            vT = lambda hp: rkv_g[s2][:, 12 + hp, ts]
            cs = lambda hp: slice(hp * 128, (hp + 1) * 128)

            def add(eng, fn, rd, wr):
                steps.append((eng, fn, rd, wr))

            seg()
            add("pe", lambda hp: (lambda e: e.matmul(ps[B(hp, 0)][:, 0:128], lhsT=twl_g[s2][0:97, ts], rhs=w2bf[0:97, cs(hp)], start=True, stop=True)),
                lambda hp: [bg[1], b_c], lambda hp: [psb[B(hp, 0)]])
            add("pe", lambda hp: (lambda e: e.matmul(ps[B(hp, 0)][:, 128:256], lhsT=a2bf[0:64, cs(hp)], rhs=al_g[s2][:, ts], start=True, stop=True)),
                lambda hp: [bg[2], b_c], lambda hp: [psb[B(hp, 0)]])
            add("act", lambda hp: (lambda e: e.activation(out=F(hp, 1), in_=ps[B(hp, 0)][:, 0:128], func=AF.Sigmoid)),
                lambda hp: [psb[B(hp, 0)]], lambda hp: [bF[hp][1]])
            add("act", lambda hp: (lambda e: e.activation(out=F(hp, 2), in_=ps[B(hp, 0)][:, 128:256], func=AF.Sigmoid, bias=pcol("a0", hp), scale=1.0)),
                lambda hp: [psb[B(hp, 0)], b_par], lambda hp: [bF[hp][2]])
            seg()
            add("pe", lambda hp: (lambda e: e.matmul(ps[B(hp, 1)][:, 0:256], lhsT=F(hp, 1), rhs=TRI[:], start=True, stop=True)),
                lambda hp: [bF[hp][1], b_c], lambda hp: [psb[B(hp, 1)]])
            add("act", lambda hp: (lambda e: e.activation(out=Ft[hp][:, 4:6, :], in_=ps[B(hp, 1)][:, 0:256].rearrange("p (j c) -> p j c", c=128), func=AF.Exp, scale=-CDEC)),
                lambda hp: [psb[B(hp, 1)]], lambda hp: [bF[hp][4], bF[hp][5]])
            add("act", lambda hp: (lambda e: e.activation(out=F(hp, 6), in_=ps[B(hp, 1)][:, 0:128], func=AF.Exp, scale=CDEC)),
                lambda hp: [psb[B(hp, 1)]], lambda hp: [bF[hp][6]])
            add("dve", lambda hp: (lambda e: e.tensor_scalar(out=F(hp, 7), in0=F(hp, 6), scalar1=Ft[hp][:, 4, 127:128], scalar2=None, op0=ALU.mult)),
                lambda hp: [bF[hp][6], bF[hp][4]], lambda hp: [bF[hp][7]])
            seg()
            add("act", lambda hp: (lambda e: e.activation(out=H(hp, 0), in_=kT(hp), func=AF.Square, scale=pcol("k_k", hp))),
                lambda hp: [bg[0], b_par], lambda hp: [bH[hp][0]])
            add("pe", lambda hp: (lambda e: e.matmul(ps[B(hp, 1)][:, 256:384], lhsT=bones[:], rhs=H(hp, 0), start=True, stop=True)),
                lambda hp: [bH[hp][0], b_c], lambda hp: [psb[B(hp, 1)]])
            add("act", lambda hp: (lambda e: e.activation(out=F(hp, 8), in_=ps[B(hp, 1)][:, 256:384], func=AF.Ln)),
                lambda hp: [psb[B(hp, 1)], bF[hp][7]], lambda hp: [bF[hp][8]])
            seg("ew")
            add("act", lambda hp: (lambda e: e.activation(out=F(hp, 8), in_=F(hp, 8), func=AF.Exp, scale=-0.5)),
                lambda hp: [], lambda hp: [bF[hp][8]])
            add("dve", lambda hp: (lambda e: e.scalar_tensor_tensor(out=F(hp, 9), in0=kT(hp), scalar=pcol("k_k", hp), in1=F(hp, 8), op0=ALU.mult, op1=ALU.mult)),
                lambda hp: [bg[0], b_par, bF[hp][8]], lambda hp: [bF[hp][9]])
            add("dve", lambda hp: (lambda e: e.tensor_scalar(out=F(hp, 10), in0=F(hp, 2), scalar1=-1.0, scalar2=pcol("k_a", hp), op0=ALU.add, op1=ALU.mult)),
                lambda hp: [bF[hp][2], b_par], lambda hp: [bF[hp][10]])
            add("dve", lambda hp: (lambda e: e.scalar_tensor_tensor(out=F(hp, 10), in0=F(hp, 10), scalar=1.0, in1=kT(hp), op0=ALU.add, op1=ALU.mult)),
                lambda hp: [bg[0]], lambda hp: [bF[hp][10]])
            add("dve", lambda hp: (lambda e: e.scalar_tensor_tensor(out=F(hp, 11), in0=F(hp, 9), scalar=-1.0, in1=F(hp, 2), op0=ALU.mult, op1=ALU.mult)),
                lambda hp: [bF[hp][9], bF[hp][2]], lambda hp: [bF[hp][11]])
            add("dve", lambda hp: (lambda e: e.tensor_tensor(out=H(hp, 2), in0=F(hp, 9), in1=F(hp, 5), op=ALU.mult)),
                lambda hp: [bF[hp][9], bF[hp][5]], lambda hp: [bH[hp][2]])
            add("dve", lambda hp: (lambda e: e.tensor_tensor(out=H(hp, 3), in0=rT(hp), in1=F(hp, 4), op=ALU.mult)),
                lambda hp: [bg[0], bF[hp][4]], lambda hp: [bH[hp][3]])
            add("dve", lambda hp: (lambda e: e.tensor_tensor(out=Ht[hp][:, 4:6, :], in0=Ft[hp][:, 10:12, :],
                                                           in1=F(hp, 6).unsqueeze(1).to_broadcast([128, 2, 128]), op=ALU.mult)),
                lambda hp: [bF[hp][10], bF[hp][11], bF[hp][6]], lambda hp: [bH[hp][4], bH[hp][5]])
            add("dve", lambda hp: (lambda e: e.tensor_tensor(out=Ht[hp][:, 6:8, :], in0=Ft[hp][:, 10:12, :],
                                                           in1=F(hp, 7).unsqueeze(1).to_broadcast([128, 2, 128]), op=ALU.mult)),
                lambda hp: [bF[hp][10], bF[hp][11], bF[hp][7]], lambda hp: [bH[hp][6], bH[hp][7]])
            seg("ew")
            add("dve", lambda hp: (lambda e: e.scalar_tensor_tensor(out=H(hp, 1), in0=rT(hp), scalar=pcol("r_k", hp), in1=F(hp, 10), op0=ALU.mult, op1=ALU.mult)),
                lambda hp: [bg[0], b_par, bF[hp][10]], lambda hp: [bH[hp][1]])
            seg()
            add("pe", lambda hp: (lambda e: e.matmul(ps[B(hp, 1)][:, 384:512], lhsT=bones[:], rhs=H(hp, 1), start=True, stop=True)),
                lambda hp: [bH[hp][1], b_c], lambda hp: [psb[B(hp, 1)]])
            add("dve", lambda hp: (lambda e: e.tensor_tensor(out=F(hp, 15), in0=ps[B(hp, 1)][:, 384:512], in1=vT(hp), op=ALU.mult)),
                lambda hp: [psb[B(hp, 1)], bg[0]], lambda hp: [bF[hp][15]])
            seg()
            for j, src in enumerate((lambda hp: H(hp, 2), vT, lambda hp: H(hp, 6), lambda hp: H(hp, 7))):
                rdj = [lambda hp: [bH[hp][2]], lambda hp: [bg[0]], lambda hp: [bH[hp][6]], lambda hp: [bH[hp][7]]][j]
                add("pe", lambda hp, j=j, src=src: (lambda e: e.transpose(psbf[B(hp, 0)][:, j * 128:(j + 1) * 128], src(hp), identB[:])),
                    lambda hp, rdj=rdj: rdj(hp) + [b_c], lambda hp: [psb[B(hp, 0)]])
            add("act", lambda hp: (lambda e: e.activation(out=Ht[hp][:, 8:12, :], in_=psbf[B(hp, 0)][:, 0:512].rearrange("p (j c) -> p j c", c=128), func=AF.Copy)),
                lambda hp: [psb[B(hp, 0)]], lambda hp: [bH[hp][8], bH[hp][9], bH[hp][10], bH[hp][11]])
            seg()
            for h in (0, 1):
                hs = slice(64 * h, 64 * h + 64)
                xb = 2 + h
                mb = 4 + h
                add("pe", lambda hp, hs=hs, h=h: (lambda e: e.matmul(ps[B(hp, 2 + h)][:, 0:256], lhsT=Ht[hp][hs, 5, :],
                                                                    rhs=Ht[hp][hs, 2:4, :], start=True, stop=True)),
                    lambda hp: [bH[hp][5], bH[hp][2], bH[hp][3]], lambda hp, h=h: [psb[B(hp, 2 + h)]])
                add("pe", lambda hp, hs=hs, h=h: (lambda e: e.matmul(ps[B(hp, 2 + h)][:, 256:512], lhsT=Ht[hp][hs, 4, :],
                                                                    rhs=Ht[hp][hs, 2:4, :], start=True, stop=True)),
                    lambda hp: [bH[hp][4], bH[hp][2], bH[hp][3]], lambda hp, h=h: [psb[B(hp, 2 + h)]])
                add("pe", lambda hp, hs=hs, h=h: (lambda e: e.matmul(ps[B(hp, h)][:, 0:128], lhsT=Ht[hp][hs, 2, :],
                                                                    rhs=Ht[hp][hs, 5, :], start=True, stop=True)),
                    lambda hp: [bH[hp][5], bH[hp][2]], lambda hp, h=h: [psb[B(hp, h)]])
                xs0 = 12 + 4 * h
                add("dve", lambda hp, h=h, xs0=xs0: (lambda e: e.tensor_tensor(out=H4(hp, xs0), in0=ps[B(hp, 2 + h)][:], in1=mask4[:], op=ALU.mult)),
                    lambda hp, h=h: [psb[B(hp, 2 + h)], b_c], lambda hp, xs0=xs0: [bH[hp][xs0 + i] for i in range(4)])
                add("dve", lambda hp, h=h: (lambda e: e.tensor_tensor(out=H(hp, 28 + h), in0=ps[B(hp, h)][:, 0:128], in1=maskLT[:], op=ALU.mult)),
                    lambda hp, h=h: [psb[B(hp, h)], b_c], lambda hp, h=h: [bH[hp][28 + h]])
            seg("ew")
            TallV = lambda hp: Ht[hp][:, 20:24, :]
            bTall = lambda hp: [bH[hp][20 + i] for i in range(4)]
            XL = lambda hp, h, lvl: H(hp, 42 + 14 * h + lvl)
            bXL = lambda hp, h: [bH[hp][42 + 14 * h + i] for i in range(7)]
            add("dve", lambda hp: (lambda e: e.tensor_tensor(out=Ht[hp][:, 35:63:14, :], in0=Ht[hp][:, 12:20:4, :],
                                                           in1=mlev[:, 0, 0:128].unsqueeze(1).to_broadcast([128, 2, 128]), op=ALU.mult)),
                lambda hp: [bH[hp][12], bH[hp][16], b_c], lambda hp: [bH[hp][35], bH[hp][49]])
            add("dve", lambda hp: (lambda e: e.tensor_tensor(
                out=Ht[hp][:, 42:70, :].rearrange("p (h l) c -> p h l c", l=14)[:, :, 0:7, :],
                in0=Ht[hp][:, 28:30, :].unsqueeze(2).to_broadcast([128, 2, 7, 128]),
                in1=mlev[:, :, 128:256].unsqueeze(1).to_broadcast([128, 2, 7, 128]), op=ALU.mult)),
                lambda hp: [bH[hp][28], bH[hp][29], b_c], lambda hp: bXL(hp, 0) + bXL(hp, 1))
            add("dve", lambda hp: (lambda e: e.tensor_tensor(out=TallV(hp), in0=Ht[hp][:, 35:63:7, :], in1=ident4[:], op=ALU.add)),
                lambda hp: [b_c, bH[hp][35], bH[hp][49]] + bXL(hp, 0) + bXL(hp, 1), lambda hp: bTall(hp))
            for lvl in range(1, 7):
                seg("inv")
                for h in (0, 1):
                    add("pe", lambda hp, h=h, lvl=lvl: (lambda e: e.matmul(ps[B(hp, 2)][:, h * 128:(h + 1) * 128], lhsT=XL(hp, h, lvl), rhs=H(hp, 20 + 2 * h), start=True, stop=True)),
                        lambda hp, h=h: bXL(hp, h) + [bH[hp][20 + 2 * h]], lambda hp: [psb[B(hp, 2)]])
                add("act", lambda hp: (lambda e: e.activation(out=Ht[hp][:, 24:26, :], in_=ps[B(hp, 2)][:, 0:256].rearrange("p (j c) -> p j c", c=128), func=AF.Copy)),
                    lambda hp: [psb[B(hp, 2)]], lambda hp: [bH[hp][24], bH[hp][25]])
                seg("inv")
                add("pe", lambda hp: (lambda e: e.matmul(ps[B(hp, 3)][:], lhsT=identB[:], rhs=Ht[hp][:, 20:24, :], start=True, stop=False)),
                    lambda hp: bTall(hp) + [b_c], lambda hp: [psb[B(hp, 3)]])
                for h in (0, 1):
                    add("pe", lambda hp, h=h: (lambda e: e.matmul(ps[B(hp, 3)][:, (2 * h) * 128:(2 * h + 1) * 128], lhsT=H(hp, 21 + 2 * h), rhs=H(hp, 24 + h), start=False, stop=False)),
                        lambda hp, h=h: [bH[hp][21 + 2 * h], bH[hp][24 + h]], lambda hp: [psb[B(hp, 3)]])
                    add("pe", lambda hp, h=h: (lambda e: e.matmul(ps[B(hp, 3)][:, (2 * h + 1) * 128:(2 * h + 2) * 128], lhsT=H(hp, 24 + h), rhs=H(hp, 21 + 2 * h), start=False, stop=(h == 1))),
                        lambda hp, h=h: [bH[hp][21 + 2 * h], bH[hp][24 + h]], lambda hp: [psb[B(hp, 3)]])
                if lvl % 2 == 1:
                    add("dve", lambda hp: (lambda e: e.tensor_copy(out=TallV(hp), in_=ps[B(hp, 3)][:].rearrange("p (j c) -> p j c", c=128))),
                        lambda hp: [psb[B(hp, 3)]], lambda hp: bTall(hp))
                else:
                    add("act", lambda hp: (lambda e: e.activation(out=TallV(hp), in_=ps[B(hp, 3)][:].rearrange("p (j c) -> p j c", c=128), func=AF.Copy)),
                        lambda hp: [psb[B(hp, 3)]], lambda hp: bTall(hp))
            seg()
            TTs = lambda hp, h: H(hp, 20 + 2 * h)
            bTT = lambda hp, h: bH[hp][20 + 2 * h]
            for h in (0, 1):
                hs = slice(64 * h, 64 * h + 64)
                add("pe", lambda hp, h=h, hs=hs: (lambda e: e.matmul(ps[B(hp, 0)][:, 64 * h:64 * h + 64], lhsT=H(hp, 14 + 4 * h), rhs=Ht[hp][:, 9, hs], start=True, stop=True)),
                    lambda hp, h=h: [bH[hp][14 + 4 * h], bH[hp][9]], lambda hp: [psb[B(hp, 0)]])
            add("act", lambda hp: (lambda e: e.activation(out=H(hp, 32), in_=ps[B(hp, 0)][:, 0:128], func=AF.Copy)),
                lambda hp: [psb[B(hp, 0)]], lambda hp: [bH[hp][32]])
            seg()
            for h in (0, 1):
                hs = slice(64 * h, 64 * h + 64)
                add("pe", lambda hp, h=h, hs=hs: (lambda e: e.matmul(ps[B(hp, 0)][hs, 256:384], lhsT=Ht[hp][:, 8, hs], rhs=TTs(hp, h), start=True, stop=True)),
                    lambda hp, h=h: [bTT(hp, h), bH[hp][8]], lambda hp: [psb[B(hp, 0)]])
            add("act", lambda hp: (lambda e: e.activation(out=H(hp, 33), in_=ps[B(hp, 0)][:, 256:384], func=AF.Copy)),
                lambda hp: [psb[B(hp, 0)]], lambda hp: [bH[hp][33]])
            seg("pm")
            add("pe", lambda hp: (lambda e: e.matmul(ps[B(hp, 1)][:, 0:128], lhsT=H(hp, 33), rhs=Sbd[:, hp, :], start=True, stop=False)),
                lambda hp: [bH[hp][33], b_S[hp]], lambda hp: [psb[B(hp, 1)]])
            for h in (0, 1):
                hs = slice(64 * h, 64 * h + 64)
                add("pe", lambda hp, h=h, hs=hs: (lambda e: e.matmul(ps[B(hp, 1)][:, 64 * h:64 * h + 64], lhsT=TTs(hp, h), rhs=Ht[hp][:, 32, hs], start=False, stop=(h == 1))),
                    lambda hp, h=h: [bTT(hp, h), bH[hp][32]], lambda hp: [psb[B(hp, 1)]])
            add("dve", lambda hp: (lambda e: e.tensor_copy(out=H(hp, 34), in_=ps[B(hp, 1)][:, 0:128])),
                lambda hp: [psb[B(hp, 1)]], lambda hp: [bH[hp][34]])
            add("pe", lambda hp: (lambda e: e.matmul(ps[B(hp, 1)][:, 128:256], lhsT=Sbd[:, hp, :], rhs=H(hp, 3), start=True, stop=False)),
                lambda hp: [b_S[hp], bH[hp][3]], lambda hp: [psb[B(hp, 1)]])
            for h in (0, 1):
                hs = slice(64 * h, 64 * h + 64)
                add("pe", lambda hp, h=h, hs=hs: (lambda e: e.matmul(ps[B(hp, 1)][hs, 128:256], lhsT=Ht[hp][:, 34, hs], rhs=H(hp, 13 + 4 * h), start=False, stop=False)),
                    lambda hp, h=h: [bH[hp][34], bH[hp][13 + 4 * h]], lambda hp: [psb[B(hp, 1)]])
                add("pe", lambda hp, h=h, hs=hs: (lambda e: e.matmul(ps[B(hp, 1)][hs, 128:256], lhsT=Ht[hp][:, 9, hs], rhs=H(hp, 15 + 4 * h), start=False, stop=True)),
                    lambda hp, h=h: [bH[hp][9], bH[hp][15 + 4 * h]], lambda hp: [psb[B(hp, 1)]])
            add("pe", lambda hp: (lambda e: e.matmul(ps[B(hp, 1)][:, 256:384], lhsT=H(hp, 10), rhs=H(hp, 9), start=True, stop=False)),
                lambda hp: [bH[hp][10], bH[hp][9]], lambda hp: [psb[B(hp, 1)]])
            add("pe", lambda hp: (lambda e: e.matmul(ps[B(hp, 1)][:, 256:384], lhsT=H(hp, 11), rhs=H(hp, 34), start=False, stop=True)),
                lambda hp: [bH[hp][11], bH[hp][34]], lambda hp: [psb[B(hp, 1)]])
            add("act", lambda hp: (lambda e: e.activation(out=F(hp, 13), in_=ps[B(hp, 1)][:, 128:256], func=AF.Copy)),
                lambda hp: [psb[B(hp, 1)]], lambda hp: [bF[hp][13]])
            add("act", lambda hp: (lambda e: e.activation(out=F(hp, 14), in_=ps[B(hp, 1)][:, 128:256], func=AF.Square)),
                lambda hp: [psb[B(hp, 1)]], lambda hp: [bF[hp][14]])
            for h in (0, 1):
                hs = slice(64 * h, 64 * h + 64)
                add("dve", lambda hp, h=h, hs=hs: (lambda e: e.scalar_tensor_tensor(out=Sbd[hs, hp, hs], in0=Sbd[hs, hp, hs], scalar=Ft[hp][hs, 4, 127:128],
                                                                                 in1=ps[B(hp, 1)][hs, 256 + 64 * h:256 + 64 * h + 64], op0=ALU.mult, op1=ALU.add)),
                    lambda hp: [psb[B(hp, 1)], bF[hp][4]], lambda hp: [b_S[hp]])
            seg()
            add("pe", lambda hp: (lambda e: e.matmul(ps[B(hp, 1)][:, 0:256], lhsT=bonesF[:], rhs=Ft[hp][:, 13:15, :], start=True, stop=True)),
                lambda hp: [bF[hp][13], bF[hp][14], b_c], lambda hp: [psb[B(hp, 1)]])
            add("act", lambda hp: (lambda e: e.activation(out=F(hp, 1), in_=ps[B(hp, 1)][:, 0:128], func=AF.Square)),
                lambda hp: [psb[B(hp, 1)]], lambda hp: [bF[hp][1]])
            add("dve", lambda hp: (lambda e: e.tensor_tensor(out=F(hp, 5), in0=F(hp, 13), in1=ps[B(hp, 1)][:, 0:128], op=ALU.subtract)),
                lambda hp: [psb[B(hp, 1)], bF[hp][13]], lambda hp: [bF[hp][5]])
            add("dve", lambda hp: (lambda e: e.tensor_tensor(out=F(hp, 2), in0=ps[B(hp, 1)][:, 128:256], in1=F(hp, 1), op=ALU.subtract)),
                lambda hp: [psb[B(hp, 1)], bF[hp][1]], lambda hp: [bF[hp][2]])
            seg("ew")
            add("act", lambda hp: (lambda e: e.activation(out=F(hp, 2), in_=F(hp, 2), func=AF.Ln, bias=gneps[:], scale=1.0)),
                lambda hp: [b_c], lambda hp: [bF[hp][2]])
            add("act", lambda hp: (lambda e: e.activation(out=F(hp, 8), in_=F(hp, 2), func=AF.Exp, scale=-0.5)),
                lambda hp: [bF[hp][2]], lambda hp: [bF[hp][8]])
            add("dve", lambda hp: (lambda e: e.tensor_tensor(out=F(hp, 6), in0=F(hp, 5), in1=F(hp, 8), op=ALU.mult)),
                lambda hp: [bF[hp][5], bF[hp][8]], lambda hp: [bF[hp][6]])
            add("dve", lambda hp: (lambda e: e.scalar_tensor_tensor(out=F(hp, 9), in0=F(hp, 6), scalar=pcol("lnx_w", hp), in1=F(hp, 15), op0=ALU.mult, op1=ALU.add)),
                lambda hp: [bF[hp][6], bF[hp][15], b_par], lambda hp: [bF[hp][9]])
            seg()
            add("pe", lambda hp: (lambda e: e.matmul(ps[B(hp, 0)][:, 0:128], lhsT=g2bf[:, cs(hp)], rhs=sgl_g[s2][:, ts], start=True, stop=True)),
                lambda hp: [bg[3], b_c], lambda hp: [psb[B(hp, 0)]])
            add("dve", lambda hp: (lambda e: e.scalar_tensor_tensor(out=ya_g[s2][:, hp, ts], in0=F(hp, 9), scalar=pcol("lnx_b", hp), in1=ps[B(hp, 0)][:, 0:128], op0=ALU.add, op1=ALU.mult)),
                lambda hp: [psb[B(hp, 0)], bF[hp][9], b_par], lambda hp: [b_ya[s2]])

            bounds = sorted(set(segs + [0, len(steps)]))
            nseg = len(bounds) - 1

            def emit_one(eng, fn, rd, wr, hp):
                rec = _Rec()
                fn(hp)(rec)
                P.op(eng, lambda e, c=rec: getattr(e, c.name)(*c.args, **c.kwargs), reads=rd(hp), writes=wr(hp))

            def emit_seg(si, pairs):
                kind = segkind.get(bounds[si], "ps")
                ops = steps[bounds[si]:bounds[si + 1]]
                if kind in ("ew", "pm"):
                    dicts = {hp: {} for hp in pairs}
                    npp = len(pairs)
                    for dwave in range(len(ops) + npp - 1):
                        for pi, hp in enumerate(pairs):
                            st = dwave - pi
                            if 0 <= st < len(ops):
                                (eng, fn, rd, wr) = ops[st]
                                curh[0] = dicts[hp] if kind == "pm" else {}
                                emit_one(eng, fn, rd, wr, hp)
                else:
                    for hp in pairs:
                        curh[0] = {}
                        for (eng, fn, rd, wr) in ops:
                            emit_one(eng, fn, rd, wr, hp)

            def post():
                if n % TPG == TPG - 1:
                    P.dma("sp", YAv[:, :, gi * GT:(gi + 1) * GT], ya_g[s2][:], b_ya[s2], reads=[b_ya[s2]])
                if dbg and n == NTL - 1:
                    P.barrier()
                    DBGF = nc.dram_tensor("DBGF", [128, NF, 128], F32, kind="ExternalOutput").ap()
                    DBGH = nc.dram_tensor("DBGH", [128, NH, 128], BF16, kind="ExternalOutput").ap()
                    DBGS = nc.dram_tensor("DBGS", [128, NP, 128], BF16, kind="ExternalOutput").ap()
                    bd = P.buf("dbgd")
                    P.dma("sp", DBGF[:, :, :], Ft[0][:], bd)
                    P.dma("sp", DBGH[:, :, :], Ht[0][:], bd)
                    P.dma("sp", DBGS[:, :, :], Sbd[:], bd)

            kinds = [segkind.get(bounds[k], 'ps') for k in range(nseg)]
            return dict(pre=pre, post=post, nseg=nseg, emit_seg=emit_seg, kinds=kinds)

        NTL = int(os.environ.get('KNT', S // 128))
        descs = {}

        def get_desc(n):
            if n not in descs:
                descs[n] = tile_body(n)
            return descs[n]

        nseg0 = get_desc(0)["nseg"]
        LAG = 0
        total = NTL * nseg0
        G0, G1 = (0, 1, 2, 3, 4, 5), ()
        for i in range(total + LAG):
            if i < total:
                n, si = divmod(i, nseg0)
                dsc = get_desc(n)
                kinds = dsc["kinds"]
                if kinds[si] == "inv":
                    if si == 0 or kinds[si - 1] != "inv":
                        sj = si
                        while sj < nseg0 and kinds[sj] == "inv":
                            sj += 1
                        ninv = sj - si
                        for dwave in range(ninv + len(G0) - 1):
                            for pi, hp in enumerate(G0):
                                st = dwave - pi
                                if 0 <= st < ninv:
                                    dsc["emit_seg"](si + st, (hp,))
                else:
                    dsc["emit_seg"](si, G0)
            j = i - LAG
            if j >= 0:
                n, si = divmod(j, nseg0)
                dsc = get_desc(n)
                if si == 0:
                    dsc["pre"]()
                if G1:
                    dsc["emit_seg"](si, G1)
                if si == nseg0 - 1:
                    dsc["post"]()
                    if n - 1 in descs:
                        del descs[n - 1]

        P.barrier()
```

```python
import math
from contextlib import ExitStack
import numpy as np
import ml_dtypes
import concourse.bass as bass
import concourse.mybir as mybir
from concourse.bass_utils import run_bass_kernel_spmd

F32 = mybir.dt.float32
BF16 = mybir.dt.bfloat16
AF = mybir.ActivationFunctionType
ALU = mybir.AluOpType

D = 1024
S = 4096
NC8 = 8
INW = 6912
DFF = 2816
TT = 512
NT = S // TT

PC = {}
_off = 0
for _n, _w in [("c", 8), ("b_ada", 48), ("g_pre_mix", 8), ("g_post_mix", 8), ("g_pre_ffn", 8),
               ("g_post_ffn", 8), ("mu", 20), ("w0", 6), ("a0", 6), ("k_k", 6), ("k_a", 6),
               ("r_k", 6), ("lnx_w", 6), ("lnx_b", 6), ("inv_freq", 1)]:
    PC[_n] = (_off, _w)
    _off += _w
NPAR = _off


class _Rec:
    def __init__(self):
        self.name = None
        self.args = ()
        self.kwargs = {}

    def __getattr__(self, name):
        def f(*a, **k):
            self.name, self.args, self.kwargs = name, a, k
            return self
        return f


class Buf:
    __slots__ = ("name", "w", "rs", "sem", "semval", "const", "excl")

    def __init__(self, name, const=False):
        self.excl = False
        self.name = name
        self.w = None
        self.rs = {}
        self.sem = None
        self.semval = 0
        self.const = const


class Prog:
    ENGS = ["pe", "act", "dve", "pool", "sp"]

    def __init__(self, nc):
        self.nc = nc
        self.streams = {e: [] for e in self.ENGS}
        self.cnt = {e: 0 for e in self.ENGS}
        self.sems = {e: nc.alloc_semaphore("s_" + e) for e in self.ENGS}
        self.seen = {e: {} for e in self.ENGS}
        self.dma_owners = []
        self.nbuf = 0

    def buf(self, name="b", const=False):
        self.nbuf += 1
        return Buf(f"{name}{self.nbuf}", const)

    def _deps(self, eng, reads, writes):
        evs = []
        for b in reads:
            if b.w is not None:
                evs.append(b.w)
        for b in writes:
            if b.w is not None:
                evs.append(b.w)
            evs.extend(b.rs.values())
        waits = {}
        seen = self.seen[eng]
        for (key, semh, val) in evs:
            if key == "pe" and eng == "pe":
                continue
            if seen.get(key, 0) >= val:
                continue
            if key in waits and waits[key][1] >= val:
                continue
            waits[key] = (semh, val)
        for key, (semh, val) in waits.items():
            seen[key] = val
        return list(waits.values())

    def _record(self, ev, reads, writes):
        for b in reads:
            if not b.const:
                b.rs[ev[0]] = ev
        for b in writes:
            b.w = ev
            b.rs = {}

    def op(self, eng, fn, reads=(), writes=()):
        if any(b.excl for b in reads):
            writes = list(writes) + [b for b in reads if b.excl]
            reads = [b for b in reads if not b.excl]
        waits = self._deps(eng, reads, writes)
        self.cnt[eng] += 1
        ev = (eng, self.sems[eng], self.cnt[eng])
        self.streams[eng].append((waits, fn, self.sems[eng], 1))
        self._record(ev, reads, writes)

    def dma(self, q, out, in_, owner, reads=(), writes=(), **kw):
        waits = self._deps(q, reads, writes)
        if owner.sem is None:
            owner.sem = self.nc.alloc_semaphore("d_" + owner.name)
            self.dma_owners.append(owner)
        owner.semval += 16
        ev = ("dma_" + owner.name, owner.sem, owner.semval)
        self.streams[q].append((waits, lambda e: e.dma_start(out=out, in_=in_, **kw), owner.sem, 16))
        self._record(ev, reads, writes)

    def barrier(self):
        for e in self.ENGS:
            waits = []
            for f in self.ENGS:
                if f != e and self.cnt[f] > self.seen[e].get(f, 0):
                    waits.append((self.sems[f], self.cnt[f]))
                    self.seen[e][f] = self.cnt[f]
            for o in self.dma_owners:
                key = "dma_" + o.name
                if o.semval > self.seen[e].get(key, 0):
                    waits.append((o.sem, o.semval))
                    self.seen[e][key] = o.semval
            if waits:
                self.streams[e].append((waits, None, None, 0))

    def emit(self):
        nc = self.nc
        P = self

        def run(name, e):
            for waits, fn, semh, inc in P.streams[name]:
                for (s, v) in waits:
                    e.wait_ge(s, v)
                if fn is not None:
                    ins = fn(e)
                    ins.then_inc(semh, inc)

        with nc.Block() as block:
            @block.tensor
            def _(e):
                run("pe", e)

            @block.scalar
            def _(e):
                run("act", e)

            @block.vector
            def _(e):
                run("dve", e)

            @block.gpsimd
            def _(e):
                run("pool", e)

            @block.sync
            def _(e):
                run("sp", e)


def build_nc(stage=99, dbg=False, inject=False):
    nc = bass.Bass("TRN2", target_bir_lowering=False)
    P = Prog(nc)
    dram_in = lambda name, shape: nc.dram_tensor(name, shape, F32, kind="ExternalInput").ap()
    xT = dram_in("xT", [D, S])
    params = dram_in("params", [128, NPAR])
    w_ada = dram_in("w_ada", [D, 6 * D])
    w_in = dram_in("w_in", [D, INW])
    w2 = dram_in("w2", [64, 768])
    a2 = dram_in("a2", [64, 768])
    g2 = dram_in("g2", [128, 768])
    w0row = dram_in("w0row", [1, 768])
    tmasks = dram_in("tmasks", [128, 7 * 512])
    w_a = dram_in("w_a", [768, D])
    w_b = dram_in("w_b", [256, D])
    w_out = dram_in("w_out", [D, D])
    w_ffn_in = dram_in("w_ffn_in", [D, 2 * DFF])
    w_ffn_out = dram_in("w_ffn_out", [DFF, D])
    outT = nc.dram_tensor("outT", [D, S], F32, kind="ExternalOutput").ap()
    okind = "ExternalOutput" if dbg else "Internal"
    PT = nc.dram_tensor("PT", [INW, S], BF16, kind=okind).ap()
    YA = nc.dram_tensor("YA", [768, S], BF16, kind=("ExternalInput" if inject else okind)).ap()
    OT = nc.dram_tensor("OT", [256, S], BF16, kind=okind).ap()
    X1 = nc.dram_tensor("X1", [D, S], F32, kind=okind).ap()
    WFI = nc.dram_tensor("WFI_bf", [D, 2 * DFF], BF16, kind="Internal").ap()
    WFO = nc.dram_tensor("WFO_bf", [DFF, D], BF16, kind="Internal").ap()

    es_all = ExitStack()
    sb = lambda es, name, shape, dt: es.enter_context(nc.sbuf_tensor(name, shape, dt))
    ps = [es_all.enter_context(nc.psum_tensor(f"ps{i}", [128, 512], F32)) for i in range(8)]
    psb = [P.buf(f"ps{i}") for i in range(8)]
    for b in psb:
        b.excl = True

    par = sb(es_all, "par", [128, NPAR], F32)
    modv = sb(es_all, "modv", [128, 48], F32)
    coef = sb(es_all, "coef", [128, 48], F32)
    omm = sb(es_all, "omm", [128, 20], F32)
    ones_bf = sb(es_all, "ones_bf", [128, 128], BF16)
    eps_t = sb(es_all, "eps_t", [128, 1], F32)
    b_par, b_modv, b_coef, b_const = P.buf("par"), P.buf("modv"), P.buf("coef"), P.buf("const")
    pcol = lambda n, i=None: (par[:, PC[n][0]:PC[n][0] + PC[n][1]] if i is None
                              else par[:, PC[n][0] + i:PC[n][0] + i + 1])

    P.dma("sp", par[:], params[:, :], b_par, writes=[b_par])
    P.op("pool", lambda e: e.memset(ones_bf[:], 1.0), writes=[b_const])
    P.op("pool", lambda e: e.memset(eps_t[:], 1e-6), writes=[b_const])

    with ExitStack() as es:
        sc = sb(es, "sc", [128, 8], F32)
        b_sc = P.buf("sc")
        P.op("act", lambda e: e.activation(out=sc[:], in_=pcol("c"), func=AF.Silu), reads=[b_par], writes=[b_sc])
        NB = 768
        wa_t = [sb(es, f"wa_t{i}", [128, 8, NB], F32) for i in range(2)]
        b_wa = [P.buf("wa") for _ in range(2)]
        w_ada_v = w_ada.rearrange("(kc p) n -> p kc n", p=128)
        for jb in range(8):
            t, bt = wa_t[jb % 2], b_wa[jb % 2]
            P.dma("sp", t[:], w_ada_v[:, :, jb * NB:(jb + 1) * NB], bt, writes=[bt])
            for j in range(6):
                jj = jb * 6 + j
                for kc in range(8):
                    P.op("pe", lambda e, t=t, j=j, kc=kc, jj=jj: e.matmul(
                        ps[6][:, jj:jj + 1], lhsT=t[:, kc, j * 128:(j + 1) * 128], rhs=sc[:, kc:kc + 1],
                        start=(kc == 0), stop=(kc == 7)), reads=[bt, b_sc], writes=[psb[6]])
        P.op("dve", lambda e: e.tensor_tensor(out=modv[:], in0=ps[6][:, 0:48], in1=pcol("b_ada"), op=ALU.add),
             reads=[psb[6], b_par], writes=[b_modv])
        P.op("dve", lambda e: e.scalar_tensor_tensor(out=coef[:, 0:8], in0=modv[:, 8:16], scalar=1.0, in1=pcol("g_pre_mix"),
                                                     op0=ALU.add, op1=ALU.mult), reads=[b_modv, b_par], writes=[b_coef])
        P.op("dve", lambda e: e.tensor_tensor(out=coef[:, 8:16], in0=modv[:, 16:24], in1=pcol("g_post_mix"), op=ALU.mult),
             reads=[b_modv, b_par], writes=[b_coef])
        P.op("dve", lambda e: e.scalar_tensor_tensor(out=coef[:, 16:24], in0=modv[:, 32:40], scalar=1.0, in1=pcol("g_pre_ffn"),
                                                     op0=ALU.add, op1=ALU.mult), reads=[b_modv, b_par], writes=[b_coef])
        P.op("dve", lambda e: e.tensor_tensor(out=coef[:, 24:32], in0=modv[:, 40:48], in1=pcol("g_post_ffn"), op=ALU.mult),
             reads=[b_modv, b_par], writes=[b_coef])
        P.op("dve", lambda e: e.tensor_scalar(out=omm[:], in0=pcol("mu"), scalar1=-1.0, scalar2=1.0, op0=ALU.mult, op1=ALU.add),
             reads=[b_par], writes=[b_coef])
        P.barrier()
    A_m = lambda kc: coef[:, kc:kc + 1]
    B_m = lambda kc: modv[:, kc:kc + 1]
    GM = lambda kc: coef[:, 8 + kc:9 + kc]
    A_f = lambda kc: coef[:, 16 + kc:17 + kc]
    B_f = lambda kc: modv[:, 24 + kc:25 + kc]
    GF = lambda kc: coef[:, 24 + kc:25 + kc]

    def rms_modulate(es, src_tile, b_src, dst, b_dst, dst_sl, A, Bc, tmpbufs, b_tmp, sqt, b_sq, rs, b_rs, psi):
        P.op("act", lambda e: e.activation(out=sqt[:], in_=src_tile[:], func=AF.Square), reads=[b_src], writes=[b_sq])
        for kc in range(8):
            P.op("pe", lambda e, kc=kc: e.matmul(ps[psi][:], lhsT=ones_bf[:], rhs=sqt[:, kc, :], start=(kc == 0), stop=(kc == 7)),
                 reads=[b_sq, b_const], writes=[psb[psi]])
        P.op("act", lambda e: e.activation(out=rs[:], in_=ps[psi][:], func=AF.Ln, bias=eps_t[:], scale=1.0 / D),
             reads=[psb[psi], b_const], writes=[b_rs])
        P.op("act", lambda e: e.activation(out=rs[:], in_=rs[:], func=AF.Exp, scale=-0.5), reads=[b_rs], writes=[b_rs])
        for kc in range(8):
            tb, btb = tmpbufs[kc % 2], b_tmp[kc % 2]
            P.op("dve", lambda e, kc=kc, tb=tb: e.scalar_tensor_tensor(out=tb[:], in0=src_tile[:, kc, :], scalar=A(kc), in1=rs[:],
                                                                     op0=ALU.mult, op1=ALU.mult),
                 reads=[b_src, b_rs, b_coef], writes=[btb])
            P.op("act", lambda e, kc=kc, tb=tb: e.activation(out=dst[:, kc, dst_sl], in_=tb[:], func=AF.Identity, bias=Bc(kc), scale=1.0),
                 reads=[btb, b_modv], writes=[b_dst])

    with ExitStack() as esA:
        hT = sb(esA, "hT", [128, 8, S], BF16)
        b_hT = [P.buf("hT") for _ in range(NT)]
        xT_v = xT.rearrange("(kc p) t -> p kc t", p=128)
        with ExitStack() as es:
            xt = [sb(es, f"xt{i}", [128, 8, TT], F32) for i in range(2)]
            b_xt = [P.buf("xt") for _ in range(2)]
            sqt = sb(es, "sqt", [128, 8, TT], BF16)
            b_sq = P.buf("sq")
            rs = sb(es, "rs", [128, TT], F32)
            b_rs = P.buf("rs")
            tmpb = [sb(es, f"tmpb{i}", [128, TT], F32) for i in range(2)]
            b_tmp = [P.buf("tmp") for _ in range(2)]
            for tt in range(NT):
                P.dma("sp", xt[tt % 2][:], xT_v[:, :, tt * TT:(tt + 1) * TT], b_xt[tt % 2], writes=[b_xt[tt % 2]])
                rms_modulate(es, xt[tt % 2], b_xt[tt % 2], hT, b_hT[tt], slice(tt * TT, (tt + 1) * TT), A_m, B_m,
                             tmpb, b_tmp, sqt, b_sq, rs, b_rs, 5)
            P.barrier()
        if stage >= 1:
            phaseA_proj(nc, P, esA, sb, ps, psb, hT, b_hT, w_in, PT, par, pcol, omm, b_par, b_coef)
        P.barrier()

    b_out = P.buf("outdma")
    b_wpre = P.buf("wpre")

    def precast_ffn():
        for k in range(8):
            P.dma("pool", WFI[k * 128:(k + 1) * 128, :].rearrange("p (a b) -> p a b", b=512),
                  w_ffn_in[k * 128:(k + 1) * 128, :].rearrange("p (a b) -> p a b", b=512), b_wpre, writes=[b_wpre])
        for k in range(DFF // 128):
            P.dma("pool", WFO[k * 128:(k + 1) * 128, :].rearrange("p (a b) -> p a b", b=512),
                  w_ffn_out[k * 128:(k + 1) * 128, :].rearrange("p (a b) -> p a b", b=512), b_wpre, writes=[b_wpre])

    if stage >= 5 and (stage < 2 or inject):
        precast_ffn()
    if stage >= 2 and not inject:
        phaseB_rwkv(nc, P, sb, ps, psb, PT, YA, par, pcol, b_par, w2, a2, g2, w0row, tmasks, dbg=dbg, after_consts=(precast_ffn if stage >= 5 else None))
    esCD = ExitStack()
    d1w = None
    if stage >= 4:
        stg32 = [sb(esCD, f"stg32_{i}", [128, 1024], F32) for i in range(2)]
        b_stg32 = [P.buf("stg32") for _ in range(2)]
        d1w = (load_weight_bf16(P, sb, esCD, "wa_bf", w_a, 6, D, stg32, b_stg32, eng="act"),
               load_weight_bf16(P, sb, esCD, "wb_bf", w_b, 2, D, stg32, b_stg32, eng="act"),
               load_weight_bf16(P, sb, esCD, "wo_bf", w_out, 8, D, stg32, b_stg32, eng="act"))
    if stage >= 3:
        phaseC_attn(nc, P, sb, ps, psb, PT, OT)
    if stage >= 4:
        phaseD1(nc, P, sb, ps, psb, PT, YA, OT, X1, xT, d1w, GM, ones_bf, eps_t, b_const, b_coef)
    esCD.close()
    if stage >= 5:
        phaseD2(nc, P, sb, ps, psb, X1, outT, WFI, WFO, b_wpre, A_f, B_f, GF, ones_bf, eps_t, b_const, b_coef, b_modv, b_out)
    if stage < 5:
        with ExitStack() as es:
            z = sb(es, "zt", [128, 8, 64], F32)
            bz = P.buf("z")
            P.op("pool", lambda e: e.memset(z[:], 0.0), writes=[bz])
            P.op("dve", lambda e: e.tensor_copy(out=z[:, 0, 0:48], in_=modv[:]), reads=[bz, b_modv], writes=[bz])
            P.dma("sp", outT.rearrange("(kc p) t -> p kc t", p=128)[:, :, 0:64], z[:], b_out, reads=[bz])
    P.barrier()
    P.emit()
    es_all.close()
    return nc


def phaseA_proj(nc, P, esA, sb, ps, psb, hT, b_hT, w_in, PT, par, pcol, omm, b_par, b_coef):
    with ExitStack() as es:
        NW = 3
        wstg = [sb(es, f"wstg{i}", [128, 8, 512], F32) for i in range(2)]
        b_wstg = [P.buf("wstg") for _ in range(2)]
        wbf = [sb(es, f"wbf{i}", [128, 8, 128], BF16) for i in range(NW)]
        b_wbf = [P.buf("wbf") for _ in range(NW)]
        stg = [sb(es, f"stg{i}", [128, S], BF16) for i in range(3)]
        b_stg = [P.buf("stg") for _ in range(3)]
        mup = [sb(es, f"mup{i}", [128, S + 1], F32) for i in range(2)]
        b_mup = [[P.buf("mup") for _ in range(NT + 1)] for _ in range(2)]
        cosT = sb(es, "cosT", [128, S], F32)
        sinT = sb(es, "sinT", [128, S], F32)
        perm = sb(es, "perm", [128, 128], BF16)
        ident = sb(es, "ident", [128, 128], BF16)
        qraw = [sb(es, f"qraw{i}", [128, TT], BF16) for i in range(2)]
        b_qraw = [P.buf("qraw") for _ in range(2)]
        t1 = [sb(es, f"t1_{i}", [128, TT], F32) for i in range(2)]
        b_t1 = [P.buf("t1") for _ in range(2)]
        t2 = [sb(es, f"t2_{i}", [128, TT], F32) for i in range(2)]
        b_t2 = [P.buf("t2") for _ in range(2)]
        sgn = sb(es, "sgn", [128, 1], F32)
        pi_t = sb(es, "pi_t", [128, 1], F32)
        b_tab = P.buf("tab")
        P.op("pool", lambda e: e.memset(ident[:], 1.0), writes=[b_tab])
        P.op("pool", lambda e: e.affine_select(out=ident[:], in_=ident[:], pattern=[[-1, 128]], compare_op=ALU.is_equal,
                                               fill=0.0, base=0, channel_multiplier=1), reads=[b_tab], writes=[b_tab])
        for h0 in (0, 64):
            P.op("pool", lambda e, h0=h0: e.tensor_copy(out=perm[:, h0:h0 + 32], in_=ident[:, h0 + 32:h0 + 64]), reads=[b_tab], writes=[b_tab])
            P.op("pool", lambda e, h0=h0: e.tensor_copy(out=perm[:, h0 + 32:h0 + 64], in_=ident[:, h0:h0 + 32]), reads=[b_tab], writes=[b_tab])
        for q4 in range(4):
            P.op("pool", lambda e, q4=q4: e.memset(sgn[q4 * 32:(q4 + 1) * 32, :], -1.0 if q4 % 2 == 0 else 1.0), writes=[b_tab])
        P.op("pool", lambda e: e.memset(pi_t[:], -math.pi), writes=[b_tab])
        for m in (0, 1):
            P.op("pool", lambda e, m=m: e.memset(mup[m][:, 0:1], 0.0), writes=[b_mup[m][0]])
        with ExitStack() as es2:
            ang_t = wstg[0]
            ki = mup[1][:, 1:S + 1].bitcast(mybir.dt.int32)
            kf = mup[0][:, 1:S + 1]
            P.op("pool", lambda e: e.iota(ang_t[:].rearrange("p a b -> p (a b)"), pattern=[[1, S]], base=0, channel_multiplier=0, allow_small_or_imprecise_dtypes=True),
                 reads=[b_tab], writes=[b_tab])
            P.op("dve", lambda e: e.tensor_scalar(out=ang_t[:].rearrange("p a b -> p (a b)"), in0=ang_t[:].rearrange("p a b -> p (a b)"), scalar1=pcol("inv_freq"), scalar2=1.0 / (2 * math.pi),
                                                  op0=ALU.mult, op1=ALU.mult), reads=[b_tab, b_par], writes=[b_tab])
            for (tab, addc) in ((sinT, 0.0), (cosT, 0.25)):
                P.op("dve", lambda e, tab=tab, addc=addc: e.tensor_scalar(out=tab[:], in0=ang_t[:].rearrange("p a b -> p (a b)"), scalar1=addc, scalar2=None, op0=ALU.add),
                     reads=[b_tab], writes=[b_tab])
                P.op("dve", lambda e, tab=tab: e.tensor_copy(out=ki, in_=tab[:]), reads=[b_tab], writes=[b_tab])
                P.op("dve", lambda e, tab=tab: e.tensor_copy(out=kf, in_=ki), reads=[b_tab], writes=[b_tab])
                P.op("dve", lambda e, tab=tab: e.tensor_tensor(out=tab[:], in0=tab[:], in1=kf, op=ALU.subtract), reads=[b_tab], writes=[b_tab])
                P.op("dve", lambda e, tab=tab: e.tensor_scalar(out=kf, in0=tab[:], scalar1=0.5, scalar2=None, op0=ALU.is_gt), reads=[b_tab], writes=[b_tab])
                P.op("dve", lambda e, tab=tab: e.tensor_tensor(out=tab[:], in0=tab[:], in1=kf, op=ALU.subtract), reads=[b_tab], writes=[b_tab])
                P.op("dve", lambda e, tab=tab: e.tensor_scalar(out=kf, in0=tab[:], scalar1=-0.5, scalar2=None, op0=ALU.is_lt), reads=[b_tab], writes=[b_tab])
                P.op("dve", lambda e, tab=tab: e.tensor_tensor(out=tab[:], in0=tab[:], in1=kf, op=ALU.add), reads=[b_tab], writes=[b_tab])
                P.op("act", lambda e, tab=tab: e.activation(out=tab[:], in_=tab[:], func=AF.Sin, scale=2 * math.pi - 2e-6), reads=[b_tab], writes=[b_tab])
            P.op("dve", lambda e: e.tensor_scalar(out=sinT[:], in0=sinT[:], scalar1=sgn[:], scalar2=None, op0=ALU.mult), reads=[b_tab], writes=[b_tab])
            P.barrier()
        b_tab.const = True

        w_in_v = w_in.rearrange("(kc p) n -> p kc n", p=128)
        NCH = INW // 128
        import os
        order = [int(v) for v in os.environ['KCH'].split(',')] if 'KCH' in os.environ else list(range(NCH))
        NCH = len(order)

        WG = 4

        def load_w(i):
            if i % WG == 0:
                gsl = (i // WG) % 2
                c0 = order[i] * 128
                ncol = 128 * min(WG, NCH - i)
                P.dma("sp", wstg[gsl][:, :, 0:ncol], w_in_v[:, :, c0:c0 + ncol], b_wstg[gsl], writes=[b_wstg[gsl]])
            gsl = (i // WG) % 2
            s_ = i % NW
            j = i % WG
            P.op("pool", lambda e, s_=s_, gsl=gsl, j=j: e.tensor_copy(out=wbf[s_][:], in_=wstg[gsl][:, :, j * 128:(j + 1) * 128]),
                 reads=[b_wstg[gsl]], writes=[b_wbf[s_]])

        load_w(0)
        if NCH > 1:
            load_w(1)
        pidx = 0
        ridx = 0
        for i, cc in enumerate(order):
            if i + 2 < NCH:
                load_w(i + 2)
            s = i % NW
            so = i % 3
            sg, bsg = stg[so], b_stg[so]
            mu_i = cc if cc < 20 else None
            mm = i % 2
            for tt in range(NT):
                pi = pidx % 5
                pidx += 1
                tsl = slice(tt * TT, (tt + 1) * TT)
                for kc in range(8):
                    P.op("pe", lambda e, pi=pi, s=s, kc=kc, tsl=tsl: e.matmul(ps[pi][:], lhsT=wbf[s][:, kc, :], rhs=hT[:, kc, tsl],
                                                                          start=(kc == 0), stop=(kc == 7)),
                         reads=[b_wbf[s], b_hT[tt]], writes=[psb[pi]])
                if cc < 20:
                    mcol = pcol("mu", cc)
                    ocol = omm[:, cc:cc + 1]
                    P.op("act", lambda e, pi=pi, mm=mm, tt=tt, mcol=mcol: e.activation(
                        out=mup[mm][:, 1 + tt * TT:1 + (tt + 1) * TT], in_=ps[pi][:], func=AF.Copy, scale=mcol),
                        reads=[psb[pi], b_par], writes=[b_mup[mm][tt + 1]])
                    if cc < 18:
                        P.op("dve", lambda e, pi=pi, mm=mm, tt=tt, ocol=ocol, sg=sg, tsl=tsl: e.scalar_tensor_tensor(
                            out=sg[:, tsl], in0=ps[pi][:], scalar=ocol, in1=mup[mm][:, tt * TT:(tt + 1) * TT], op0=ALU.mult, op1=ALU.add),
                            reads=[psb[pi], b_coef, b_mup[mm][tt], b_mup[mm][tt + 1]], writes=[bsg])
                    else:
                        tb, btb = t1[tt % 2], b_t1[tt % 2]
                        P.op("dve", lambda e, pi=pi, mm=mm, tt=tt, ocol=ocol, tb=tb: e.scalar_tensor_tensor(
                            out=tb[:], in0=ps[pi][:], scalar=ocol, in1=mup[mm][:, tt * TT:(tt + 1) * TT], op0=ALU.mult, op1=ALU.add),
                            reads=[psb[pi], b_coef, b_mup[mm][tt], b_mup[mm][tt + 1]], writes=[btb])
                        if cc == 18:
                            P.op("act", lambda e, tb=tb, sg=sg, tsl=tsl: e.activation(out=sg[0:64, tsl], in_=tb[0:64, :], func=AF.Tanh),
                                 reads=[btb], writes=[bsg])
                            P.op("act", lambda e, tb=tb, sg=sg, tsl=tsl: e.activation(out=sg[64:128, tsl], in_=tb[64:128, :], func=AF.Copy),
                                 reads=[btb], writes=[bsg])
                        else:
                            P.op("act", lambda e, tb=tb, sg=sg, tsl=tsl: e.activation(out=sg[:, tsl], in_=tb[:], func=AF.Sigmoid),
                                 reads=[btb], writes=[bsg])
                elif cc < 32:
                    ri = ridx % 2
                    ridx += 1
                    qr, bqr = qraw[ri], b_qraw[ri]
                    P.op("act", lambda e, pi=pi, qr=qr: e.activation(out=qr[:], in_=ps[pi][:], func=AF.Copy), reads=[psb[pi]], writes=[bqr])
                    P.op("pe", lambda e, ri=ri, qr=qr: e.matmul(ps[5 + ri][:], lhsT=perm[:], rhs=qr[:], start=True, stop=True),
                         reads=[bqr, b_tab], writes=[psb[5 + ri]])
                    P.op("dve", lambda e, pi=pi, ri=ri, tsl=tsl: e.tensor_tensor(out=t1[ri][:], in0=ps[pi][:], in1=cosT[:, tsl], op=ALU.mult),
                         reads=[psb[pi], b_tab], writes=[b_t1[ri]])
                    P.op("dve", lambda e, ri=ri, tsl=tsl: e.tensor_tensor(out=t2[ri][:], in0=ps[5 + ri][:], in1=sinT[:, tsl], op=ALU.mult),
                         reads=[psb[5 + ri], b_tab], writes=[b_t2[ri]])
                    P.op("dve", lambda e, ri=ri, sg=sg, tsl=tsl: e.tensor_tensor(out=sg[:, tsl], in0=t1[ri][:], in1=t2[ri][:], op=ALU.add),
                         reads=[b_t1[ri], b_t2[ri]], writes=[bsg])
                elif cc < 38:
                    P.op("act", lambda e, pi=pi, sg=sg, tsl=tsl: e.activation(out=sg[:, tsl], in_=ps[pi][:], func=AF.Copy),
                         reads=[psb[pi]], writes=[bsg])
                else:
                    P.op("act", lambda e, pi=pi, sg=sg, tsl=tsl: e.activation(out=sg[:, tsl], in_=ps[pi][:], func=AF.Sigmoid),
                         reads=[psb[pi]], writes=[bsg])
            P.dma("sp", PT[cc * 128:(cc + 1) * 128, :], sg[:], bsg, reads=[bsg])
        P.barrier()


def _pack_params(inp, b):
    cols = np.zeros((128, NPAR), np.float32)

    def put(name, vec):
        o, w = PC[name]
        cols[:, o:o + w] = np.asarray(vec, np.float32).reshape(w, 128).T

    put("c", inp["c"][b])
    put("b_ada", inp["b_ada"][0])
    for n in ("g_pre_mix", "g_post_mix", "g_pre_ffn", "g_post_ffn", "w0", "a0", "k_k", "k_a", "lnx_w", "lnx_b"):
        put(n, inp[n][0])
    put("mu", inp["mu_shift"][0])
    put("r_k", inp["r_k"][0].reshape(-1))
    half = 32
    inv_freq = (10000.0 ** (-np.arange(half, dtype=np.float32) / half)).astype(np.float32)
    cols[:, PC["inv_freq"][0]] = np.tile(inv_freq, 4)
    return cols


def _tri_masks():
    idx = np.arange(128)
    out = np.zeros((128, 7, 4, 128), np.float32)
    for lvl in range(7):
        b = 1 << lvl
        mU = ((idx[:, None] // b) % 2 == 0) & ((idx[None, :] // b) == (idx[:, None] // b) + 1)
        mL = mU.T
        out[:, lvl, 0], out[:, lvl, 1], out[:, lvl, 2], out[:, lvl, 3] = mU, mL, mU, mL
    return np.ascontiguousarray(out.reshape(128, 7 * 512))


def make_in_maps(inp):
    shared = {
        "tmasks": _tri_masks(),
        "w_ada": np.ascontiguousarray(inp["w_ada"][0]), "w_in": np.ascontiguousarray(inp["w_in"][0]),
        "w2": np.ascontiguousarray(inp["w2"][0]), "a2": np.ascontiguousarray(inp["a2"][0]),
        "g2": np.ascontiguousarray(inp["g2"][0]), "w_a": np.ascontiguousarray(inp["w_a"][0]),
        "w_b": np.ascontiguousarray(inp["w_b"][0]), "w_out": np.ascontiguousarray(inp["w_out"][0]),
        "w_ffn_in": np.ascontiguousarray(inp["w_ffn_in"][0]), "w_ffn_out": np.ascontiguousarray(inp["w_ffn_out"][0]),
    }
    maps = []
    for b in range(NC8):
        m = dict(shared)
        m["xT"] = np.ascontiguousarray(np.asarray(inp["x"][b], np.float32).T)
        m["params"] = _pack_params(inp, b)
        m["w0row"] = np.ascontiguousarray(inp["w0"][0].reshape(1, 768))
        maps.append(m)
    return maps


def kernel(**inputs):
    inp = {k: np.asarray(v) for k, v in inputs.items()}
    nc = build_nc()
    in_maps = make_in_maps(inp)
    res = run_bass_kernel_spmd(nc, in_maps, core_ids=list(range(NC8)))
    out = np.stack([np.ascontiguousarray(r["outT"].T) for r in res.results], axis=0)
    return out.astype(np.float32)


def phaseC_attn(nc, P, sb, ps, psb, PT, OT):
    with ExitStack() as es:
        ident = sb(es, "identC", [128, 128], BF16)
        maskT = sb(es, "maskT", [128, 256], BF16)
        ones64 = sb(es, "ones64", [128, 64], BF16)
        b_c = P.buf("constC")
        P.op("pool", lambda e: e.memset(ident[:], 1.0), writes=[b_c])
        P.op("pool", lambda e: e.affine_select(out=ident[:], in_=ident[:], pattern=[[-1, 128]], compare_op=ALU.is_equal,
                                               fill=0.0, base=0, channel_multiplier=1), reads=[b_c], writes=[b_c])
        P.op("pool", lambda e: e.memset(maskT[:], 1.0), reads=[b_c], writes=[b_c])
        P.op("pool", lambda e: e.affine_select(out=maskT[:, 0:128], in_=maskT[:, 0:128], pattern=[[-1, 128]], compare_op=ALU.is_ge,
                                               fill=0.0, base=0, channel_multiplier=1), reads=[b_c], writes=[b_c])
        P.op("pool", lambda e: e.affine_select(out=maskT[:, 128:256], in_=maskT[:, 128:256], pattern=[[1, 128]], compare_op=ALU.is_ge,
                                               fill=0.0, base=0, channel_multiplier=-1), reads=[b_c], writes=[b_c])
        P.op("pool", lambda e: e.memset(ones64[:], 1.0), reads=[b_c], writes=[b_c])
        P.barrier()
        b_c.const = True
        qkv = [[sb(es, f"qkv{i}_{j}", [128, S], BF16) for j in range(3)] for i in range(2)]
        b_qkv = [[P.buf("qkv") for j in range(3)] for i in range(2)]
        vtok = sb(es, "vtok", [128, 32, 128], BF16)
        b_vtok = [P.buf("vtok") for _ in range(8)]
        acc = sb(es, "acc", [128, 2, S], F32)
        b_acc = P.buf("acc")
        NPT = 4
        pT = [sb(es, f"pT{i}", [128, 2, 256], BF16) for i in range(NPT)]
        b_pT = [P.buf("pT") for _ in range(NPT)]
        o_bf = sb(es, "o_bf", [128, S], BF16)
        b_obf = P.buf("obf")
        rec = sb(es, "rec", [128, S], F32)
        b_rec = P.buf("rec")
        ps6b = ps[6][:].bitcast(BF16)

        def load_pair(idx, pp):
            st = idx % 2
            for j, base in enumerate((2560, 3328, 4096)):
                r0 = base + pp * 128
                P.dma("sp", qkv[st][j][:], PT[r0:r0 + 128, :], b_qkv[st][j], writes=[b_qkv[st][j]])

        seq = [(spn, g) for spn in (0, 1) for g in (0, 1, 2)]
        load_pair(0, seq[0][1] * 2 + seq[0][0])
        cnt = 0
        for idx, (spn, g) in enumerate(seq):
            pp = g * 2 + spn
            if idx + 1 < len(seq):
                load_pair(idx + 1, seq[idx + 1][1] * 2 + seq[idx + 1][0])
            st = idx % 2
            d = (1, 4, 16)[g]
            nb = S // d // 128
            qT, kT, vT = qkv[st]
            bq, bk, bv = b_qkv[st]
            view = lambda t: t[:].rearrange("p (n i r) -> p r n i", i=128, r=d)
            qv, kv, vv = view(qT), view(kT), view(vT)
            accv = acc[:].rearrange("p c (n i r) -> p c r n i", i=128, r=d)
            for g4 in range(8):
                for j in range(4):
                    b = g4 * 4 + j
                    r, n = b // nb, b % nb
                    P.op("pe", lambda e, j=j, r=r, n=n, vv=vv: e.transpose(ps6b[:, j * 128:(j + 1) * 128], vv[:, r, n, :], ident[:]),
                         reads=[bv, b_c], writes=[psb[6]])
                P.op("act", lambda e, g4=g4: e.activation(out=vtok[:, g4 * 4:(g4 + 1) * 4, :],
                                                          in_=ps6b[:, 0:512].rearrange("p (j c) -> p j c", c=128), func=AF.Copy),
                     reads=[psb[6]], writes=[b_vtok[g4]])
            def part1(b, cnt_, kv=kv, qv=qv, bk=bk, bq=bq, nb=nb):
                r, n = b // nb, b % nb
                np_ = n - 1 if n > 0 else n
                slot = cnt_ % NPT
                sbk = cnt_ % 3
                for h in (0, 1):
                    hs = slice(64 * h, 64 * h + 64)
                    bank = ((0, 1, 6), (2, 3, 7))[h][sbk]
                    P.op("pe", lambda e, bank=bank, hs=hs, r=r, np_=np_, n=n: e.matmul(
                        ps[bank][:, 0:128], lhsT=kv[hs, r, np_, :], rhs=qv[hs, r, n, :], start=True, stop=True),
                        reads=[bk, bq], writes=[psb[bank]])
                    P.op("pe", lambda e, bank=bank, hs=hs, r=r, n=n: e.matmul(
                        ps[bank][:, 128:256], lhsT=kv[hs, r, n, :], rhs=qv[hs, r, n, :], start=True, stop=True),
                        reads=[bk, bq], writes=[psb[bank]])
                    P.op("act", lambda e, bank=bank, slot=slot, h=h: e.activation(out=pT[slot][:, h, :], in_=ps[bank][:, 0:256],
                                                                                 func=AF.Exp, scale=0.125),
                         reads=[psb[bank]], writes=[b_pT[slot]])
                for h in (0, 1):
                    P.op("dve", lambda e, slot=slot, h=h: e.tensor_tensor(out=pT[slot][:, h, :], in0=pT[slot][:, h, :], in1=maskT[:], op=ALU.mult),
                         reads=[b_pT[slot], b_c], writes=[b_pT[slot]])

            def part2(b, cnt_, accv=accv, g=g, nb=nb):
                r, n = b // nb, b % nb
                bprev = b - 1 if n > 0 else b
                slot = cnt_ % NPT
                ob = 4 + cnt_ % 2
                for h in (0, 1):
                    hs = slice(64 * h, 64 * h + 64)
                    if n > 0:
                        P.op("pe", lambda e, ob=ob, hs=hs, bprev=bprev, slot=slot, h=h: e.matmul(
                            ps[ob][hs, 0:128], lhsT=vtok[:, bprev, hs], rhs=pT[slot][:, h, 0:128], start=True, stop=False),
                            reads=[b_vtok[bprev // 4], b_pT[slot]], writes=[psb[ob]])
                    P.op("pe", lambda e, ob=ob, hs=hs, b=b, slot=slot, h=h, n=n: e.matmul(
                        ps[ob][hs, 0:128], lhsT=vtok[:, b, hs], rhs=pT[slot][:, h, 128:256], start=(n == 0), stop=True),
                        reads=[b_vtok[b // 4], b_pT[slot]], writes=[psb[ob]])
                    if n > 0:
                        P.op("pe", lambda e, ob=ob, hs=hs, slot=slot, h=h: e.matmul(
                            ps[ob][hs, 128:256], lhsT=ones64[:], rhs=pT[slot][:, h, 0:128], start=True, stop=False),
                            reads=[b_c, b_pT[slot]], writes=[psb[ob]])
                    P.op("pe", lambda e, ob=ob, hs=hs, slot=slot, h=h, n=n: e.matmul(
                        ps[ob][hs, 128:256], lhsT=ones64[:], rhs=pT[slot][:, h, 128:256], start=(n == 0), stop=True),
                        reads=[b_c, b_pT[slot]], writes=[psb[ob]])
                src = ps[ob][:, 0:256].rearrange("p (c i) -> p c i", i=128)
                if g == 0:
                    P.op("dve", lambda e, src=src, r=r, n=n: e.tensor_copy(out=accv[:, :, r, n, :], in_=src),
                         reads=[psb[ob]], writes=[b_acc])
                else:
                    P.op("dve", lambda e, src=src, r=r, n=n: e.tensor_tensor(out=accv[:, :, r, n, :], in0=src, in1=accv[:, :, r, n, :], op=ALU.add),
                         reads=[psb[ob], b_acc], writes=[b_acc])

            part1(0, cnt)
            part1(1, cnt + 1)
            for b in range(32):
                if b + 2 < 32:
                    part1(b + 2, cnt + b + 2)
                part2(b, cnt + b)
            cnt += 32
            if g == 2:
                P.op("dve", lambda e: e.reciprocal(out=rec[:], in_=acc[:, 1, :]), reads=[b_acc], writes=[b_rec])
                P.op("pool", lambda e: e.tensor_tensor(out=o_bf[:], in0=acc[:, 0, :], in1=rec[:], op=ALU.mult), reads=[b_acc, b_rec], writes=[b_obf])
                P.dma("sp", OT[spn * 128:(spn + 1) * 128, :], o_bf[:], b_obf, reads=[b_obf])
        P.barrier()


def load_weight_bf16(P, sb, es, name, w_dram, nk, ncols, stg32, b_stg32, cnt0=0, eng="pool"):
    wt = sb(es, name, [128, nk, ncols], BF16)
    bw = P.buf(name)
    for kc in range(nk):
        si = (cnt0 + kc) % len(stg32)
        for c0 in range(0, ncols, 1024):
            c1 = min(ncols, c0 + 1024)
            P.dma("sp", stg32[si][:, 0:c1 - c0], w_dram[kc * 128:(kc + 1) * 128, c0:c1], b_stg32[si], writes=[b_stg32[si]])
            if eng == "act":
                P.op("act", lambda e, si=si, kc=kc, c0=c0, c1=c1: e.activation(out=wt[:, kc, c0:c1], in_=stg32[si][:, 0:c1 - c0], func=AF.Copy),
                     reads=[b_stg32[si]], writes=[bw])
            else:
                P.op(eng, lambda e, si=si, kc=kc, c0=c0, c1=c1: e.tensor_copy(out=wt[:, kc, c0:c1], in_=stg32[si][:, 0:c1 - c0]),
                     reads=[b_stg32[si]], writes=[bw])
            si = (si + 1) % len(stg32)
    return wt, bw


def phaseD1(nc, P, sb, ps, psb, PT, YA, OT, X1, xT, d1w, GM, ones_bf, eps_t, b_const, b_coef):
    with ExitStack() as es:
        (wa, b_wa), (wb, b_wb), (wo, b_wo) = d1w
        ya_t = [sb(es, f"ya_t{i}", [128, 6, TT], BF16) for i in range(2)]
        o_t = [sb(es, f"o_t{i}", [128, 2, TT], BF16) for i in range(2)]
        sga_t = [sb(es, f"sga_t{i}", [128, 8, TT], BF16) for i in range(2)]
        sgb_t = [sb(es, f"sgb_t{i}", [128, 8, TT], BF16) for i in range(2)]
        x_t = [sb(es, f"x_t{i}", [128, 8, TT], F32) for i in range(2)]
        b_in = [[P.buf("d1in") for _ in range(5)] for _ in range(2)]
        merged = sb(es, "merged", [128, 8, TT], BF16)
        b_merged = P.buf("merged")
        m3 = sb(es, "m3", [128, 8, TT], F32)
        b_m3 = P.buf("m3")
        sq = sb(es, "sqD", [128, 8, TT], BF16)
        b_sq = P.buf("sqD")
        rs = sb(es, "rsD", [128, TT], F32)
        b_rs = P.buf("rsD")
        m1 = [sb(es, f"m1_{i}", [128, TT], F32) for i in range(2)]
        m2 = [sb(es, f"m2_{i}", [128, TT], F32) for i in range(2)]
        b_m1 = [P.buf("m1") for _ in range(2)]
        b_m2 = [P.buf("m2") for _ in range(2)]
        x1_t = [sb(es, f"x1_t{i}", [128, 8, TT], F32) for i in range(2)]
        b_x1 = [P.buf("x1t") for _ in range(2)]
        YAv = YA.rearrange("(kc p) t -> p kc t", p=128)
        OTv = OT.rearrange("(kc p) t -> p kc t", p=128)
        GAv = PT[4864:5888, :].rearrange("(kc p) t -> p kc t", p=128)
        GBv = PT[5888:6912, :].rearrange("(kc p) t -> p kc t", p=128)
        xTv = xT.rearrange("(kc p) t -> p kc t", p=128)
        X1v = X1.rearrange("(kc p) t -> p kc t", p=128)

        def loads(tt):
            s2 = tt % 2
            tsl = slice(tt * TT, (tt + 1) * TT)
            for j, (dst, src) in enumerate(((ya_t, YAv), (o_t, OTv), (sga_t, GAv), (sgb_t, GBv), (x_t, xTv))):
                P.dma("sp", dst[s2][:], src[:, :, tsl], b_in[s2][j], writes=[b_in[s2][j]])

        merged2 = [merged, sb(es, "merged_b", [128, 8, TT], BF16)]
        b_merged2 = [b_merged, P.buf("merged_b")]
        cnt = [0]

        def s1(tt):
            s2 = tt % 2
            mg, bmg = merged2[s2], b_merged2[s2]
            for jc in range(8):
                c = cnt[0]
                cnt[0] += 1
                pa, pb = c % 2, 2 + c % 2
                mi = c % 2
                js = slice(jc * 128, (jc + 1) * 128)
                for kc in range(6):
                    P.op("pe", lambda e, pa=pa, kc=kc, js=js, s2=s2: e.matmul(ps[pa][:], lhsT=wa[:, kc, js], rhs=ya_t[s2][:, kc, :],
                                                                           start=(kc == 0), stop=(kc == 5)),
                         reads=[b_wa, b_in[s2][0]], writes=[psb[pa]])
                for kc in range(2):
                    P.op("pe", lambda e, pb=pb, kc=kc, js=js, s2=s2: e.matmul(ps[pb][:], lhsT=wb[:, kc, js], rhs=o_t[s2][:, kc, :],
                                                                           start=(kc == 0), stop=(kc == 1)),
                         reads=[b_wb, b_in[s2][1]], writes=[psb[pb]])
                P.op("dve", lambda e, pa=pa, mi=mi, jc=jc, s2=s2: e.tensor_tensor(out=m1[mi][:], in0=ps[pa][:], in1=sga_t[s2][:, jc, :], op=ALU.mult),
                     reads=[psb[pa], b_in[s2][2]], writes=[b_m1[mi]])
                P.op("dve", lambda e, pb=pb, mi=mi, jc=jc, s2=s2: e.tensor_tensor(out=m2[mi][:], in0=ps[pb][:], in1=sgb_t[s2][:, jc, :], op=ALU.mult),
                     reads=[psb[pb], b_in[s2][3]], writes=[b_m2[mi]])
                P.op("dve", lambda e, mi=mi, jc=jc, mg=mg: e.tensor_tensor(out=mg[:, jc, :], in0=m1[mi][:], in1=m2[mi][:], op=ALU.add),
                     reads=[b_m1[mi], b_m2[mi]], writes=[bmg])

        def s2f(tt):
            s2 = tt % 2
            mg, bmg = merged2[s2], b_merged2[s2]
            for jc in range(8):
                po = 4 + jc % 2
                js = slice(jc * 128, (jc + 1) * 128)
                for kc in range(8):
                    P.op("pe", lambda e, po=po, kc=kc, js=js, mg=mg: e.matmul(ps[po][:], lhsT=wo[:, kc, js], rhs=mg[:, kc, :],
                                                                           start=(kc == 0), stop=(kc == 7)),
                         reads=[b_wo, bmg], writes=[psb[po]])
                P.op("act", lambda e, po=po, jc=jc: e.activation(out=m3[:, jc, :], in_=ps[po][:], func=AF.Copy), reads=[psb[po]], writes=[b_m3])
            P.op("act", lambda e: e.activation(out=sq[:], in_=m3[:], func=AF.Square), reads=[b_m3], writes=[b_sq])
            for kc in range(8):
                P.op("pe", lambda e, kc=kc: e.matmul(ps[6][:], lhsT=ones_bf[:], rhs=sq[:, kc, :], start=(kc == 0), stop=(kc == 7)),
                     reads=[b_sq, b_const], writes=[psb[6]])
            P.op("act", lambda e: e.activation(out=rs[:], in_=ps[6][:], func=AF.Ln, bias=eps_t[:], scale=1.0 / D),
                 reads=[psb[6], b_const], writes=[b_rs])
            P.op("act", lambda e: e.activation(out=rs[:], in_=rs[:], func=AF.Exp, scale=-0.5), reads=[b_rs], writes=[b_rs])

        def s3(tt):
            s2 = tt % 2
            tsl = slice(tt * TT, (tt + 1) * TT)
            for jc in range(8):
                mi = jc % 2
                P.op("dve", lambda e, mi=mi, jc=jc: e.scalar_tensor_tensor(out=m1[mi][:], in0=m3[:, jc, :], scalar=GM(jc), in1=rs[:],
                                                                         op0=ALU.mult, op1=ALU.mult),
                     reads=[b_m3, b_rs, b_coef], writes=[b_m1[mi]])
                P.op("dve", lambda e, mi=mi, jc=jc, s2=s2: e.tensor_tensor(out=x1_t[s2][:, jc, :], in0=m1[mi][:], in1=x_t[s2][:, jc, :], op=ALU.add),
                     reads=[b_m1[mi], b_in[s2][4]], writes=[b_x1[s2]])
            P.dma("sp", X1v[:, :, tsl], x1_t[s2][:], b_x1[s2], reads=[b_x1[s2]])

        loads(0)
        if NT > 1:
            loads(1)
        s1(0)
        for tt in range(NT):
            s2f(tt)
            if tt + 1 < NT:
                s1(tt + 1)
            s3(tt)
            if tt + 2 < NT:
                loads(tt + 2)
        P.barrier()


def phaseD2(nc, P, sb, ps, psb, X1, outT, WFI, WFO, b_wpre, A_f, B_f, GF, ones_bf, eps_t, b_const, b_coef, b_modv, b_out):
    T2 = 256
    NT2 = S // T2
    NH = DFF // 128
    with ExitStack() as es:
        wfi = sb(es, "wfi", [128, 8, 2 * DFF], BF16)
        wfo = sb(es, "wfo", [128, NH, D], BF16)
        b_wfi, b_wfo = P.buf("wfi"), P.buf("wfo")
        WFIv = WFI.rearrange("(kc p) n -> p kc n", p=128)
        WFOv = WFO.rearrange("(kc p) n -> p kc n", p=128)
        for kc in range(8):
            P.dma("sp", wfi[:, kc, :], WFIv[:, kc, :], b_wfi, reads=[b_wpre], writes=[b_wfi])
        for k0 in range(0, NH, 11):
            P.dma("sp", wfo[:, k0:k0 + 11, :], WFOv[:, k0:k0 + 11, :], b_wfo, reads=[b_wpre], writes=[b_wfo])
        x1_t = [sb(es, f"x1f{i}", [128, 8, T2], F32) for i in range(2)]
        b_x1 = [P.buf("x1f") for _ in range(2)]
        sq = sb(es, "sqF", [128, 8, T2], BF16)
        b_sq = P.buf("sqF")
        rs = sb(es, "rsF", [128, T2], F32)
        b_rs = P.buf("rsF")
        tmpb = [sb(es, f"tmpF{i}", [128, T2], F32) for i in range(2)]
        b_tmp = [P.buf("tmpF") for _ in range(2)]
        h2 = sb(es, "h2", [128, 8, T2], BF16)
        b_h2 = P.buf("h2")
        su = [sb(es, f"su{i}", [128, T2], F32) for i in range(2)]
        b_su = [P.buf("su") for _ in range(2)]
        actT = sb(es, "actT", [128, NH, T2], BF16)
        b_act = P.buf("actT")
        f_t = sb(es, "f_t", [128, 8, T2], F32)
        b_f = P.buf("f_t")
        X1v = X1.rearrange("(kc p) t -> p kc t", p=128)
        outv = outT.rearrange("(kc p) t -> p kc t", p=128)
        h2b = [h2, sb(es, "h2b", [128, 8, T2], BF16)]
        b_h2b = [b_h2, P.buf("h2b")]
        sqE = sb(es, "sqE", [128, 8, T2], BF16)
        b_sqE = P.buf("sqE")
        rsE = sb(es, "rsE", [128, T2], F32)
        b_rsE = P.buf("rsE")
        cnt = [0]

        def load_x1(tt):
            P.dma("sp", x1_t[tt % 2][:], X1v[:, :, tt * T2:(tt + 1) * T2], b_x1[tt % 2], writes=[b_x1[tt % 2]])

        def pro_sq(tt):
            s2 = tt % 2
            xt = x1_t[s2]
            P.op("act", lambda e, xt=xt: e.activation(out=sq[:], in_=xt[:], func=AF.Square), reads=[b_x1[s2]], writes=[b_sq])

        def pro(tt, with_sq=True):
            s2 = tt % 2
            xt = x1_t[s2]
            hh, bhh = h2b[s2], b_h2b[s2]
            if with_sq:
                pro_sq(tt)
            for kc in range(8):
                P.op("pe", lambda e, kc=kc: e.matmul(ps[6][:, 0:T2], lhsT=ones_bf[:], rhs=sq[:, kc, :], start=(kc == 0), stop=(kc == 7)),
                     reads=[b_sq, b_const], writes=[psb[6]])
            P.op("act", lambda e: e.activation(out=rs[:], in_=ps[6][:, 0:T2], func=AF.Ln, bias=eps_t[:], scale=1.0 / D),
                 reads=[psb[6], b_const], writes=[b_rs])
            P.op("act", lambda e: e.activation(out=rs[:], in_=rs[:], func=AF.Exp, scale=-0.5), reads=[b_rs], writes=[b_rs])
            for kc in range(8):
                tb, btb = tmpb[kc % 2], b_tmp[kc % 2]
                P.op("dve", lambda e, kc=kc, tb=tb, xt=xt: e.scalar_tensor_tensor(out=tb[:], in0=xt[:, kc, :], scalar=A_f(kc), in1=rs[:],
                                                                               op0=ALU.mult, op1=ALU.mult),
                     reads=[b_x1[s2], b_rs, b_coef], writes=[btb])
                P.op("act", lambda e, kc=kc, tb=tb, hh=hh: e.activation(out=hh[:, kc, :], in_=tb[:], func=AF.Identity, bias=B_f(kc), scale=1.0),
                     reads=[btb, b_modv], writes=[bhh])

        def ug(tt, mid=None):
            s2 = tt % 2
            hh, bhh = h2b[s2], b_h2b[s2]
            for hc in range(NH):
                if mid is not None and hc == NH // 2:
                    mid()
                c = cnt[0]
                cnt[0] += 1
                pu, pg = c % 2, 2 + c % 2
                si = c % 2
                for kc in range(8):
                    P.op("pe", lambda e, pu=pu, kc=kc, hc=hc, hh=hh: e.matmul(ps[pu][:, 0:T2], lhsT=wfi[:, kc, hc * 128:(hc + 1) * 128], rhs=hh[:, kc, :],
                                                                           start=(kc == 0), stop=(kc == 7)),
                         reads=[b_wfi, bhh], writes=[psb[pu]])
                for kc in range(8):
                    P.op("pe", lambda e, pg=pg, kc=kc, hc=hc, hh=hh: e.matmul(ps[pg][:, 0:T2], lhsT=wfi[:, kc, DFF + hc * 128:DFF + (hc + 1) * 128], rhs=hh[:, kc, :],
                                                                           start=(kc == 0), stop=(kc == 7)),
                         reads=[b_wfi, bhh], writes=[psb[pg]])
                P.op("act", lambda e, pu=pu, si=si: e.activation(out=su[si][:], in_=ps[pu][:, 0:T2], func=AF.Silu), reads=[psb[pu]], writes=[b_su[si]])
                P.op("dve", lambda e, pg=pg, si=si, hc=hc: e.tensor_tensor(out=actT[:, hc, :], in0=ps[pg][:, 0:T2], in1=su[si][:], op=ALU.mult),
                     reads=[psb[pg], b_su[si]], writes=[b_act])

        def ff(tt):
            for jc in range(8):
                pf = 4 + jc % 2
                for hc in range(NH):
                    P.op("pe", lambda e, pf=pf, hc=hc, jc=jc: e.matmul(ps[pf][:, 0:T2], lhsT=wfo[:, hc, jc * 128:(jc + 1) * 128], rhs=actT[:, hc, :],
                                                                    start=(hc == 0), stop=(hc == NH - 1)),
                         reads=[b_wfo, b_act], writes=[psb[pf]])
                P.op("act", lambda e, pf=pf, jc=jc: e.activation(out=f_t[:, jc, :], in_=ps[pf][:, 0:T2], func=AF.Copy), reads=[psb[pf]], writes=[b_f])

        def epi(tt):
            s2 = tt % 2
            xt = x1_t[s2]
            P.op("act", lambda e: e.activation(out=sqE[:], in_=f_t[:], func=AF.Square), reads=[b_f], writes=[b_sqE])
            for kc in range(8):
                P.op("pe", lambda e, kc=kc: e.matmul(ps[7][:, 0:T2], lhsT=ones_bf[:], rhs=sqE[:, kc, :], start=(kc == 0), stop=(kc == 7)),
                     reads=[b_sqE, b_const], writes=[psb[7]])
            P.op("act", lambda e: e.activation(out=rsE[:], in_=ps[7][:, 0:T2], func=AF.Ln, bias=eps_t[:], scale=1.0 / D),
                 reads=[psb[7], b_const], writes=[b_rsE])
            P.op("act", lambda e: e.activation(out=rsE[:], in_=rsE[:], func=AF.Exp, scale=-0.5), reads=[b_rsE], writes=[b_rsE])
            for jc in range(8):
                tb, btb = tmpb[jc % 2], b_tmp[jc % 2]
                P.op("dve", lambda e, jc=jc, tb=tb: e.scalar_tensor_tensor(out=tb[:], in0=f_t[:, jc, :], scalar=GF(jc), in1=rsE[:],
                                                                        op0=ALU.mult, op1=ALU.mult),
                     reads=[b_f, b_rsE, b_coef], writes=[btb])
                P.op("dve", lambda e, jc=jc, tb=tb, xt=xt: e.tensor_tensor(out=xt[:, jc, :], in0=tb[:], in1=xt[:, jc, :], op=ALU.add),
                     reads=[btb], writes=[b_x1[s2]])
            P.dma("sp", outv[:, :, tt * T2:(tt + 1) * T2], xt[:], b_x1[s2], reads=[b_x1[s2]])

        load_x1(0)
        if NT2 > 1:
            load_x1(1)
        pro(0)
        for tt in range(NT2):
            if tt + 1 < NT2:
                ug(tt, mid=lambda tt=tt: pro_sq(tt + 1))
                pro(tt + 1, with_sq=False)
            else:
                ug(tt)
            ff(tt)
            epi(tt)
            if tt + 2 < NT2:
                load_x1(tt + 2)
        P.barrier()


def phaseB_rwkv(nc, P, sb, ps, psb, PT, YA, par, pcol, b_par, w2, a2, g2, w0row, tmasks, dbg=False, after_consts=None):
    CDEC = math.exp(-0.5)
    NP = 6
    with ExitStack() as es:
        identB = sb(es, "identB", [128, 128], BF16)
        TRI = sb(es, "TRI", [128, 256], F32)
        bones = sb(es, "bones", [128, 128], BF16)
        bonesF = sb(es, "bonesF", [128, 128], F32)
        mask4 = sb(es, "mask4", [128, 512], BF16)
        maskLT = sb(es, "maskLT", [128, 128], F32)
        gneps = sb(es, "gneps", [128, 1], F32)
        w0bc = sb(es, "w0bc", [128, 768], F32)
        lw = [sb(es, f"lw{i}", [128, 768], BF16) for i in range(3)]
        Sbd = sb(es, "Sbd", [128, NP, 128], BF16)
        mlev = sb(es, "mlev", [128, 7, 512], BF16)
        ident4 = sb(es, "ident4", [128, 4, 128], BF16)
        b_c = P.buf("constB")
        b_S = [P.buf("S") for _ in range(NP)]
        cw = lambda fn, rd=(): P.op("pool", fn, reads=[b_c] + list(rd), writes=[b_c])
        cw(lambda e: e.memset(identB[:], 1.0))
        cw(lambda e: e.affine_select(out=identB[:], in_=identB[:], pattern=[[-1, 128]], compare_op=ALU.is_equal, fill=0.0, base=0, channel_multiplier=1))
        cw(lambda e: e.memset(TRI[:], 1.0))
        cw(lambda e: e.affine_select(out=TRI[:, 0:128], in_=TRI[:, 0:128], pattern=[[1, 128]], compare_op=ALU.is_ge, fill=0.0, base=0, channel_multiplier=-1))
        cw(lambda e: e.affine_select(out=TRI[:, 128:256], in_=TRI[:, 128:256], pattern=[[1, 128]], compare_op=ALU.is_gt, fill=0.0, base=0, channel_multiplier=-1))
        cw(lambda e: e.memset(mask4[:], 1.0))
        for q4 in range(4):
            op_ = ALU.is_gt if q4 % 2 == 0 else ALU.is_ge
            cw(lambda e, q4=q4, op_=op_: e.affine_select(out=mask4[:, q4 * 128:(q4 + 1) * 128], in_=mask4[:, q4 * 128:(q4 + 1) * 128],
                                                         pattern=[[1, 128]], compare_op=op_, fill=0.0, base=0, channel_multiplier=-1))
        cw(lambda e: e.memset(maskLT[:], 1.0))
        cw(lambda e: e.affine_select(out=maskLT[:], in_=maskLT[:], pattern=[[-1, 128]], compare_op=ALU.is_gt, fill=0.0, base=0, channel_multiplier=1))
        cw(lambda e: e.memset(bones[:], 0.0))
        cw(lambda e: e.memset(bonesF[:], 0.0))
        for h in (0, 1):
            hs = slice(64 * h, 64 * h + 64)
            cw(lambda e, hs=hs: e.memset(bones[hs, hs], 1.0))
            cw(lambda e, hs=hs: e.memset(bonesF[hs, hs], 1.0 / 64))
        cw(lambda e: e.memset(gneps[:], 64e-5))
        cw(lambda e: e.memset(Sbd[:], 0.0))
        for j4 in range(4):
            cw(lambda e, j4=j4: e.tensor_copy(out=ident4[:, j4, :], in_=identB[:]))
        with ExitStack() as es2:
            st32 = sb(es2, "st32B", [128, 768], F32)
            b_st = P.buf("st32B")
            for i, (wd, nr) in enumerate(((w2, 64), (a2, 64), (g2, 128))):
                P.dma("sp", st32[0:nr, :], wd[:, :], b_st, writes=[b_st])
                P.op("dve", lambda e, i=i, nr=nr: e.tensor_copy(out=lw[i][0:nr, :], in_=st32[0:nr, :]), reads=[b_st], writes=[b_c])
            P.dma("sp", w0bc[:], w0row.partition_broadcast(128), b_st, reads=[b_st], writes=[b_c])
            st_b = sb(es2, "st32Bb", [128, 768], F32)
            hi_b = sb(es2, "hi96", [128, 768], BF16)
            b_w0 = P.buf("w0hl")
            P.op("dve", lambda e: e.memset(lw[0][64:128, :], 0.0), reads=[b_c], writes=[b_c])
            P.dma("sp", st_b[64:65, :], w0row[:, :], b_w0, writes=[b_w0])
            P.dma("sp", st_b[96:97, :], w0row[:, :], b_w0, writes=[b_w0])
            P.op("dve", lambda e: e.tensor_copy(out=lw[0][64:65, :], in_=st_b[64:65, :]), reads=[b_w0, b_c], writes=[b_c])
            P.op("dve", lambda e: e.tensor_copy(out=hi_b[96:97, :], in_=st_b[96:97, :]), reads=[b_w0], writes=[b_w0])
            P.op("dve", lambda e: e.tensor_copy(out=st32[96:97, :], in_=hi_b[96:97, :]), reads=[b_w0, b_st], writes=[b_st])
            P.op("dve", lambda e: e.tensor_tensor(out=lw[0][96:97, :], in0=st_b[96:97, :], in1=st32[96:97, :], op=ALU.subtract),
                 reads=[b_w0, b_st, b_c], writes=[b_c])
            with ExitStack() as es3:
                mst = sb(es3, "mst", [128, 7 * 512], F32)
                b_mst = P.buf("mst")
                P.dma("sp", mst[:], tmasks[:, :], b_mst, writes=[b_mst])
                P.op("dve", lambda e: e.tensor_copy(out=mlev[:].rearrange("p a b -> p (a b)"), in_=mst[:]), reads=[b_mst], writes=[b_c])
                P.barrier()
            P.barrier()
        b_c.const = True
        w2bf, a2bf, g2bf = lw
        if after_consts is not None:
            after_consts()

        GT = 256
        NG = S // GT
        rkv_g = [sb(es, f"rkv_g{i}", [128, 18, GT], BF16) for i in range(2)]
        twl_g = [sb(es, f"twl_g{i}", [128, GT], BF16) for i in range(2)]
        b_twl1 = P.buf("twl_ones")
        for i in range(2):
            P.op("pool", lambda e, i=i: e.memset(twl_g[i][64:128, :], 1.0), writes=[b_twl1])
        P.barrier()
        al_g = [sb(es, f"al_g{i}", [64, GT], BF16) for i in range(2)]
        sgl_g = [sb(es, f"sgl_g{i}", [128, GT], BF16) for i in range(2)]
        ya_g = [sb(es, f"ya_g{i}", [128, NP, GT], BF16) for i in range(2)]
        b_g = [[P.buf("grp") for _ in range(4)] for _ in range(2)]
        b_ya = [P.buf("ya_g") for _ in range(2)]
        NF, NH = 16, 70
        Ft = [sb(es, f"Ft{hp}", [128, NF, 128], F32) for hp in range(NP)]
        Ht = [sb(es, f"Ht{hp}", [128, NH, 128], BF16) for hp in range(NP)]
        bF = [[P.buf("F") for _ in range(NF)] for _ in range(NP)]
        bH = [[P.buf("H") for _ in range(NH)] for _ in range(NP)]
        PTv = PT[0:2304, :].rearrange("(c p) t -> p c t", p=128)
        YAv = YA.rearrange("(c p) t -> p c t", p=128)
        ps0b = ps[0][:].bitcast(BF16)

        def load_group(gi):
            s2 = gi % 2
            gsl = slice(gi * GT, (gi + 1) * GT)
            P.dma("sp", rkv_g[s2][:], PTv[:, :, gsl], b_g[s2][0], writes=[b_g[s2][0]])
            P.dma("sp", twl_g[s2][0:64, :], PT[2304:2368, gsl], b_g[s2][1], writes=[b_g[s2][1]])
            P.dma("sp", al_g[s2][:], PT[2368:2432, gsl], b_g[s2][2], writes=[b_g[s2][2]])
            P.dma("sp", sgl_g[s2][:], PT[2432:2560, gsl], b_g[s2][3], writes=[b_g[s2][3]])

        load_group(0)
        import os
        rr = [0]

        def tile_body(n):
            TPG = GT // 128
            gi, s2 = n // TPG, (n // TPG) % 2
            def pre():
                if n % TPG == 0 and gi + 1 < NG:
                    load_group(gi + 1)
            ts = slice((n % TPG) * 128, (n % TPG + 1) * 128)
            bg = b_g[s2]
            steps = []
            segs = []
            curh = [{}]

            def B(hp, k):
                cur = curh[0]
                if k not in cur:
                    cur[k] = rr[0] % 8
                    rr[0] += 1
                return cur[k]
            psbf = [ps[i][:].bitcast(BF16) for i in range(8)]

            segkind = {}

            def seg(kind="ps"):
                segs.append(len(steps))
                segkind[len(steps)] = kind
            F = lambda hp, i: Ft[hp][:, i, :]
            H = lambda hp, i: Ht[hp][:, i, :]
            H4 = lambda hp, i: Ht[hp][:, i:i + 4, :].rearrange("p a b -> p (a b)")
            rT = lambda hp: rkv_g[s2][:, hp, ts]
            kT = lambda hp: rkv_g[s2][:, 6 + hp, ts]
            vT = lambda hp: rkv_g[s2][:, 12 + hp, ts]
            cs = lambda hp: slice(hp * 128, (hp + 1) * 128)

            def add(eng, fn, rd, wr):
                steps.append((eng, fn, rd, wr))

            seg()
            add("pe", lambda hp: (lambda e: e.matmul(ps[B(hp, 0)][:, 0:128], lhsT=twl_g[s2][0:97, ts], rhs=w2bf[0:97, cs(hp)], start=True, stop=True)),
                lambda hp: [bg[1], b_c], lambda hp: [psb[B(hp, 0)]])
            add("pe", lambda hp: (lambda e: e.matmul(ps[B(hp, 0)][:, 128:256], lhsT=a2bf[0:64, cs(hp)], rhs=al_g[s2][:, ts], start=True, stop=True)),
                lambda hp: [bg[2], b_c], lambda hp: [psb[B(hp, 0)]])
            add("act", lambda hp: (lambda e: e.activation(out=F(hp, 1), in_=ps[B(hp, 0)][:, 0:128], func=AF.Sigmoid)),
                lambda hp: [psb[B(hp, 0)]], lambda hp: [bF[hp][1]])
            add("act", lambda hp: (lambda e: e.activation(out=F(hp, 2), in_=ps[B(hp, 0)][:, 128:256], func=AF.Sigmoid, bias=pcol("a0", hp), scale=1.0)),
                lambda hp: [psb[B(hp, 0)], b_par], lambda hp: [bF[hp][2]])
            seg()
            add("pe", lambda hp: (lambda e: e.matmul(ps[B(hp, 1)][:, 0:256], lhsT=F(hp, 1), rhs=TRI[:], start=True, stop=True)),
                lambda hp: [bF[hp][1], b_c], lambda hp: [psb[B(hp, 1)]])
            add("act", lambda hp: (lambda e: e.activation(out=Ft[hp][:, 4:6, :], in_=ps[B(hp, 1)][:, 0:256].rearrange("p (j c) -> p j c", c=128), func=AF.Exp, scale=-CDEC)),
                lambda hp: [psb[B(hp, 1)]], lambda hp: [bF[hp][4], bF[hp][5]])
            add("act", lambda hp: (lambda e: e.activation(out=F(hp, 6), in_=ps[B(hp, 1)][:, 0:128], func=AF.Exp, scale=CDEC)),
                lambda hp: [psb[B(hp, 1)]], lambda hp: [bF[hp][6]])
            add("dve", lambda hp: (lambda e: e.tensor_scalar(out=F(hp, 7), in0=F(hp, 6), scalar1=Ft[hp][:, 4, 127:128], scalar2=None, op0=ALU.mult)),
                lambda hp: [bF[hp][6], bF[hp][4]], lambda hp: [bF[hp][7]])
            seg()
            add("act", lambda hp: (lambda e: e.activation(out=H(hp, 0), in_=kT(hp), func=AF.Square, scale=pcol("k_k", hp))),
                lambda hp: [bg[0], b_par], lambda hp: [bH[hp][0]])
            add("pe", lambda hp: (lambda e: e.matmul(ps[B(hp, 1)][:, 256:384], lhsT=bones[:], rhs=H(hp, 0), start=True, stop=True)),
                lambda hp: [bH[hp][0], b_c], lambda hp: [psb[B(hp, 1)]])
            add("act", lambda hp: (lambda e: e.activation(out=F(hp, 8), in_=ps[B(hp, 1)][:, 256:384], func=AF.Ln)),
                lambda hp: [psb[B(hp, 1)], bF[hp][7]], lambda hp: [bF[hp][8]])
            seg("ew")
            add("act", lambda hp: (lambda e: e.activation(out=F(hp, 8), in_=F(hp, 8), func=AF.Exp, scale=-0.5)),
                lambda hp: [], lambda hp: [bF[hp][8]])
            add("dve", lambda hp: (lambda e: e.scalar_tensor_tensor(out=F(hp, 9), in0=kT(hp), scalar=pcol("k_k", hp), in1=F(hp, 8), op0=ALU.mult, op1=ALU.mult)),
                lambda hp: [bg[0], b_par, bF[hp][8]], lambda hp: [bF[hp][9]])
            add("dve", lambda hp: (lambda e: e.tensor_scalar(out=F(hp, 10), in0=F(hp, 2), scalar1=-1.0, scalar2=pcol("k_a", hp), op0=ALU.add, op1=ALU.mult)),
                lambda hp: [bF[hp][2], b_par], lambda hp: [bF[hp][10]])
            add("dve", lambda hp: (lambda e: e.scalar_tensor_tensor(out=F(hp, 10), in0=F(hp, 10), scalar=1.0, in1=kT(hp), op0=ALU.add, op1=ALU.mult)),
                lambda hp: [bg[0]], lambda hp: [bF[hp][10]])
            add("dve", lambda hp: (lambda e: e.scalar_tensor_tensor(out=F(hp, 11), in0=F(hp, 9), scalar=-1.0, in1=F(hp, 2), op0=ALU.mult, op1=ALU.mult)),
                lambda hp: [bF[hp][9], bF[hp][2]], lambda hp: [bF[hp][11]])
            add("dve", lambda hp: (lambda e: e.tensor_tensor(out=H(hp, 2), in0=F(hp, 9), in1=F(hp, 5), op=ALU.mult)),
                lambda hp: [bF[hp][9], bF[hp][5]], lambda hp: [bH[hp][2]])
            add("dve", lambda hp: (lambda e: e.tensor_tensor(out=H(hp, 3), in0=rT(hp), in1=F(hp, 4), op=ALU.mult)),
                lambda hp: [bg[0], bF[hp][4]], lambda hp: [bH[hp][3]])
            add("dve", lambda hp: (lambda e: e.tensor_tensor(out=Ht[hp][:, 4:6, :], in0=Ft[hp][:, 10:12, :],
                                                           in1=F(hp, 6).unsqueeze(1).to_broadcast([128, 2, 128]), op=ALU.mult)),
                lambda hp: [bF[hp][10], bF[hp][11], bF[hp][6]], lambda hp: [bH[hp][4], bH[hp][5]])
            add("dve", lambda hp: (lambda e: e.tensor_tensor(out=Ht[hp][:, 6:8, :], in0=Ft[hp][:, 10:12, :],
                                                           in1=F(hp, 7).unsqueeze(1).to_broadcast([128, 2, 128]), op=ALU.mult)),
                lambda hp: [bF[hp][10], bF[hp][11], bF[hp][7]], lambda hp: [bH[hp][6], bH[hp][7]])
            seg("ew")
            add("dve", lambda hp: (lambda e: e.scalar_tensor_tensor(out=H(hp, 1), in0=rT(hp), scalar=pcol("r_k", hp), in1=F(hp, 10), op0=ALU.mult, op1=ALU.mult)),
                lambda hp: [bg[0], b_par, bF[hp][10]], lambda hp: [bH[hp][1]])
            seg()
            add("pe", lambda hp: (lambda e: e.matmul(ps[B(hp, 1)][:, 384:512], lhsT=bones[:], rhs=H(hp, 1), start=True, stop=True)),
                lambda hp: [bH[hp][1], b_c], lambda hp: [psb[B(hp, 1)]])
            add("dve", lambda hp: (lambda e: e.tensor_tensor(out=F(hp, 15), in0=ps[B(hp, 1)][:, 384:512], in1=vT(hp), op=ALU.mult)),
                lambda hp: [psb[B(hp, 1)], bg[0]], lambda hp: [bF[hp][15]])
            seg()
            for j, src in enumerate((lambda hp: H(hp, 2), vT, lambda hp: H(hp, 6), lambda hp: H(hp, 7))):
                rdj = [lambda hp: [bH[hp][2]], lambda hp: [bg[0]], lambda hp: [bH[hp][6]], lambda hp: [bH[hp][7]]][j]
                add("pe", lambda hp, j=j, src=src: (lambda e: e.transpose(psbf[B(hp, 0)][:, j * 128:(j + 1) * 128], src(hp), identB[:])),
                    lambda hp, rdj=rdj: rdj(hp) + [b_c], lambda hp: [psb[B(hp, 0)]])
            add("act", lambda hp: (lambda e: e.activation(out=Ht[hp][:, 8:12, :], in_=psbf[B(hp, 0)][:, 0:512].rearrange("p (j c) -> p j c", c=128), func=AF.Copy)),
                lambda hp: [psb[B(hp, 0)]], lambda hp: [bH[hp][8], bH[hp][9], bH[hp][10], bH[hp][11]])
            seg()
            for h in (0, 1):
                hs = slice(64 * h, 64 * h + 64)
                xb = 2 + h
                mb = 4 + h
                add("pe", lambda hp, hs=hs, h=h: (lambda e: e.matmul(ps[B(hp, 2 + h)][:, 0:256], lhsT=Ht[hp][hs, 5, :],
                                                                    rhs=Ht[hp][hs, 2:4, :], start=True, stop=True)),
                    lambda hp: [bH[hp][5], bH[hp][2], bH[hp][3]], lambda hp, h=h: [psb[B(hp, 2 + h)]])
                add("pe", lambda hp, hs=hs, h=h: (lambda e: e.matmul(ps[B(hp, 2 + h)][:, 256:512], lhsT=Ht[hp][hs, 4, :],
                                                                    rhs=Ht[hp][hs, 2:4, :], start=True, stop=True)),
                    lambda hp: [bH[hp][4], bH[hp][2], bH[hp][3]], lambda hp, h=h: [psb[B(hp, 2 + h)]])
                add("pe", lambda hp, hs=hs, h=h: (lambda e: e.matmul(ps[B(hp, h)][:, 0:128], lhsT=Ht[hp][hs, 2, :],
                                                                    rhs=Ht[hp][hs, 5, :], start=True, stop=True)),
                    lambda hp: [bH[hp][5], bH[hp][2]], lambda hp, h=h: [psb[B(hp, h)]])
                xs0 = 12 + 4 * h
                add("dve", lambda hp, h=h, xs0=xs0: (lambda e: e.tensor_tensor(out=H4(hp, xs0), in0=ps[B(hp, 2 + h)][:], in1=mask4[:], op=ALU.mult)),
                    lambda hp, h=h: [psb[B(hp, 2 + h)], b_c], lambda hp, xs0=xs0: [bH[hp][xs0 + i] for i in range(4)])
                add("dve", lambda hp, h=h: (lambda e: e.tensor_tensor(out=H(hp, 28 + h), in0=ps[B(hp, h)][:, 0:128], in1=maskLT[:], op=ALU.mult)),
                    lambda hp, h=h: [psb[B(hp, h)], b_c], lambda hp, h=h: [bH[hp][28 + h]])
            seg("ew")
            TallV = lambda hp: Ht[hp][:, 20:24, :]
            bTall = lambda hp: [bH[hp][20 + i] for i in range(4)]
            XL = lambda hp, h, lvl: H(hp, 42 + 14 * h + lvl)
            bXL = lambda hp, h: [bH[hp][42 + 14 * h + i] for i in range(7)]
            add("dve", lambda hp: (lambda e: e.tensor_tensor(out=Ht[hp][:, 35:63:14, :], in0=Ht[hp][:, 12:20:4, :],
                                                           in1=mlev[:, 0, 0:128].unsqueeze(1).to_broadcast([128, 2, 128]), op=ALU.mult)),
                lambda hp: [bH[hp][12], bH[hp][16], b_c], lambda hp: [bH[hp][35], bH[hp][49]])
            add("dve", lambda hp: (lambda e: e.tensor_tensor(
                out=Ht[hp][:, 42:70, :].rearrange("p (h l) c -> p h l c", l=14)[:, :, 0:7, :],
                in0=Ht[hp][:, 28:30, :].unsqueeze(2).to_broadcast([128, 2, 7, 128]),
                in1=mlev[:, :, 128:256].unsqueeze(1).to_broadcast([128, 2, 7, 128]), op=ALU.mult)),
                lambda hp: [bH[hp][28], bH[hp][29], b_c], lambda hp: bXL(hp, 0) + bXL(hp, 1))
            add("dve", lambda hp: (lambda e: e.tensor_tensor(out=TallV(hp), in0=Ht[hp][:, 35:63:7, :], in1=ident4[:], op=ALU.add)),
                lambda hp: [b_c, bH[hp][35], bH[hp][49]] + bXL(hp, 0) + bXL(hp, 1), lambda hp: bTall(hp))
            for lvl in range(1, 7):
                seg("inv")
                for h in (0, 1):
                    add("pe", lambda hp, h=h, lvl=lvl: (lambda e: e.matmul(ps[B(hp, 2)][:, h * 128:(h + 1) * 128], lhsT=XL(hp, h, lvl), rhs=H(hp, 20 + 2 * h), start=True, stop=True)),
                        lambda hp, h=h: bXL(hp, h) + [bH[hp][20 + 2 * h]], lambda hp: [psb[B(hp, 2)]])
                add("act", lambda hp: (lambda e: e.activation(out=Ht[hp][:, 24:26, :], in_=ps[B(hp, 2)][:, 0:256].rearrange("p (j c) -> p j c", c=128), func=AF.Copy)),
                    lambda hp: [psb[B(hp, 2)]], lambda hp: [bH[hp][24], bH[hp][25]])
                seg("inv")
                add("pe", lambda hp: (lambda e: e.matmul(ps[B(hp, 3)][:], lhsT=identB[:], rhs=Ht[hp][:, 20:24, :], start=True, stop=False)),
                    lambda hp: bTall(hp) + [b_c], lambda hp: [psb[B(hp, 3)]])
                for h in (0, 1):
                    add("pe", lambda hp, h=h: (lambda e: e.matmul(ps[B(hp, 3)][:, (2 * h) * 128:(2 * h + 1) * 128], lhsT=H(hp, 21 + 2 * h), rhs=H(hp, 24 + h), start=False, stop=False)),
                        lambda hp, h=h: [bH[hp][21 + 2 * h], bH[hp][24 + h]], lambda hp: [psb[B(hp, 3)]])
                    add("pe", lambda hp, h=h: (lambda e: e.matmul(ps[B(hp, 3)][:, (2 * h + 1) * 128:(2 * h + 2) * 128], lhsT=H(hp, 24 + h), rhs=H(hp, 21 + 2 * h), start=False, stop=(h == 1))),
                        lambda hp, h=h: [bH[hp][21 + 2 * h], bH[hp][24 + h]], lambda hp: [psb[B(hp, 3)]])
                if lvl % 2 == 1:
                    add("dve", lambda hp: (lambda e: e.tensor_copy(out=TallV(hp), in_=ps[B(hp, 3)][:].rearrange("p (j c) -> p j c", c=128))),
                        lambda hp: [psb[B(hp, 3)]], lambda hp: bTall(hp))
                else:
                    add("act", lambda hp: (lambda e: e.activation(out=TallV(hp), in_=ps[B(hp, 3)][:].rearrange("p (j c) -> p j c", c=128), func=AF.Copy)),
                        lambda hp: [psb[B(hp, 3)]], lambda hp: bTall(hp))
            seg()
            TTs = lambda hp, h: H(hp, 20 + 2 * h)
            bTT = lambda hp, h: bH[hp][20 + 2 * h]
            for h in (0, 1):
                hs = slice(64 * h, 64 * h + 64)
                add("pe", lambda hp, h=h, hs=hs: (lambda e: e.matmul(ps[B(hp, 0)][:, 64 * h:64 * h + 64], lhsT=H(hp, 14 + 4 * h), rhs=Ht[hp][:, 9, hs], start=True, stop=True)),
                    lambda hp, h=h: [bH[hp][14 + 4 * h], bH[hp][9]], lambda hp: [psb[B(hp, 0)]])
            add("act", lambda hp: (lambda e: e.activation(out=H(hp, 32), in_=ps[B(hp, 0)][:, 0:128], func=AF.Copy)),
                lambda hp: [psb[B(hp, 0)]], lambda hp: [bH[hp][32]])
            seg()
            for h in (0, 1):
                hs = slice(64 * h, 64 * h + 64)
                add("pe", lambda hp, h=h, hs=hs: (lambda e: e.matmul(ps[B(hp, 0)][hs, 256:384], lhsT=Ht[hp][:, 8, hs], rhs=TTs(hp, h), start=True, stop=True)),
                    lambda hp, h=h: [bTT(hp, h), bH[hp][8]], lambda hp: [psb[B(hp, 0)]])
            add("act", lambda hp: (lambda e: e.activation(out=H(hp, 33), in_=ps[B(hp, 0)][:, 256:384], func=AF.Copy)),
                lambda hp: [psb[B(hp, 0)]], lambda hp: [bH[hp][33]])
            seg("pm")
            add("pe", lambda hp: (lambda e: e.matmul(ps[B(hp, 1)][:, 0:128], lhsT=H(hp, 33), rhs=Sbd[:, hp, :], start=True, stop=False)),
                lambda hp: [bH[hp][33], b_S[hp]], lambda hp: [psb[B(hp, 1)]])
            for h in (0, 1):
                hs = slice(64 * h, 64 * h + 64)
                add("pe", lambda hp, h=h, hs=hs: (lambda e: e.matmul(ps[B(hp, 1)][:, 64 * h:64 * h + 64], lhsT=TTs(hp, h), rhs=Ht[hp][:, 32, hs], start=False, stop=(h == 1))),
                    lambda hp, h=h: [bTT(hp, h), bH[hp][32]], lambda hp: [psb[B(hp, 1)]])
            add("dve", lambda hp: (lambda e: e.tensor_copy(out=H(hp, 34), in_=ps[B(hp, 1)][:, 0:128])),
                lambda hp: [psb[B(hp, 1)]], lambda hp: [bH[hp][34]])
            add("pe", lambda hp: (lambda e: e.matmul(ps[B(hp, 1)][:, 128:256], lhsT=Sbd[:, hp, :], rhs=H(hp, 3), start=True, stop=False)),
                lambda hp: [b_S[hp], bH[hp][3]], lambda hp: [psb[B(hp, 1)]])
            for h in (0, 1):
                hs = slice(64 * h, 64 * h + 64)
                add("pe", lambda hp, h=h, hs=hs: (lambda e: e.matmul(ps[B(hp, 1)][hs, 128:256], lhsT=Ht[hp][:, 34, hs], rhs=H(hp, 13 + 4 * h), start=False, stop=False)),
                    lambda hp, h=h: [bH[hp][34], bH[hp][13 + 4 * h]], lambda hp: [psb[B(hp, 1)]])
                add("pe", lambda hp, h=h, hs=hs: (lambda e: e.matmul(ps[B(hp, 1)][hs, 128:256], lhsT=Ht[hp][:, 9, hs], rhs=H(hp, 15 + 4 * h), start=False, stop=True)),
                    lambda hp, h=h: [bH[hp][9], bH[hp][15 + 4 * h]], lambda hp: [psb[B(hp, 1)]])
            add("pe", lambda hp: (lambda e: e.matmul(ps[B(hp, 1)][:, 256:384], lhsT=H(hp, 10), rhs=H(hp, 9), start=True, stop=False)),
                lambda hp: [bH[hp][10], bH[hp][9]], lambda hp: [psb[B(hp, 1)]])
            add("pe", lambda hp: (lambda e: e.matmul(ps[B(hp, 1)][:, 256:384], lhsT=H(hp, 11), rhs=H(hp, 34), start=False, stop=True)),
                lambda hp: [bH[hp][11], bH[hp][34]], lambda hp: [psb[B(hp, 1)]])
            add("act", lambda hp: (lambda e: e.activation(out=F(hp, 13), in_=ps[B(hp, 1)][:, 128:256], func=AF.Copy)),
                lambda hp: [psb[B(hp, 1)]], lambda hp: [bF[hp][13]])
            add("act", lambda hp: (lambda e: e.activation(out=F(hp, 14), in_=ps[B(hp, 1)][:, 128:256], func=AF.Square)),
                lambda hp: [psb[B(hp, 1)]], lambda hp: [bF[hp][14]])
            for h in (0, 1):
                hs = slice(64 * h, 64 * h + 64)
                add("dve", lambda hp, h=h, hs=hs: (lambda e: e.scalar_tensor_tensor(out=Sbd[hs, hp, hs], in0=Sbd[hs, hp, hs], scalar=Ft[hp][hs, 4, 127:128],
                                                                                 in1=ps[B(hp, 1)][hs, 256 + 64 * h:256 + 64 * h + 64], op0=ALU.mult, op1=ALU.add)),
                    lambda hp: [psb[B(hp, 1)], bF[hp][4]], lambda hp: [b_S[hp]])
            seg()
            add("pe", lambda hp: (lambda e: e.matmul(ps[B(hp, 1)][:, 0:256], lhsT=bonesF[:], rhs=Ft[hp][:, 13:15, :], start=True, stop=True)),
                lambda hp: [bF[hp][13], bF[hp][14], b_c], lambda hp: [psb[B(hp, 1)]])
            add("act", lambda hp: (lambda e: e.activation(out=F(hp, 1), in_=ps[B(hp, 1)][:, 0:128], func=AF.Square)),
                lambda hp: [psb[B(hp, 1)]], lambda hp: [bF[hp][1]])
            add("dve", lambda hp: (lambda e: e.tensor_tensor(out=F(hp, 5), in0=F(hp, 13), in1=ps[B(hp, 1)][:, 0:128], op=ALU.subtract)),
                lambda hp: [psb[B(hp, 1)], bF[hp][13]], lambda hp: [bF[hp][5]])
            add("dve", lambda hp: (lambda e: e.tensor_tensor(out=F(hp, 2), in0=ps[B(hp, 1)][:, 128:256], in1=F(hp, 1), op=ALU.subtract)),
                lambda hp: [psb[B(hp, 1)], bF[hp][1]], lambda hp: [bF[hp][2]])
            seg("ew")
            add("act", lambda hp: (lambda e: e.activation(out=F(hp, 2), in_=F(hp, 2), func=AF.Ln, bias=gneps[:], scale=1.0)),
                lambda hp: [b_c], lambda hp: [bF[hp][2]])
            add("act", lambda hp: (lambda e: e.activation(out=F(hp, 8), in_=F(hp, 2), func=AF.Exp, scale=-0.5)),
                lambda hp: [bF[hp][2]], lambda hp: [bF[hp][8]])
            add("dve", lambda hp: (lambda e: e.tensor_tensor(out=F(hp, 6), in0=F(hp, 5), in1=F(hp, 8), op=ALU.mult)),
                lambda hp: [bF[hp][5], bF[hp][8]], lambda hp: [bF[hp][6]])
            add("dve", lambda hp: (lambda e: e.scalar_tensor_tensor(out=F(hp, 9), in0=F(hp, 6), scalar=pcol("lnx_w", hp), in1=F(hp, 15), op0=ALU.mult, op1=ALU.add)),
                lambda hp: [bF[hp][6], bF[hp][15], b_par], lambda hp: [bF[hp][9]])
            seg()
            add("pe", lambda hp: (lambda e: e.matmul(ps[B(hp, 0)][:, 0:128], lhsT=g2bf[:, cs(hp)], rhs=sgl_g[s2][:, ts], start=True, stop=True)),
                lambda hp: [bg[3], b_c], lambda hp: [psb[B(hp, 0)]])
            add("dve", lambda hp: (lambda e: e.scalar_tensor_tensor(out=ya_g[s2][:, hp, ts], in0=F(hp, 9), scalar=pcol("lnx_b", hp), in1=ps[B(hp, 0)][:, 0:128], op0=ALU.add, op1=ALU.mult)),
                lambda hp: [psb[B(hp, 0)], bF[hp][9], b_par], lambda hp: [b_ya[s2]])

            bounds = sorted(set(segs + [0, len(steps)]))
            nseg = len(bounds) - 1

            def emit_one(eng, fn, rd, wr, hp):
                rec = _Rec()
                fn(hp)(rec)
                P.op(eng, lambda e, c=rec: getattr(e, c.name)(*c.args, **c.kwargs), reads=rd(hp), writes=wr(hp))

            def emit_seg(si, pairs):
                kind = segkind.get(bounds[si], "ps")
                ops = steps[bounds[si]:bounds[si + 1]]
                if kind in ("ew", "pm"):
                    dicts = {hp: {} for hp in pairs}
                    npp = len(pairs)
                    for dwave in range(len(ops) + npp - 1):
                        for pi, hp in enumerate(pairs):
                            st = dwave - pi
                            if 0 <= st < len(ops):
                                (eng, fn, rd, wr) = ops[st]
                                curh[0] = dicts[hp] if kind == "pm" else {}
                                emit_one(eng, fn, rd, wr, hp)
                else:
                    for hp in pairs:
                        curh[0] = {}
                        for (eng, fn, rd, wr) in ops:
                            emit_one(eng, fn, rd, wr, hp)

            def post():
                if n % TPG == TPG - 1:
                    P.dma("sp", YAv[:, :, gi * GT:(gi + 1) * GT], ya_g[s2][:], b_ya[s2], reads=[b_ya[s2]])
                if dbg and n == NTL - 1:
                    P.barrier()
                    DBGF = nc.dram_tensor("DBGF", [128, NF, 128], F32, kind="ExternalOutput").ap()
                    DBGH = nc.dram_tensor("DBGH", [128, NH, 128], BF16, kind="ExternalOutput").ap()
                    DBGS = nc.dram_tensor("DBGS", [128, NP, 128], BF16, kind="ExternalOutput").ap()
                    bd = P.buf("dbgd")
                    P.dma("sp", DBGF[:, :, :], Ft[0][:], bd)
                    P.dma("sp", DBGH[:, :, :], Ht[0][:], bd)
                    P.dma("sp", DBGS[:, :, :], Sbd[:], bd)

            kinds = [segkind.get(bounds[k], 'ps') for k in range(nseg)]
            return dict(pre=pre, post=post, nseg=nseg, emit_seg=emit_seg, kinds=kinds)

        NTL = int(os.environ.get('KNT', S // 128))
        descs = {}

        def get_desc(n):
            if n not in descs:
                descs[n] = tile_body(n)
            return descs[n]

        nseg0 = get_desc(0)["nseg"]
        LAG = 0
        total = NTL * nseg0
        G0, G1 = (0, 1, 2, 3, 4, 5), ()
        for i in range(total + LAG):
            if i < total:
                n, si = divmod(i, nseg0)
                dsc = get_desc(n)
                kinds = dsc["kinds"]
                if kinds[si] == "inv":
                    if si == 0 or kinds[si - 1] != "inv":
                        sj = si
                        while sj < nseg0 and kinds[sj] == "inv":
                            sj += 1
                        ninv = sj - si
                        for dwave in range(ninv + len(G0) - 1):
                            for pi, hp in enumerate(G0):
                                st = dwave - pi
                                if 0 <= st < ninv:
                                    dsc["emit_seg"](si + st, (hp,))
                else:
                    dsc["emit_seg"](si, G0)
            j = i - LAG
            if j >= 0:
                n, si = divmod(j, nseg0)
                dsc = get_desc(n)
                if si == 0:
                    dsc["pre"]()
                if G1:
                    dsc["emit_seg"](si, G1)
                if si == nseg0 - 1:
                    dsc["post"]()
                    if n - 1 in descs:
                        del descs[n - 1]

        P.barrier()
```

```python
import math
from contextlib import ExitStack
import numpy as np
import ml_dtypes
import concourse.bass as bass
import concourse.mybir as mybir
from concourse.bass_utils import run_bass_kernel_spmd

F32 = mybir.dt.float32
BF16 = mybir.dt.bfloat16
AF = mybir.ActivationFunctionType
ALU = mybir.AluOpType

D = 1024
S = 4096
NC8 = 8
INW = 6912
DFF = 2816
TT = 512
NT = S // TT

PC = {}
_off = 0
for _n, _w in [("c", 8), ("b_ada", 48), ("g_pre_mix", 8), ("g_post_mix", 8), ("g_pre_ffn", 8),
               ("g_post_ffn", 8), ("mu", 20), ("w0", 6), ("a0", 6), ("k_k", 6), ("k_a", 6),
               ("r_k", 6), ("lnx_w", 6), ("lnx_b", 6), ("inv_freq", 1)]:
    PC[_n] = (_off, _w)
    _off += _w
NPAR = _off


class _Rec:
    def __init__(self):
        self.name = None
        self.args = ()
        self.kwargs = {}

    def __getattr__(self, name):
        def f(*a, **k):
            self.name, self.args, self.kwargs = name, a, k
            return self
        return f


class Buf:
    __slots__ = ("name", "w", "rs", "sem", "semval", "const", "excl")

    def __init__(self, name, const=False):
        self.excl = False
        self.name = name
        self.w = None
        self.rs = {}
        self.sem = None
        self.semval = 0
        self.const = const


class Prog:
    ENGS = ["pe", "act", "dve", "pool", "sp"]

    def __init__(self, nc):
        self.nc = nc
        self.streams = {e: [] for e in self.ENGS}
        self.cnt = {e: 0 for e in self.ENGS}
        self.sems = {e: nc.alloc_semaphore("s_" + e) for e in self.ENGS}
        self.seen = {e: {} for e in self.ENGS}
        self.dma_owners = []
        self.nbuf = 0

    def buf(self, name="b", const=False):
        self.nbuf += 1
        return Buf(f"{name}{self.nbuf}", const)

    def _deps(self, eng, reads, writes):
        evs = []
        for b in reads:
            if b.w is not None:
                evs.append(b.w)
        for b in writes:
            if b.w is not None:
                evs.append(b.w)
            evs.extend(b.rs.values())
        waits = {}
        seen = self.seen[eng]
        for (key, semh, val) in evs:
            if key == "pe" and eng == "pe":
                continue
            if seen.get(key, 0) >= val:
                continue
            if key in waits and waits[key][1] >= val:
                continue
            waits[key] = (semh, val)
        for key, (semh, val) in waits.items():
            seen[key] = val
        return list(waits.values())

    def _record(self, ev, reads, writes):
        for b in reads:
            if not b.const:
                b.rs[ev[0]] = ev
        for b in writes:
            b.w = ev
            b.rs = {}

    def op(self, eng, fn, reads=(), writes=()):
        if any(b.excl for b in reads):
            writes = list(writes) + [b for b in reads if b.excl]
            reads = [b for b in reads if not b.excl]
        waits = self._deps(eng, reads, writes)
        self.cnt[eng] += 1
        ev = (eng, self.sems[eng], self.cnt[eng])
        self.streams[eng].append((waits, fn, self.sems[eng], 1))
        self._record(ev, reads, writes)

    def dma(self, q, out, in_, owner, reads=(), writes=(), **kw):
        waits = self._deps(q, reads, writes)
        if owner.sem is None:
            owner.sem = self.nc.alloc_semaphore("d_" + owner.name)
            self.dma_owners.append(owner)
        owner.semval += 16
        ev = ("dma_" + owner.name, owner.sem, owner.semval)
        self.streams[q].append((waits, lambda e: e.dma_start(out=out, in_=in_, **kw), owner.sem, 16))
        self._record(ev, reads, writes)

    def barrier(self):
        for e in self.ENGS:
            waits = []
            for f in self.ENGS:
                if f != e and self.cnt[f] > self.seen[e].get(f, 0):
                    waits.append((self.sems[f], self.cnt[f]))
                    self.seen[e][f] = self.cnt[f]
            for o in self.dma_owners:
                key = "dma_" + o.name
                if o.semval > self.seen[e].get(key, 0):
                    waits.append((o.sem, o.semval))
                    self.seen[e][key] = o.semval
            if waits:
                self.streams[e].append((waits, None, None, 0))

    def emit(self):
        nc = self.nc
        P = self

        def run(name, e):
            for waits, fn, semh, inc in P.streams[name]:
                for (s, v) in waits:
                    e.wait_ge(s, v)
                if fn is not None:
                    ins = fn(e)
                    ins.then_inc(semh, inc)

        with nc.Block() as block:
            @block.tensor
            def _(e):
                run("pe", e)

            @block.scalar
            def _(e):
                run("act", e)

            @block.vector
            def _(e):
                run("dve", e)

            @block.gpsimd
            def _(e):
                run("pool", e)

            @block.sync
            def _(e):
                run("sp", e)


def build_nc(stage=99, dbg=False, inject=False):
    nc = bass.Bass("TRN2", target_bir_lowering=False)
    P = Prog(nc)
    dram_in = lambda name, shape: nc.dram_tensor(name, shape, F32, kind="ExternalInput").ap()
    xT = dram_in("xT", [D, S])
    params = dram_in("params", [128, NPAR])
    w_ada = dram_in("w_ada", [D, 6 * D])
    w_in = dram_in("w_in", [D, INW])
    w2 = dram_in("w2", [64, 768])
    a2 = dram_in("a2", [64, 768])
    g2 = dram_in("g2", [128, 768])
    w0row = dram_in("w0row", [1, 768])
    tmasks = dram_in("tmasks", [128, 7 * 512])
    w_a = dram_in("w_a", [768, D])
    w_b = dram_in("w_b", [256, D])
    w_out = dram_in("w_out", [D, D])
    w_ffn_in = dram_in("w_ffn_in", [D, 2 * DFF])
    w_ffn_out = dram_in("w_ffn_out", [DFF, D])
    outT = nc.dram_tensor("outT", [D, S], F32, kind="ExternalOutput").ap()
    okind = "ExternalOutput" if dbg else "Internal"
    PT = nc.dram_tensor("PT", [INW, S], BF16, kind=okind).ap()
    YA = nc.dram_tensor("YA", [768, S], BF16, kind=("ExternalInput" if inject else okind)).ap()
    OT = nc.dram_tensor("OT", [256, S], BF16, kind=okind).ap()
    X1 = nc.dram_tensor("X1", [D, S], F32, kind=okind).ap()
    WFI = nc.dram_tensor("WFI_bf", [D, 2 * DFF], BF16, kind="Internal").ap()
    WFO = nc.dram_tensor("WFO_bf", [DFF, D], BF16, kind="Internal").ap()

    es_all = ExitStack()
    sb = lambda es, name, shape, dt: es.enter_context(nc.sbuf_tensor(name, shape, dt))
    ps = [es_all.enter_context(nc.psum_tensor(f"ps{i}", [128, 512], F32)) for i in range(8)]
    psb = [P.buf(f"ps{i}") for i in range(8)]
    for b in psb:
        b.excl = True

    par = sb(es_all, "par", [128, NPAR], F32)
    modv = sb(es_all, "modv", [128, 48], F32)
    coef = sb(es_all, "coef", [128, 48], F32)
    omm = sb(es_all, "omm", [128, 20], F32)
    ones_bf = sb(es_all, "ones_bf", [128, 128], BF16)
    eps_t = sb(es_all, "eps_t", [128, 1], F32)
    b_par, b_modv, b_coef, b_const = P.buf("par"), P.buf("modv"), P.buf("coef"), P.buf("const")
    pcol = lambda n, i=None: (par[:, PC[n][0]:PC[n][0] + PC[n][1]] if i is None
                              else par[:, PC[n][0] + i:PC[n][0] + i + 1])

    P.dma("sp", par[:], params[:, :], b_par, writes=[b_par])
    P.op("pool", lambda e: e.memset(ones_bf[:], 1.0), writes=[b_const])
    P.op("pool", lambda e: e.memset(eps_t[:], 1e-6), writes=[b_const])

    with ExitStack() as es:
        sc = sb(es, "sc", [128, 8], F32)
        b_sc = P.buf("sc")
        P.op("act", lambda e: e.activation(out=sc[:], in_=pcol("c"), func=AF.Silu), reads=[b_par], writes=[b_sc])
        NB = 768
        wa_t = [sb(es, f"wa_t{i}", [128, 8, NB], F32) for i in range(2)]
        b_wa = [P.buf("wa") for _ in range(2)]
        w_ada_v = w_ada.rearrange("(kc p) n -> p kc n", p=128)
        for jb in range(8):
            t, bt = wa_t[jb % 2], b_wa[jb % 2]
            P.dma("sp", t[:], w_ada_v[:, :, jb * NB:(jb + 1) * NB], bt, writes=[bt])
            for j in range(6):
                jj = jb * 6 + j
                for kc in range(8):
                    P.op("pe", lambda e, t=t, j=j, kc=kc, jj=jj: e.matmul(
                        ps[6][:, jj:jj + 1], lhsT=t[:, kc, j * 128:(j + 1) * 128], rhs=sc[:, kc:kc + 1],
                        start=(kc == 0), stop=(kc == 7)), reads=[bt, b_sc], writes=[psb[6]])
        P.op("dve", lambda e: e.tensor_tensor(out=modv[:], in0=ps[6][:, 0:48], in1=pcol("b_ada"), op=ALU.add),
             reads=[psb[6], b_par], writes=[b_modv])
        P.op("dve", lambda e: e.scalar_tensor_tensor(out=coef[:, 0:8], in0=modv[:, 8:16], scalar=1.0, in1=pcol("g_pre_mix"),
                                                     op0=ALU.add, op1=ALU.mult), reads=[b_modv, b_par], writes=[b_coef])
        P.op("dve", lambda e: e.tensor_tensor(out=coef[:, 8:16], in0=modv[:, 16:24], in1=pcol("g_post_mix"), op=ALU.mult),
             reads=[b_modv, b_par], writes=[b_coef])
        P.op("dve", lambda e: e.scalar_tensor_tensor(out=coef[:, 16:24], in0=modv[:, 32:40], scalar=1.0, in1=pcol("g_pre_ffn"),
                                                     op0=ALU.add, op1=ALU.mult), reads=[b_modv, b_par], writes=[b_coef])
        P.op("dve", lambda e: e.tensor_tensor(out=coef[:, 24:32], in0=modv[:, 40:48], in1=pcol("g_post_ffn"), op=ALU.mult),
             reads=[b_modv, b_par], writes=[b_coef])
        P.op("dve", lambda e: e.tensor_scalar(out=omm[:], in0=pcol("mu"), scalar1=-1.0, scalar2=1.0, op0=ALU.mult, op1=ALU.add),
             reads=[b_par], writes=[b_coef])
        P.barrier()
    A_m = lambda kc: coef[:, kc:kc + 1]
    B_m = lambda kc: modv[:, kc:kc + 1]
    GM = lambda kc: coef[:, 8 + kc:9 + kc]
    A_f = lambda kc: coef[:, 16 + kc:17 + kc]
    B_f = lambda kc: modv[:, 24 + kc:25 + kc]
    GF = lambda kc: coef[:, 24 + kc:25 + kc]

    def rms_modulate(es, src_tile, b_src, dst, b_dst, dst_sl, A, Bc, tmpbufs, b_tmp, sqt, b_sq, rs, b_rs, psi):
        P.op("act", lambda e: e.activation(out=sqt[:], in_=src_tile[:], func=AF.Square), reads=[b_src], writes=[b_sq])
        for kc in range(8):
            P.op("pe", lambda e, kc=kc: e.matmul(ps[psi][:], lhsT=ones_bf[:], rhs=sqt[:, kc, :], start=(kc == 0), stop=(kc == 7)),
                 reads=[b_sq, b_const], writes=[psb[psi]])
        P.op("act", lambda e: e.activation(out=rs[:], in_=ps[psi][:], func=AF.Ln, bias=eps_t[:], scale=1.0 / D),
             reads=[psb[psi], b_const], writes=[b_rs])
        P.op("act", lambda e: e.activation(out=rs[:], in_=rs[:], func=AF.Exp, scale=-0.5), reads=[b_rs], writes=[b_rs])
        for kc in range(8):
            tb, btb = tmpbufs[kc % 2], b_tmp[kc % 2]
            P.op("dve", lambda e, kc=kc, tb=tb: e.scalar_tensor_tensor(out=tb[:], in0=src_tile[:, kc, :], scalar=A(kc), in1=rs[:],
                                                                     op0=ALU.mult, op1=ALU.mult),
                 reads=[b_src, b_rs, b_coef], writes=[btb])
            P.op("act", lambda e, kc=kc, tb=tb: e.activation(out=dst[:, kc, dst_sl], in_=tb[:], func=AF.Identity, bias=Bc(kc), scale=1.0),
                 reads=[btb, b_modv], writes=[b_dst])

    with ExitStack() as esA:
        hT = sb(esA, "hT", [128, 8, S], BF16)
        b_hT = [P.buf("hT") for _ in range(NT)]
        xT_v = xT.rearrange("(kc p) t -> p kc t", p=128)
        with ExitStack() as es:
            xt = [sb(es, f"xt{i}", [128, 8, TT], F32) for i in range(2)]
            b_xt = [P.buf("xt") for _ in range(2)]
            sqt = sb(es, "sqt", [128, 8, TT], BF16)
            b_sq = P.buf("sq")
            rs = sb(es, "rs", [128, TT], F32)
            b_rs = P.buf("rs")
            tmpb = [sb(es, f"tmpb{i}", [128, TT], F32) for i in range(2)]
            b_tmp = [P.buf("tmp") for _ in range(2)]
            for tt in range(NT):
                P.dma("sp", xt[tt % 2][:], xT_v[:, :, tt * TT:(tt + 1) * TT], b_xt[tt % 2], writes=[b_xt[tt % 2]])
                rms_modulate(es, xt[tt % 2], b_xt[tt % 2], hT, b_hT[tt], slice(tt * TT, (tt + 1) * TT), A_m, B_m,
                             tmpb, b_tmp, sqt, b_sq, rs, b_rs, 5)
            P.barrier()
        if stage >= 1:
            phaseA_proj(nc, P, esA, sb, ps, psb, hT, b_hT, w_in, PT, par, pcol, omm, b_par, b_coef)
        P.barrier()

    b_out = P.buf("outdma")
    b_wpre = P.buf("wpre")

    def precast_ffn():
        for k in range(8):
            P.dma("pool", WFI[k * 128:(k + 1) * 128, :].rearrange("p (a b) -> p a b", b=512),
                  w_ffn_in[k * 128:(k + 1) * 128, :].rearrange("p (a b) -> p a b", b=512), b_wpre, writes=[b_wpre])
        for k in range(DFF // 128):
            P.dma("pool", WFO[k * 128:(k + 1) * 128, :].rearrange("p (a b) -> p a b", b=512),
                  w_ffn_out[k * 128:(k + 1) * 128, :].rearrange("p (a b) -> p a b", b=512), b_wpre, writes=[b_wpre])

    if stage >= 5 and (stage < 2 or inject):
        precast_ffn()
    if stage >= 2 and not inject:
        phaseB_rwkv(nc, P, sb, ps, psb, PT, YA, par, pcol, b_par, w2, a2, g2, w0row, tmasks, dbg=dbg, after_consts=(precast_ffn if stage >= 5 else None))
    esCD = ExitStack()
    d1w = None
    if stage >= 4:
        stg32 = [sb(esCD, f"stg32_{i}", [128, 1024], F32) for i in range(2)]
        b_stg32 = [P.buf("stg32") for _ in range(2)]
        d1w = (load_weight_bf16(P, sb, esCD, "wa_bf", w_a, 6, D, stg32, b_stg32, eng="act"),
               load_weight_bf16(P, sb, esCD, "wb_bf", w_b, 2, D, stg32, b_stg32, eng="act"),
               load_weight_bf16(P, sb, esCD, "wo_bf", w_out, 8, D, stg32, b_stg32, eng="act"))
    if stage >= 3:
        phaseC_attn(nc, P, sb, ps, psb, PT, OT)
    if stage >= 4:
        phaseD1(nc, P, sb, ps, psb, PT, YA, OT, X1, xT, d1w, GM, ones_bf, eps_t, b_const, b_coef)
    esCD.close()
    if stage >= 5:
        phaseD2(nc, P, sb, ps, psb, X1, outT, WFI, WFO, b_wpre, A_f, B_f, GF, ones_bf, eps_t, b_const, b_coef, b_modv, b_out)
    if stage < 5:
        with ExitStack() as es:
            z = sb(es, "zt", [128, 8, 64], F32)
            bz = P.buf("z")
            P.op("pool", lambda e: e.memset(z[:], 0.0), writes=[bz])
            P.op("dve", lambda e: e.tensor_copy(out=z[:, 0, 0:48], in_=modv[:]), reads=[bz, b_modv], writes=[bz])
            P.dma("sp", outT.rearrange("(kc p) t -> p kc t", p=128)[:, :, 0:64], z[:], b_out, reads=[bz])
    P.barrier()
    P.emit()
    es_all.close()
    return nc


def phaseA_proj(nc, P, esA, sb, ps, psb, hT, b_hT, w_in, PT, par, pcol, omm, b_par, b_coef):
    with ExitStack() as es:
        NW = 3
        wstg = [sb(es, f"wstg{i}", [128, 8, 512], F32) for i in range(2)]
        b_wstg = [P.buf("wstg") for _ in range(2)]
        wbf = [sb(es, f"wbf{i}", [128, 8, 128], BF16) for i in range(NW)]
        b_wbf = [P.buf("wbf") for _ in range(NW)]
        stg = [sb(es, f"stg{i}", [128, S], BF16) for i in range(3)]
        b_stg = [P.buf("stg") for _ in range(3)]
        mup = [sb(es, f"mup{i}", [128, S + 1], F32) for i in range(2)]
        b_mup = [[P.buf("mup") for _ in range(NT + 1)] for _ in range(2)]
        cosT = sb(es, "cosT", [128, S], F32)
        sinT = sb(es, "sinT", [128, S], F32)
        perm = sb(es, "perm", [128, 128], BF16)
        ident = sb(es, "ident", [128, 128], BF16)
        qraw = [sb(es, f"qraw{i}", [128, TT], BF16) for i in range(2)]
        b_qraw = [P.buf("qraw") for _ in range(2)]
        t1 = [sb(es, f"t1_{i}", [128, TT], F32) for i in range(2)]
        b_t1 = [P.buf("t1") for _ in range(2)]
        t2 = [sb(es, f"t2_{i}", [128, TT], F32) for i in range(2)]
        b_t2 = [P.buf("t2") for _ in range(2)]
        sgn = sb(es, "sgn", [128, 1], F32)
        pi_t = sb(es, "pi_t", [128, 1], F32)
        b_tab = P.buf("tab")
        P.op("pool", lambda e: e.memset(ident[:], 1.0), writes=[b_tab])
        P.op("pool", lambda e: e.affine_select(out=ident[:], in_=ident[:], pattern=[[-1, 128]], compare_op=ALU.is_equal,
                                               fill=0.0, base=0, channel_multiplier=1), reads=[b_tab], writes=[b_tab])
        for h0 in (0, 64):
            P.op("pool", lambda e, h0=h0: e.tensor_copy(out=perm[:, h0:h0 + 32], in_=ident[:, h0 + 32:h0 + 64]), reads=[b_tab], writes=[b_tab])
            P.op("pool", lambda e, h0=h0: e.tensor_copy(out=perm[:, h0 + 32:h0 + 64], in_=ident[:, h0:h0 + 32]), reads=[b_tab], writes=[b_tab])
        for q4 in range(4):
            P.op("pool", lambda e, q4=q4: e.memset(sgn[q4 * 32:(q4 + 1) * 32, :], -1.0 if q4 % 2 == 0 else 1.0), writes=[b_tab])
        P.op("pool", lambda e: e.memset(pi_t[:], -math.pi), writes=[b_tab])
        for m in (0, 1):
            P.op("pool", lambda e, m=m: e.memset(mup[m][:, 0:1], 0.0), writes=[b_mup[m][0]])
        with ExitStack() as es2:
            ang_t = wstg[0]
            ki = mup[1][:, 1:S + 1].bitcast(mybir.dt.int32)
            kf = mup[0][:, 1:S + 1]
            P.op("pool", lambda e: e.iota(ang_t[:].rearrange("p a b -> p (a b)"), pattern=[[1, S]], base=0, channel_multiplier=0, allow_small_or_imprecise_dtypes=True),
                 reads=[b_tab], writes=[b_tab])
            P.op("dve", lambda e: e.tensor_scalar(out=ang_t[:].rearrange("p a b -> p (a b)"), in0=ang_t[:].rearrange("p a b -> p (a b)"), scalar1=pcol("inv_freq"), scalar2=1.0 / (2 * math.pi),
                                                  op0=ALU.mult, op1=ALU.mult), reads=[b_tab, b_par], writes=[b_tab])
            for (tab, addc) in ((sinT, 0.0), (cosT, 0.25)):
                P.op("dve", lambda e, tab=tab, addc=addc: e.tensor_scalar(out=tab[:], in0=ang_t[:].rearrange("p a b -> p (a b)"), scalar1=addc, scalar2=None, op0=ALU.add),
                     reads=[b_tab], writes=[b_tab])
                P.op("dve", lambda e, tab=tab: e.tensor_copy(out=ki, in_=tab[:]), reads=[b_tab], writes=[b_tab])
                P.op("dve", lambda e, tab=tab: e.tensor_copy(out=kf, in_=ki), reads=[b_tab], writes=[b_tab])
                P.op("dve", lambda e, tab=tab: e.tensor_tensor(out=tab[:], in0=tab[:], in1=kf, op=ALU.subtract), reads=[b_tab], writes=[b_tab])
                P.op("dve", lambda e, tab=tab: e.tensor_scalar(out=kf, in0=tab[:], scalar1=0.5, scalar2=None, op0=ALU.is_gt), reads=[b_tab], writes=[b_tab])
                P.op("dve", lambda e, tab=tab: e.tensor_tensor(out=tab[:], in0=tab[:], in1=kf, op=ALU.subtract), reads=[b_tab], writes=[b_tab])
                P.op("dve", lambda e, tab=tab: e.tensor_scalar(out=kf, in0=tab[:], scalar1=-0.5, scalar2=None, op0=ALU.is_lt), reads=[b_tab], writes=[b_tab])
                P.op("dve", lambda e, tab=tab: e.tensor_tensor(out=tab[:], in0=tab[:], in1=kf, op=ALU.add), reads=[b_tab], writes=[b_tab])
                P.op("act", lambda e, tab=tab: e.activation(out=tab[:], in_=tab[:], func=AF.Sin, scale=2 * math.pi - 2e-6), reads=[b_tab], writes=[b_tab])
            P.op("dve", lambda e: e.tensor_scalar(out=sinT[:], in0=sinT[:], scalar1=sgn[:], scalar2=None, op0=ALU.mult), reads=[b_tab], writes=[b_tab])
            P.barrier()
        b_tab.const = True

        w_in_v = w_in.rearrange("(kc p) n -> p kc n", p=128)
        NCH = INW // 128
        import os
        order = [int(v) for v in os.environ['KCH'].split(',')] if 'KCH' in os.environ else list(range(NCH))
        NCH = len(order)

        WG = 4

        def load_w(i):
            if i % WG == 0:
                gsl = (i // WG) % 2
                c0 = order[i] * 128
                ncol = 128 * min(WG, NCH - i)
                P.dma("sp", wstg[gsl][:, :, 0:ncol], w_in_v[:, :, c0:c0 + ncol], b_wstg[gsl], writes=[b_wstg[gsl]])
            gsl = (i // WG) % 2
            s_ = i % NW
            j = i % WG
            P.op("pool", lambda e, s_=s_, gsl=gsl, j=j: e.tensor_copy(out=wbf[s_][:], in_=wstg[gsl][:, :, j * 128:(j + 1) * 128]),
                 reads=[b_wstg[gsl]], writes=[b_wbf[s_]])

        load_w(0)
        if NCH > 1:
            load_w(1)
        pidx = 0
        ridx = 0
        for i, cc in enumerate(order):
            if i + 2 < NCH:
                load_w(i + 2)
            s = i % NW
            so = i % 3
            sg, bsg = stg[so], b_stg[so]
            mu_i = cc if cc < 20 else None
            mm = i % 2
            for tt in range(NT):
                pi = pidx % 5
                pidx += 1
                tsl = slice(tt * TT, (tt + 1) * TT)
                for kc in range(8):
                    P.op("pe", lambda e, pi=pi, s=s, kc=kc, tsl=tsl: e.matmul(ps[pi][:], lhsT=wbf[s][:, kc, :], rhs=hT[:, kc, tsl],
                                                                          start=(kc == 0), stop=(kc == 7)),
                         reads=[b_wbf[s], b_hT[tt]], writes=[psb[pi]])
                if cc < 20:
                    mcol = pcol("mu", cc)
                    ocol = omm[:, cc:cc + 1]
                    P.op("act", lambda e, pi=pi, mm=mm, tt=tt, mcol=mcol: e.activation(
                        out=mup[mm][:, 1 + tt * TT:1 + (tt + 1) * TT], in_=ps[pi][:], func=AF.Copy, scale=mcol),
                        reads=[psb[pi], b_par], writes=[b_mup[mm][tt + 1]])
                    if cc < 18:
                        P.op("dve", lambda e, pi=pi, mm=mm, tt=tt, ocol=ocol, sg=sg, tsl=tsl: e.scalar_tensor_tensor(
                            out=sg[:, tsl], in0=ps[pi][:], scalar=ocol, in1=mup[mm][:, tt * TT:(tt + 1) * TT], op0=ALU.mult, op1=ALU.add),
                            reads=[psb[pi], b_coef, b_mup[mm][tt], b_mup[mm][tt + 1]], writes=[bsg])
                    else:
                        tb, btb = t1[tt % 2], b_t1[tt % 2]
                        P.op("dve", lambda e, pi=pi, mm=mm, tt=tt, ocol=ocol, tb=tb: e.scalar_tensor_tensor(
                            out=tb[:], in0=ps[pi][:], scalar=ocol, in1=mup[mm][:, tt * TT:(tt + 1) * TT], op0=ALU.mult, op1=ALU.add),
                            reads=[psb[pi], b_coef, b_mup[mm][tt], b_mup[mm][tt + 1]], writes=[btb])
                        if cc == 18:
                            P.op("act", lambda e, tb=tb, sg=sg, tsl=tsl: e.activation(out=sg[0:64, tsl], in_=tb[0:64, :], func=AF.Tanh),
                                 reads=[btb], writes=[bsg])
                            P.op("act", lambda e, tb=tb, sg=sg, tsl=tsl: e.activation(out=sg[64:128, tsl], in_=tb[64:128, :], func=AF.Copy),
                                 reads=[btb], writes=[bsg])
                        else:
                            P.op("act", lambda e, tb=tb, sg=sg, tsl=tsl: e.activation(out=sg[:, tsl], in_=tb[:], func=AF.Sigmoid),
                                 reads=[btb], writes=[bsg])
                elif cc < 32:
                    ri = ridx % 2
                    ridx += 1
                    qr, bqr = qraw[ri], b_qraw[ri]
                    P.op("act", lambda e, pi=pi, qr=qr: e.activation(out=qr[:], in_=ps[pi][:], func=AF.Copy), reads=[psb[pi]], writes=[bqr])
                    P.op("pe", lambda e, ri=ri, qr=qr: e.matmul(ps[5 + ri][:], lhsT=perm[:], rhs=qr[:], start=True, stop=True),
                         reads=[bqr, b_tab], writes=[psb[5 + ri]])
                    P.op("dve", lambda e, pi=pi, ri=ri, tsl=tsl: e.tensor_tensor(out=t1[ri][:], in0=ps[pi][:], in1=cosT[:, tsl], op=ALU.mult),
                         reads=[psb[pi], b_tab], writes=[b_t1[ri]])
                    P.op("dve", lambda e, ri=ri, tsl=tsl: e.tensor_tensor(out=t2[ri][:], in0=ps[5 + ri][:], in1=sinT[:, tsl], op=ALU.mult),
                         reads=[psb[5 + ri], b_tab], writes=[b_t2[ri]])
                    P.op("dve", lambda e, ri=ri, sg=sg, tsl=tsl: e.tensor_tensor(out=sg[:, tsl], in0=t1[ri][:], in1=t2[ri][:], op=ALU.add),
                         reads=[b_t1[ri], b_t2[ri]], writes=[bsg])
                elif cc < 38:
                    P.op("act", lambda e, pi=pi, sg=sg, tsl=tsl: e.activation(out=sg[:, tsl], in_=ps[pi][:], func=AF.Copy),
                         reads=[psb[pi]], writes=[bsg])
                else:
                    P.op("act", lambda e, pi=pi, sg=sg, tsl=tsl: e.activation(out=sg[:, tsl], in_=ps[pi][:], func=AF.Sigmoid),
                         reads=[psb[pi]], writes=[bsg])
            P.dma("sp", PT[cc * 128:(cc + 1) * 128, :], sg[:], bsg, reads=[bsg])
        P.barrier()


def _pack_params(inp, b):
    cols = np.zeros((128, NPAR), np.float32)

    def put(name, vec):
        o, w = PC[name]
        cols[:, o:o + w] = np.asarray(vec, np.float32).reshape(w, 128).T

    put("c", inp["c"][b])
    put("b_ada", inp["b_ada"][0])
    for n in ("g_pre_mix", "g_post_mix", "g_pre_ffn", "g_post_ffn", "w0", "a0", "k_k", "k_a", "lnx_w", "lnx_b"):
        put(n, inp[n][0])
    put("mu", inp["mu_shift"][0])
    put("r_k", inp["r_k"][0].reshape(-1))
    half = 32
    inv_freq = (10000.0 ** (-np.arange(half, dtype=np.float32) / half)).astype(np.float32)
    cols[:, PC["inv_freq"][0]] = np.tile(inv_freq, 4)
    return cols


def _tri_masks():
    idx = np.arange(128)
    out = np.zeros((128, 7, 4, 128), np.float32)
    for lvl in range(7):
        b = 1 << lvl
        mU = ((idx[:, None] // b) % 2 == 0) & ((idx[None, :] // b) == (idx[:, None] // b) + 1)
        mL = mU.T
        out[:, lvl, 0], out[:, lvl, 1], out[:, lvl, 2], out[:, lvl, 3] = mU, mL, mU, mL
    return np.ascontiguousarray(out.reshape(128, 7 * 512))


def make_in_maps(inp):
    shared = {
        "tmasks": _tri_masks(),
        "w_ada": np.ascontiguousarray(inp["w_ada"][0]), "w_in": np.ascontiguousarray(inp["w_in"][0]),
        "w2": np.ascontiguousarray(inp["w2"][0]), "a2": np.ascontiguousarray(inp["a2"][0]),
        "g2": np.ascontiguousarray(inp["g2"][0]), "w_a": np.ascontiguousarray(inp["w_a"][0]),
        "w_b": np.ascontiguousarray(inp["w_b"][0]), "w_out": np.ascontiguousarray(inp["w_out"][0]),
        "w_ffn_in": np.ascontiguousarray(inp["w_ffn_in"][0]), "w_ffn_out": np.ascontiguousarray(inp["w_ffn_out"][0]),
    }
    maps = []
    for b in range(NC8):
        m = dict(shared)
        m["xT"] = np.ascontiguousarray(np.asarray(inp["x"][b], np.float32).T)
        m["params"] = _pack_params(inp, b)
        m["w0row"] = np.ascontiguousarray(inp["w0"][0].reshape(1, 768))
        maps.append(m)
    return maps


def kernel(**inputs):
    inp = {k: np.asarray(v) for k, v in inputs.items()}
    nc = build_nc()
    in_maps = make_in_maps(inp)
    res = run_bass_kernel_spmd(nc, in_maps, core_ids=list(range(NC8)))
    out = np.stack([np.ascontiguousarray(r["outT"].T) for r in res.results], axis=0)
    return out.astype(np.float32)


def phaseC_attn(nc, P, sb, ps, psb, PT, OT):
    with ExitStack() as es:
        ident = sb(es, "identC", [128, 128], BF16)
        maskT = sb(es, "maskT", [128, 256], BF16)
        ones64 = sb(es, "ones64", [128, 64], BF16)
        b_c = P.buf("constC")
        P.op("pool", lambda e: e.memset(ident[:], 1.0), writes=[b_c])
        P.op("pool", lambda e: e.affine_select(out=ident[:], in_=ident[:], pattern=[[-1, 128]], compare_op=ALU.is_equal,
                                               fill=0.0, base=0, channel_multiplier=1), reads=[b_c], writes=[b_c])
        P.op("pool", lambda e: e.memset(maskT[:], 1.0), reads=[b_c], writes=[b_c])
        P.op("pool", lambda e: e.affine_select(out=maskT[:, 0:128], in_=maskT[:, 0:128], pattern=[[-1, 128]], compare_op=ALU.is_ge,
                                               fill=0.0, base=0, channel_multiplier=1), reads=[b_c], writes=[b_c])
        P.op("pool", lambda e: e.affine_select(out=maskT[:, 128:256], in_=maskT[:, 128:256], pattern=[[1, 128]], compare_op=ALU.is_ge,
                                               fill=0.0, base=0, channel_multiplier=-1), reads=[b_c], writes=[b_c])
        P.op("pool", lambda e: e.memset(ones64[:], 1.0), reads=[b_c], writes=[b_c])
        P.barrier()
        b_c.const = True
        qkv = [[sb(es, f"qkv{i}_{j}", [128, S], BF16) for j in range(3)] for i in range(2)]
        b_qkv = [[P.buf("qkv") for j in range(3)] for i in range(2)]
        vtok = sb(es, "vtok", [128, 32, 128], BF16)
        b_vtok = [P.buf("vtok") for _ in range(8)]
        acc = sb(es, "acc", [128, 2, S], F32)
        b_acc = P.buf("acc")
        NPT = 4
        pT = [sb(es, f"pT{i}", [128, 2, 256], BF16) for i in range(NPT)]
        b_pT = [P.buf("pT") for _ in range(NPT)]
        o_bf = sb(es, "o_bf", [128, S], BF16)
        b_obf = P.buf("obf")
        rec = sb(es, "rec", [128, S], F32)
        b_rec = P.buf("rec")
        ps6b = ps[6][:].bitcast(BF16)

        def load_pair(idx, pp):
            st = idx % 2
            for j, base in enumerate((2560, 3328, 4096)):
                r0 = base + pp * 128
                P.dma("sp", qkv[st][j][:], PT[r0:r0 + 128, :], b_qkv[st][j], writes=[b_qkv[st][j]])

        seq = [(spn, g) for spn in (0, 1) for g in (0, 1, 2)]
        load_pair(0, seq[0][1] * 2 + seq[0][0])
        cnt = 0
        for idx, (spn, g) in enumerate(seq):
            pp = g * 2 + spn
            if idx + 1 < len(seq):
                load_pair(idx + 1, seq[idx + 1][1] * 2 + seq[idx + 1][0])
            st = idx % 2
            d = (1, 4, 16)[g]
            nb = S // d // 128
            qT, kT, vT = qkv[st]
            bq, bk, bv = b_qkv[st]
            view = lambda t: t[:].rearrange("p (n i r) -> p r n i", i=128, r=d)
            qv, kv, vv = view(qT), view(kT), view(vT)
            accv = acc[:].rearrange("p c (n i r) -> p c r n i", i=128, r=d)
            for g4 in range(8):
                for j in range(4):
                    b = g4 * 4 + j
                    r, n = b // nb, b % nb
                    P.op("pe", lambda e, j=j, r=r, n=n, vv=vv: e.transpose(ps6b[:, j * 128:(j + 1) * 128], vv[:, r, n, :], ident[:]),
                         reads=[bv, b_c], writes=[psb[6]])
                P.op("act", lambda e, g4=g4: e.activation(out=vtok[:, g4 * 4:(g4 + 1) * 4, :],
                                                          in_=ps6b[:, 0:512].rearrange("p (j c) -> p j c", c=128), func=AF.Copy),
                     reads=[psb[6]], writes=[b_vtok[g4]])
            def part1(b, cnt_, kv=kv, qv=qv, bk=bk, bq=bq, nb=nb):
                r, n = b // nb, b % nb
                np_ = n - 1 if n > 0 else n
                slot = cnt_ % NPT
                sbk = cnt_ % 3
                for h in (0, 1):
                    hs = slice(64 * h, 64 * h + 64)
                    bank = ((0, 1, 6), (2, 3, 7))[h][sbk]
                    P.op("pe", lambda e, bank=bank, hs=hs, r=r, np_=np_, n=n: e.matmul(
                        ps[bank][:, 0:128], lhsT=kv[hs, r, np_, :], rhs=qv[hs, r, n, :], start=True, stop=True),
                        reads=[bk, bq], writes=[psb[bank]])
                    P.op("pe", lambda e, bank=bank, hs=hs, r=r, n=n: e.matmul(
                        ps[bank][:, 128:256], lhsT=kv[hs, r, n, :], rhs=qv[hs, r, n, :], start=True, stop=True),
                        reads=[bk, bq], writes=[psb[bank]])
                    P.op("act", lambda e, bank=bank, slot=slot, h=h: e.activation(out=pT[slot][:, h, :], in_=ps[bank][:, 0:256],
                                                                                 func=AF.Exp, scale=0.125),
                         reads=[psb[bank]], writes=[b_pT[slot]])
                for h in (0, 1):
                    P.op("dve", lambda e, slot=slot, h=h: e.tensor_tensor(out=pT[slot][:, h, :], in0=pT[slot][:, h, :], in1=maskT[:], op=ALU.mult),
                         reads=[b_pT[slot], b_c], writes=[b_pT[slot]])

            def part2(b, cnt_, accv=accv, g=g, nb=nb):
                r, n = b // nb, b % nb
                bprev = b - 1 if n > 0 else b
                slot = cnt_ % NPT
                ob = 4 + cnt_ % 2
                for h in (0, 1):
                    hs = slice(64 * h, 64 * h + 64)
                    if n > 0:
                        P.op("pe", lambda e, ob=ob, hs=hs, bprev=bprev, slot=slot, h=h: e.matmul(
                            ps[ob][hs, 0:128], lhsT=vtok[:, bprev, hs], rhs=pT[slot][:, h, 0:128], start=True, stop=False),
                            reads=[b_vtok[bprev // 4], b_pT[slot]], writes=[psb[ob]])
                    P.op("pe", lambda e, ob=ob, hs=hs, b=b, slot=slot, h=h, n=n: e.matmul(
                        ps[ob][hs, 0:128], lhsT=vtok[:, b, hs], rhs=pT[slot][:, h, 128:256], start=(n == 0), stop=True),
                        reads=[b_vtok[b // 4], b_pT[slot]], writes=[psb[ob]])
                    if n > 0:
                        P.op("pe", lambda e, ob=ob, hs=hs, slot=slot, h=h: e.matmul(
                            ps[ob][hs, 128:256], lhsT=ones64[:], rhs=pT[slot][:, h, 0:128], start=True, stop=False),
                            reads=[b_c, b_pT[slot]], writes=[psb[ob]])
                    P.op("pe", lambda e, ob=ob, hs=hs, slot=slot, h=h, n=n: e.matmul(
                        ps[ob][hs, 128:256], lhsT=ones64[:], rhs=pT[slot][:, h, 128:256], start=(n == 0), stop=True),
                        reads=[b_c, b_pT[slot]], writes=[psb[ob]])
                src = ps[ob][:, 0:256].rearrange("p (c i) -> p c i", i=128)
                if g == 0:
                    P.op("dve", lambda e, src=src, r=r, n=n: e.tensor_copy(out=accv[:, :, r, n, :], in_=src),
                         reads=[psb[ob]], writes=[b_acc])
                else:
                    P.op("dve", lambda e, src=src, r=r, n=n: e.tensor_tensor(out=accv[:, :, r, n, :], in0=src, in1=accv[:, :, r, n, :], op=ALU.add),
                         reads=[psb[ob], b_acc], writes=[b_acc])

            part1(0, cnt)
            part1(1, cnt + 1)
            for b in range(32):
                if b + 2 < 32:
                    part1(b + 2, cnt + b + 2)
                part2(b, cnt + b)
            cnt += 32
            if g == 2:
                P.op("dve", lambda e: e.reciprocal(out=rec[:], in_=acc[:, 1, :]), reads=[b_acc], writes=[b_rec])
                P.op("pool", lambda e: e.tensor_tensor(out=o_bf[:], in0=acc[:, 0, :], in1=rec[:], op=ALU.mult), reads=[b_acc, b_rec], writes=[b_obf])
                P.dma("sp", OT[spn * 128:(spn + 1) * 128, :], o_bf[:], b_obf, reads=[b_obf])
        P.barrier()


def load_weight_bf16(P, sb, es, name, w_dram, nk, ncols, stg32, b_stg32, cnt0=0, eng="pool"):
    wt = sb(es, name, [128, nk, ncols], BF16)
    bw = P.buf(name)
    for kc in range(nk):
        si = (cnt0 + kc) % len(stg32)
        for c0 in range(0, ncols, 1024):
            c1 = min(ncols, c0 + 1024)
            P.dma("sp", stg32[si][:, 0:c1 - c0], w_dram[kc * 128:(kc + 1) * 128, c0:c1], b_stg32[si], writes=[b_stg32[si]])
            if eng == "act":
                P.op("act", lambda e, si=si, kc=kc, c0=c0, c1=c1: e.activation(out=wt[:, kc, c0:c1], in_=stg32[si][:, 0:c1 - c0], func=AF.Copy),
                     reads=[b_stg32[si]], writes=[bw])
            else:
                P.op(eng, lambda e, si=si, kc=kc, c0=c0, c1=c1: e.tensor_copy(out=wt[:, kc, c0:c1], in_=stg32[si][:, 0:c1 - c0]),
                     reads=[b_stg32[si]], writes=[bw])
            si = (si + 1) % len(stg32)
    return wt, bw


def phaseD1(nc, P, sb, ps, psb, PT, YA, OT, X1, xT, d1w, GM, ones_bf, eps_t, b_const, b_coef):
    with ExitStack() as es:
        (wa, b_wa), (wb, b_wb), (wo, b_wo) = d1w
        ya_t = [sb(es, f"ya_t{i}", [128, 6, TT], BF16) for i in range(2)]
        o_t = [sb(es, f"o_t{i}", [128, 2, TT], BF16) for i in range(2)]
        sga_t = [sb(es, f"sga_t{i}", [128, 8, TT], BF16) for i in range(2)]
        sgb_t = [sb(es, f"sgb_t{i}", [128, 8, TT], BF16) for i in range(2)]
        x_t = [sb(es, f"x_t{i}", [128, 8, TT], F32) for i in range(2)]
        b_in = [[P.buf("d1in") for _ in range(5)] for _ in range(2)]
        merged = sb(es, "merged", [128, 8, TT], BF16)
        b_merged = P.buf("merged")
        m3 = sb(es, "m3", [128, 8, TT], F32)
        b_m3 = P.buf("m3")
        sq = sb(es, "sqD", [128, 8, TT], BF16)
        b_sq = P.buf("sqD")
        rs = sb(es, "rsD", [128, TT], F32)
        b_rs = P.buf("rsD")
        m1 = [sb(es, f"m1_{i}", [128, TT], F32) for i in range(2)]
        m2 = [sb(es, f"m2_{i}", [128, TT], F32) for i in range(2)]
        b_m1 = [P.buf("m1") for _ in range(2)]
        b_m2 = [P.buf("m2") for _ in range(2)]
        x1_t = [sb(es, f"x1_t{i}", [128, 8, TT], F32) for i in range(2)]
        b_x1 = [P.buf("x1t") for _ in range(2)]
        YAv = YA.rearrange("(kc p) t -> p kc t", p=128)
        OTv = OT.rearrange("(kc p) t -> p kc t", p=128)
        GAv = PT[4864:5888, :].rearrange("(kc p) t -> p kc t", p=128)
        GBv = PT[5888:6912, :].rearrange("(kc p) t -> p kc t", p=128)
        xTv = xT.rearrange("(kc p) t -> p kc t", p=128)
        X1v = X1.rearrange("(kc p) t -> p kc t", p=128)

        def loads(tt):
            s2 = tt % 2
            tsl = slice(tt * TT, (tt + 1) * TT)
            for j, (dst, src) in enumerate(((ya_t, YAv), (o_t, OTv), (sga_t, GAv), (sgb_t, GBv), (x_t, xTv))):
                P.dma("sp", dst[s2][:], src[:, :, tsl], b_in[s2][j], writes=[b_in[s2][j]])

        merged2 = [merged, sb(es, "merged_b", [128, 8, TT], BF16)]
        b_merged2 = [b_merged, P.buf("merged_b")]
        cnt = [0]

        def s1(tt):
            s2 = tt % 2
            mg, bmg = merged2[s2], b_merged2[s2]
            for jc in range(8):
                c = cnt[0]
                cnt[0] += 1
                pa, pb = c % 2, 2 + c % 2
                mi = c % 2
                js = slice(jc * 128, (jc + 1) * 128)
                for kc in range(6):
                    P.op("pe", lambda e, pa=pa, kc=kc, js=js, s2=s2: e.matmul(ps[pa][:], lhsT=wa[:, kc, js], rhs=ya_t[s2][:, kc, :],
                                                                           start=(kc == 0), stop=(kc == 5)),
                         reads=[b_wa, b_in[s2][0]], writes=[psb[pa]])
                for kc in range(2):
                    P.op("pe", lambda e, pb=pb, kc=kc, js=js, s2=s2: e.matmul(ps[pb][:], lhsT=wb[:, kc, js], rhs=o_t[s2][:, kc, :],
                                                                           start=(kc == 0), stop=(kc == 1)),
                         reads=[b_wb, b_in[s2][1]], writes=[psb[pb]])
                P.op("dve", lambda e, pa=pa, mi=mi, jc=jc, s2=s2: e.tensor_tensor(out=m1[mi][:], in0=ps[pa][:], in1=sga_t[s2][:, jc, :], op=ALU.mult),
                     reads=[psb[pa], b_in[s2][2]], writes=[b_m1[mi]])
                P.op("dve", lambda e, pb=pb, mi=mi, jc=jc, s2=s2: e.tensor_tensor(out=m2[mi][:], in0=ps[pb][:], in1=sgb_t[s2][:, jc, :], op=ALU.mult),
                     reads=[psb[pb], b_in[s2][3]], writes=[b_m2[mi]])
                P.op("dve", lambda e, mi=mi, jc=jc, mg=mg: e.tensor_tensor(out=mg[:, jc, :], in0=m1[mi][:], in1=m2[mi][:], op=ALU.add),
                     reads=[b_m1[mi], b_m2[mi]], writes=[bmg])

        def s2f(tt):
            s2 = tt % 2
            mg, bmg = merged2[s2], b_merged2[s2]
            for jc in range(8):
                po = 4 + jc % 2
                js = slice(jc * 128, (jc + 1) * 128)
                for kc in range(8):
                    P.op("pe", lambda e, po=po, kc=kc, js=js, mg=mg: e.matmul(ps[po][:], lhsT=wo[:, kc, js], rhs=mg[:, kc, :],
                                                                           start=(kc == 0), stop=(kc == 7)),
                         reads=[b_wo, bmg], writes=[psb[po]])
                P.op("act", lambda e, po=po, jc=jc: e.activation(out=m3[:, jc, :], in_=ps[po][:], func=AF.Copy), reads=[psb[po]], writes=[b_m3])
            P.op("act", lambda e: e.activation(out=sq[:], in_=m3[:], func=AF.Square), reads=[b_m3], writes=[b_sq])

        def s2b(tt):
            for kc in range(8):
                P.op("pe", lambda e, kc=kc: e.matmul(ps[6][:], lhsT=ones_bf[:], rhs=sq[:, kc, :], start=(kc == 0), stop=(kc == 7)),
                     reads=[b_sq, b_const], writes=[psb[6]])
            P.op("act", lambda e: e.activation(out=rs[:], in_=ps[6][:], func=AF.Ln, bias=eps_t[:], scale=1.0 / D),
                 reads=[psb[6], b_const], writes=[b_rs])
            P.op("act", lambda e: e.activation(out=rs[:], in_=rs[:], func=AF.Exp, scale=-0.5), reads=[b_rs], writes=[b_rs])

        def s3(tt):
            s2 = tt % 2
            tsl = slice(tt * TT, (tt + 1) * TT)
            for jc in range(8):
                mi = jc % 2
                P.op("dve", lambda e, mi=mi, jc=jc: e.scalar_tensor_tensor(out=m1[mi][:], in0=m3[:, jc, :], scalar=GM(jc), in1=rs[:],
                                                                         op0=ALU.mult, op1=ALU.mult),
                     reads=[b_m3, b_rs, b_coef], writes=[b_m1[mi]])
                P.op("dve", lambda e, mi=mi, jc=jc, s2=s2: e.tensor_tensor(out=x1_t[s2][:, jc, :], in0=m1[mi][:], in1=x_t[s2][:, jc, :], op=ALU.add),
                     reads=[b_m1[mi], b_in[s2][4]], writes=[b_x1[s2]])
            P.dma("sp", X1v[:, :, tsl], x1_t[s2][:], b_x1[s2], reads=[b_x1[s2]])

        loads(0)
        if NT > 1:
            loads(1)
        s1(0)
        for tt in range(NT):
            s2f(tt)
            if tt + 1 < NT:
                s1(tt + 1)
            s2b(tt)
            s3(tt)
            if tt + 2 < NT:
                loads(tt + 2)
        P.barrier()


def phaseD2(nc, P, sb, ps, psb, X1, outT, WFI, WFO, b_wpre, A_f, B_f, GF, ones_bf, eps_t, b_const, b_coef, b_modv, b_out):
    T2 = 256
    NT2 = S // T2
    NH = DFF // 128
    with ExitStack() as es:
        wfi = sb(es, "wfi", [128, 8, 2 * DFF], BF16)
        wfo = sb(es, "wfo", [128, NH, D], BF16)
        b_wfi, b_wfo = P.buf("wfi"), P.buf("wfo")
        WFIv = WFI.rearrange("(kc p) n -> p kc n", p=128)
        WFOv = WFO.rearrange("(kc p) n -> p kc n", p=128)
        for kc in range(8):
            P.dma("sp", wfi[:, kc, :], WFIv[:, kc, :], b_wfi, reads=[b_wpre], writes=[b_wfi])
        for k0 in range(0, NH, 11):
            P.dma("sp", wfo[:, k0:k0 + 11, :], WFOv[:, k0:k0 + 11, :], b_wfo, reads=[b_wpre], writes=[b_wfo])
        x1_t = [sb(es, f"x1f{i}", [128, 8, T2], F32) for i in range(2)]
        b_x1 = [P.buf("x1f") for _ in range(2)]
        sq = sb(es, "sqF", [128, 8, T2], BF16)
        b_sq = P.buf("sqF")
        rs = sb(es, "rsF", [128, T2], F32)
        b_rs = P.buf("rsF")
        tmpb = [sb(es, f"tmpF{i}", [128, T2], F32) for i in range(2)]
        b_tmp = [P.buf("tmpF") for _ in range(2)]
        h2 = sb(es, "h2", [128, 8, T2], BF16)
        b_h2 = P.buf("h2")
        su = [sb(es, f"su{i}", [128, T2], F32) for i in range(2)]
        b_su = [P.buf("su") for _ in range(2)]
        actT = sb(es, "actT", [128, NH, T2], BF16)
        b_act = P.buf("actT")
        f_t = sb(es, "f_t", [128, 8, T2], F32)
        b_f = P.buf("f_t")
        X1v = X1.rearrange("(kc p) t -> p kc t", p=128)
        outv = outT.rearrange("(kc p) t -> p kc t", p=128)
        h2b = [h2, sb(es, "h2b", [128, 8, T2], BF16)]
        b_h2b = [b_h2, P.buf("h2b")]
        sqE = sb(es, "sqE", [128, 8, T2], BF16)
        b_sqE = P.buf("sqE")
        rsE = sb(es, "rsE", [128, T2], F32)
        b_rsE = P.buf("rsE")
        cnt = [0]

        def load_x1(tt):
            P.dma("sp", x1_t[tt % 2][:], X1v[:, :, tt * T2:(tt + 1) * T2], b_x1[tt % 2], writes=[b_x1[tt % 2]])

        def pro_sq(tt):
            s2 = tt % 2
            xt = x1_t[s2]
            P.op("act", lambda e, xt=xt: e.activation(out=sq[:], in_=xt[:], func=AF.Square), reads=[b_x1[s2]], writes=[b_sq])

        def pro(tt, with_sq=True):
            s2 = tt % 2
            xt = x1_t[s2]
            hh, bhh = h2b[s2], b_h2b[s2]
            if with_sq:
                pro_sq(tt)
            for kc in range(8):
                P.op("pe", lambda e, kc=kc: e.matmul(ps[6][:, 0:T2], lhsT=ones_bf[:], rhs=sq[:, kc, :], start=(kc == 0), stop=(kc == 7)),
                     reads=[b_sq, b_const], writes=[psb[6]])
            P.op("act", lambda e: e.activation(out=rs[:], in_=ps[6][:, 0:T2], func=AF.Ln, bias=eps_t[:], scale=1.0 / D),
                 reads=[psb[6], b_const], writes=[b_rs])
            P.op("act", lambda e: e.activation(out=rs[:], in_=rs[:], func=AF.Exp, scale=-0.5), reads=[b_rs], writes=[b_rs])
            for kc in range(8):
                tb, btb = tmpb[kc % 2], b_tmp[kc % 2]
                P.op("dve", lambda e, kc=kc, tb=tb, xt=xt: e.scalar_tensor_tensor(out=tb[:], in0=xt[:, kc, :], scalar=A_f(kc), in1=rs[:],
                                                                               op0=ALU.mult, op1=ALU.mult),
                     reads=[b_x1[s2], b_rs, b_coef], writes=[btb])
                P.op("act", lambda e, kc=kc, tb=tb, hh=hh: e.activation(out=hh[:, kc, :], in_=tb[:], func=AF.Identity, bias=B_f(kc), scale=1.0),
                     reads=[btb, b_modv], writes=[bhh])

        def ug(tt, mid=None):
            s2 = tt % 2
            hh, bhh = h2b[s2], b_h2b[s2]
            for hc in range(NH):
                if mid is not None and hc == NH // 2:
                    mid()
                c = cnt[0]
                cnt[0] += 1
                pu, pg = c % 2, 2 + c % 2
                si = c % 2
                for kc in range(8):
                    P.op("pe", lambda e, pu=pu, kc=kc, hc=hc, hh=hh: e.matmul(ps[pu][:, 0:T2], lhsT=wfi[:, kc, hc * 128:(hc + 1) * 128], rhs=hh[:, kc, :],
                                                                           start=(kc == 0), stop=(kc == 7)),
                         reads=[b_wfi, bhh], writes=[psb[pu]])
                for kc in range(8):
                    P.op("pe", lambda e, pg=pg, kc=kc, hc=hc, hh=hh: e.matmul(ps[pg][:, 0:T2], lhsT=wfi[:, kc, DFF + hc * 128:DFF + (hc + 1) * 128], rhs=hh[:, kc, :],
                                                                           start=(kc == 0), stop=(kc == 7)),
                         reads=[b_wfi, bhh], writes=[psb[pg]])
                P.op("act", lambda e, pu=pu, si=si: e.activation(out=su[si][:], in_=ps[pu][:, 0:T2], func=AF.Silu), reads=[psb[pu]], writes=[b_su[si]])
                P.op("dve", lambda e, pg=pg, si=si, hc=hc: e.tensor_tensor(out=actT[:, hc, :], in0=ps[pg][:, 0:T2], in1=su[si][:], op=ALU.mult),
                     reads=[psb[pg], b_su[si]], writes=[b_act])

        def ff(tt):
            for jc in range(8):
                pf = 4 + jc % 2
                for hc in range(NH):
                    P.op("pe", lambda e, pf=pf, hc=hc, jc=jc: e.matmul(ps[pf][:, 0:T2], lhsT=wfo[:, hc, jc * 128:(jc + 1) * 128], rhs=actT[:, hc, :],
                                                                    start=(hc == 0), stop=(hc == NH - 1)),
                         reads=[b_wfo, b_act], writes=[psb[pf]])
                P.op("act", lambda e, pf=pf, jc=jc: e.activation(out=f_t[:, jc, :], in_=ps[pf][:, 0:T2], func=AF.Copy), reads=[psb[pf]], writes=[b_f])

        def epi(tt):
            s2 = tt % 2
            xt = x1_t[s2]
            P.op("act", lambda e: e.activation(out=sqE[:], in_=f_t[:], func=AF.Square), reads=[b_f], writes=[b_sqE])
            for kc in range(8):
                P.op("pe", lambda e, kc=kc: e.matmul(ps[7][:, 0:T2], lhsT=ones_bf[:], rhs=sqE[:, kc, :], start=(kc == 0), stop=(kc == 7)),
                     reads=[b_sqE, b_const], writes=[psb[7]])
            P.op("act", lambda e: e.activation(out=rsE[:], in_=ps[7][:, 0:T2], func=AF.Ln, bias=eps_t[:], scale=1.0 / D),
                 reads=[psb[7], b_const], writes=[b_rsE])
            P.op("act", lambda e: e.activation(out=rsE[:], in_=rsE[:], func=AF.Exp, scale=-0.5), reads=[b_rsE], writes=[b_rsE])
            for jc in range(8):
                tb, btb = tmpb[jc % 2], b_tmp[jc % 2]
                P.op("dve", lambda e, jc=jc, tb=tb: e.scalar_tensor_tensor(out=tb[:], in0=f_t[:, jc, :], scalar=GF(jc), in1=rsE[:],
                                                                        op0=ALU.mult, op1=ALU.mult),
                     reads=[b_f, b_rsE, b_coef], writes=[btb])
                P.op("dve", lambda e, jc=jc, tb=tb, xt=xt: e.tensor_tensor(out=xt[:, jc, :], in0=tb[:], in1=xt[:, jc, :], op=ALU.add),
                     reads=[btb], writes=[b_x1[s2]])
            P.dma("sp", outv[:, :, tt * T2:(tt + 1) * T2], xt[:], b_x1[s2], reads=[b_x1[s2]])

        load_x1(0)
        if NT2 > 1:
            load_x1(1)
        pro(0)
        for tt in range(NT2):
            if tt + 1 < NT2:
                ug(tt, mid=lambda tt=tt: pro_sq(tt + 1))
                pro(tt + 1, with_sq=False)
            else:
                ug(tt)
            ff(tt)
            epi(tt)
            if tt + 2 < NT2:
                load_x1(tt + 2)
        P.barrier()


def phaseB_rwkv(nc, P, sb, ps, psb, PT, YA, par, pcol, b_par, w2, a2, g2, w0row, tmasks, dbg=False, after_consts=None):
    CDEC = math.exp(-0.5)
    NP = 6
    with ExitStack() as es:
        identB = sb(es, "identB", [128, 128], BF16)
        TRI = sb(es, "TRI", [128, 256], F32)
        bones = sb(es, "bones", [128, 128], BF16)
        bonesF = sb(es, "bonesF", [128, 128], F32)
        mask4 = sb(es, "mask4", [128, 512], BF16)
        maskLT = sb(es, "maskLT", [128, 128], F32)
        gneps = sb(es, "gneps", [128, 1], F32)
        w0bc = sb(es, "w0bc", [128, 768], F32)
        lw = [sb(es, f"lw{i}", [128, 768], BF16) for i in range(3)]
        Sbd = sb(es, "Sbd", [128, NP, 128], BF16)
        mlev = sb(es, "mlev", [128, 7, 512], BF16)
        ident4 = sb(es, "ident4", [128, 4, 128], BF16)
        b_c = P.buf("constB")
        b_S = [P.buf("S") for _ in range(NP)]
        cw = lambda fn, rd=(): P.op("pool", fn, reads=[b_c] + list(rd), writes=[b_c])
        cw(lambda e: e.memset(identB[:], 1.0))
        cw(lambda e: e.affine_select(out=identB[:], in_=identB[:], pattern=[[-1, 128]], compare_op=ALU.is_equal, fill=0.0, base=0, channel_multiplier=1))
        cw(lambda e: e.memset(TRI[:], 1.0))
        cw(lambda e: e.affine_select(out=TRI[:, 0:128], in_=TRI[:, 0:128], pattern=[[1, 128]], compare_op=ALU.is_ge, fill=0.0, base=0, channel_multiplier=-1))
        cw(lambda e: e.affine_select(out=TRI[:, 128:256], in_=TRI[:, 128:256], pattern=[[1, 128]], compare_op=ALU.is_gt, fill=0.0, base=0, channel_multiplier=-1))
        cw(lambda e: e.memset(mask4[:], 1.0))
        for q4 in range(4):
            op_ = ALU.is_gt if q4 % 2 == 0 else ALU.is_ge
            cw(lambda e, q4=q4, op_=op_: e.affine_select(out=mask4[:, q4 * 128:(q4 + 1) * 128], in_=mask4[:, q4 * 128:(q4 + 1) * 128],
                                                         pattern=[[1, 128]], compare_op=op_, fill=0.0, base=0, channel_multiplier=-1))
        cw(lambda e: e.memset(maskLT[:], 1.0))
        cw(lambda e: e.affine_select(out=maskLT[:], in_=maskLT[:], pattern=[[-1, 128]], compare_op=ALU.is_gt, fill=0.0, base=0, channel_multiplier=1))
        cw(lambda e: e.memset(bones[:], 0.0))
        cw(lambda e: e.memset(bonesF[:], 0.0))
        for h in (0, 1):
            hs = slice(64 * h, 64 * h + 64)
            cw(lambda e, hs=hs: e.memset(bones[hs, hs], 1.0))
            cw(lambda e, hs=hs: e.memset(bonesF[hs, hs], 1.0 / 64))
        cw(lambda e: e.memset(gneps[:], 64e-5))
        cw(lambda e: e.memset(Sbd[:], 0.0))
        for j4 in range(4):
            cw(lambda e, j4=j4: e.tensor_copy(out=ident4[:, j4, :], in_=identB[:]))
        with ExitStack() as es2:
            st32 = sb(es2, "st32B", [128, 768], F32)
            b_st = P.buf("st32B")
            for i, (wd, nr) in enumerate(((w2, 64), (a2, 64), (g2, 128))):
                P.dma("sp", st32[0:nr, :], wd[:, :], b_st, writes=[b_st])
                P.op("dve", lambda e, i=i, nr=nr: e.tensor_copy(out=lw[i][0:nr, :], in_=st32[0:nr, :]), reads=[b_st], writes=[b_c])
            P.dma("sp", w0bc[:], w0row.partition_broadcast(128), b_st, reads=[b_st], writes=[b_c])
            st_b = sb(es2, "st32Bb", [128, 768], F32)
            hi_b = sb(es2, "hi96", [128, 768], BF16)
            b_w0 = P.buf("w0hl")
            P.op("dve", lambda e: e.memset(lw[0][64:128, :], 0.0), reads=[b_c], writes=[b_c])
            P.dma("sp", st_b[64:65, :], w0row[:, :], b_w0, writes=[b_w0])
            P.dma("sp", st_b[96:97, :], w0row[:, :], b_w0, writes=[b_w0])
            P.op("dve", lambda e: e.tensor_copy(out=lw[0][64:65, :], in_=st_b[64:65, :]), reads=[b_w0, b_c], writes=[b_c])
            P.op("dve", lambda e: e.tensor_copy(out=hi_b[96:97, :], in_=st_b[96:97, :]), reads=[b_w0], writes=[b_w0])
            P.op("dve", lambda e: e.tensor_copy(out=st32[96:97, :], in_=hi_b[96:97, :]), reads=[b_w0, b_st], writes=[b_st])
            P.op("dve", lambda e: e.tensor_tensor(out=lw[0][96:97, :], in0=st_b[96:97, :], in1=st32[96:97, :], op=ALU.subtract),
                 reads=[b_w0, b_st, b_c], writes=[b_c])
            with ExitStack() as es3:
                mst = sb(es3, "mst", [128, 7 * 512], F32)
                b_mst = P.buf("mst")
                P.dma("sp", mst[:], tmasks[:, :], b_mst, writes=[b_mst])
                P.op("dve", lambda e: e.tensor_copy(out=mlev[:].rearrange("p a b -> p (a b)"), in_=mst[:]), reads=[b_mst], writes=[b_c])
                P.barrier()
            P.barrier()
        b_c.const = True
        w2bf, a2bf, g2bf = lw
        if after_consts is not None:
            after_consts()

        GT = 256
        NG = S // GT
        rkv_g = [sb(es, f"rkv_g{i}", [128, 18, GT], BF16) for i in range(2)]
        twl_g = [sb(es, f"twl_g{i}", [128, GT], BF16) for i in range(2)]
        b_twl1 = P.buf("twl_ones")
        for i in range(2):
            P.op("pool", lambda e, i=i: e.memset(twl_g[i][64:128, :], 1.0), writes=[b_twl1])
        P.barrier()
        al_g = [sb(es, f"al_g{i}", [64, GT], BF16) for i in range(2)]
        sgl_g = [sb(es, f"sgl_g{i}", [128, GT], BF16) for i in range(2)]
        ya_g = [sb(es, f"ya_g{i}", [128, NP, GT], BF16) for i in range(2)]
        b_g = [[P.buf("grp") for _ in range(4)] for _ in range(2)]
        b_ya = [P.buf("ya_g") for _ in range(2)]
        NF, NH = 16, 70
        Ft = [sb(es, f"Ft{hp}", [128, NF, 128], F32) for hp in range(NP)]
        Ht = [sb(es, f"Ht{hp}", [128, NH, 128], BF16) for hp in range(NP)]
        bF = [[P.buf("F") for _ in range(NF)] for _ in range(NP)]
        bH = [[P.buf("H") for _ in range(NH)] for _ in range(NP)]
        PTv = PT[0:2304, :].rearrange("(c p) t -> p c t", p=128)
        YAv = YA.rearrange("(c p) t -> p c t", p=128)
        ps0b = ps[0][:].bitcast(BF16)

        def load_group(gi):
            s2 = gi % 2
            gsl = slice(gi * GT, (gi + 1) * GT)
            P.dma("sp", rkv_g[s2][:], PTv[:, :, gsl], b_g[s2][0], writes=[b_g[s2][0]])
            P.dma("sp", twl_g[s2][0:64, :], PT[2304:2368, gsl], b_g[s2][1], writes=[b_g[s2][1]])
            P.dma("sp", al_g[s2][:], PT[2368:2432, gsl], b_g[s2][2], writes=[b_g[s2][2]])
            P.dma("sp", sgl_g[s2][:], PT[2432:2560, gsl], b_g[s2][3], writes=[b_g[s2][3]])

        load_group(0)
        import os
        rr = [0]

        def tile_body(n):
            TPG = GT // 128
            gi, s2 = n // TPG, (n // TPG) % 2
            def pre():
                if n % TPG == 0 and gi + 1 < NG:
                    load_group(gi + 1)
            ts = slice((n % TPG) * 128, (n % TPG + 1) * 128)
            bg = b_g[s2]
            steps = []
            segs = []
            curh = [{}]

            def B(hp, k):
                cur = curh[0]
                if k not in cur:
                    cur[k] = rr[0] % 8
                    rr[0] += 1
                return cur[k]
            psbf = [ps[i][:].bitcast(BF16) for i in range(8)]

            segkind = {}

            def seg(kind="ps"):
                segs.append(len(steps))
                segkind[len(steps)] = kind
            F = lambda hp, i: Ft[hp][:, i, :]
            H = lambda hp, i: Ht[hp][:, i, :]
            H4 = lambda hp, i: Ht[hp][:, i:i + 4, :].rearrange("p a b -> p (a b)")
            rT = lambda hp: rkv_g[s2][:, hp, ts]
            kT = lambda hp: rkv_g[s2][:, 6 + hp, ts]
            vT = lambda hp: rkv_g[s2][:, 12 + hp, ts]
            cs = lambda hp: slice(hp * 128, (hp + 1) * 128)

            def add(eng, fn, rd, wr):
                steps.append((eng, fn, rd, wr))

            seg()
            add("pe", lambda hp: (lambda e: e.matmul(ps[B(hp, 0)][:, 0:128], lhsT=twl_g[s2][0:97, ts], rhs=w2bf[0:97, cs(hp)], start=True, stop=True)),
                lambda hp: [bg[1], b_c], lambda hp: [psb[B(hp, 0)]])
            add("pe", lambda hp: (lambda e: e.matmul(ps[B(hp, 0)][:, 128:256], lhsT=a2bf[0:64, cs(hp)], rhs=al_g[s2][:, ts], start=True, stop=True)),
                lambda hp: [bg[2], b_c], lambda hp: [psb[B(hp, 0)]])
            add("act", lambda hp: (lambda e: e.activation(out=F(hp, 1), in_=ps[B(hp, 0)][:, 0:128], func=AF.Sigmoid)),
                lambda hp: [psb[B(hp, 0)]], lambda hp: [bF[hp][1]])
            add("act", lambda hp: (lambda e: e.activation(out=F(hp, 2), in_=ps[B(hp, 0)][:, 128:256], func=AF.Sigmoid, bias=pcol("a0", hp), scale=1.0)),
                lambda hp: [psb[B(hp, 0)], b_par], lambda hp: [bF[hp][2]])
            seg()
            add("pe", lambda hp: (lambda e: e.matmul(ps[B(hp, 1)][:, 0:256], lhsT=F(hp, 1), rhs=TRI[:], start=True, stop=True)),
                lambda hp: [bF[hp][1], b_c], lambda hp: [psb[B(hp, 1)]])
            add("act", lambda hp: (lambda e: e.activation(out=Ft[hp][:, 4:6, :], in_=ps[B(hp, 1)][:, 0:256].rearrange("p (j c) -> p j c", c=128), func=AF.Exp, scale=-CDEC)),
                lambda hp: [psb[B(hp, 1)]], lambda hp: [bF[hp][4], bF[hp][5]])
            add("act", lambda hp: (lambda e: e.activation(out=F(hp, 6), in_=ps[B(hp, 1)][:, 0:128], func=AF.Exp, scale=CDEC)),
                lambda hp: [psb[B(hp, 1)]], lambda hp: [bF[hp][6]])
            add("dve", lambda hp: (lambda e: e.tensor_scalar(out=F(hp, 7), in0=F(hp, 6), scalar1=Ft[hp][:, 4, 127:128], scalar2=None, op0=ALU.mult)),
                lambda hp: [bF[hp][6], bF[hp][4]], lambda hp: [bF[hp][7]])
            seg()
            add("act", lambda hp: (lambda e: e.activation(out=H(hp, 0), in_=kT(hp), func=AF.Square, scale=pcol("k_k", hp))),
                lambda hp: [bg[0], b_par], lambda hp: [bH[hp][0]])
            add("pe", lambda hp: (lambda e: e.matmul(ps[B(hp, 1)][:, 256:384], lhsT=bones[:], rhs=H(hp, 0), start=True, stop=True)),
                lambda hp: [bH[hp][0], b_c], lambda hp: [psb[B(hp, 1)]])
            add("act", lambda hp: (lambda e: e.activation(out=F(hp, 8), in_=ps[B(hp, 1)][:, 256:384], func=AF.Ln)),
                lambda hp: [psb[B(hp, 1)], bF[hp][7]], lambda hp: [bF[hp][8]])
            seg("ew")
            add("act", lambda hp: (lambda e: e.activation(out=F(hp, 8), in_=F(hp, 8), func=AF.Exp, scale=-0.5)),
                lambda hp: [], lambda hp: [bF[hp][8]])
            add("dve", lambda hp: (lambda e: e.scalar_tensor_tensor(out=F(hp, 9), in0=kT(hp), scalar=pcol("k_k", hp), in1=F(hp, 8), op0=ALU.mult, op1=ALU.mult)),
                lambda hp: [bg[0], b_par, bF[hp][8]], lambda hp: [bF[hp][9]])
            add("dve", lambda hp: (lambda e: e.tensor_scalar(out=F(hp, 10), in0=F(hp, 2), scalar1=-1.0, scalar2=pcol("k_a", hp), op0=ALU.add, op1=ALU.mult)),
                lambda hp: [bF[hp][2], b_par], lambda hp: [bF[hp][10]])
            add("dve", lambda hp: (lambda e: e.scalar_tensor_tensor(out=F(hp, 10), in0=F(hp, 10), scalar=1.0, in1=kT(hp), op0=ALU.add, op1=ALU.mult)),
                lambda hp: [bg[0]], lambda hp: [bF[hp][10]])
            add("dve", lambda hp: (lambda e: e.scalar_tensor_tensor(out=F(hp, 11), in0=F(hp, 9), scalar=-1.0, in1=F(hp, 2), op0=ALU.mult, op1=ALU.mult)),
                lambda hp: [bF[hp][9], bF[hp][2]], lambda hp: [bF[hp][11]])
            add("dve", lambda hp: (lambda e: e.tensor_tensor(out=H(hp, 2), in0=F(hp, 9), in1=F(hp, 5), op=ALU.mult)),
                lambda hp: [bF[hp][9], bF[hp][5]], lambda hp: [bH[hp][2]])
            add("dve", lambda hp: (lambda e: e.tensor_tensor(out=H(hp, 3), in0=rT(hp), in1=F(hp, 4), op=ALU.mult)),
                lambda hp: [bg[0], bF[hp][4]], lambda hp: [bH[hp][3]])
            add("dve", lambda hp: (lambda e: e.tensor_tensor(out=Ht[hp][:, 4:6, :], in0=Ft[hp][:, 10:12, :],
                                                           in1=F(hp, 6).unsqueeze(1).to_broadcast([128, 2, 128]), op=ALU.mult)),
                lambda hp: [bF[hp][10], bF[hp][11], bF[hp][6]], lambda hp: [bH[hp][4], bH[hp][5]])
            add("dve", lambda hp: (lambda e: e.tensor_tensor(out=Ht[hp][:, 6:8, :], in0=Ft[hp][:, 10:12, :],
                                                           in1=F(hp, 7).unsqueeze(1).to_broadcast([128, 2, 128]), op=ALU.mult)),
                lambda hp: [bF[hp][10], bF[hp][11], bF[hp][7]], lambda hp: [bH[hp][6], bH[hp][7]])
            seg("ew")
            add("dve", lambda hp: (lambda e: e.scalar_tensor_tensor(out=H(hp, 1), in0=rT(hp), scalar=pcol("r_k", hp), in1=F(hp, 10), op0=ALU.mult, op1=ALU.mult)),
                lambda hp: [bg[0], b_par, bF[hp][10]], lambda hp: [bH[hp][1]])
            seg()
            add("pe", lambda hp: (lambda e: e.matmul(ps[B(hp, 1)][:, 384:512], lhsT=bones[:], rhs=H(hp, 1), start=True, stop=True)),
                lambda hp: [bH[hp][1], b_c], lambda hp: [psb[B(hp, 1)]])
            add("dve", lambda hp: (lambda e: e.tensor_tensor(out=F(hp, 15), in0=ps[B(hp, 1)][:, 384:512], in1=vT(hp), op=ALU.mult)),
                lambda hp: [psb[B(hp, 1)], bg[0]], lambda hp: [bF[hp][15]])
            seg()
            for j, src in enumerate((lambda hp: H(hp, 2), vT, lambda hp: H(hp, 6), lambda hp: H(hp, 7))):
                rdj = [lambda hp: [bH[hp][2]], lambda hp: [bg[0]], lambda hp: [bH[hp][6]], lambda hp: [bH[hp][7]]][j]
                add("pe", lambda hp, j=j, src=src: (lambda e: e.transpose(psbf[B(hp, 0)][:, j * 128:(j + 1) * 128], src(hp), identB[:])),
                    lambda hp, rdj=rdj: rdj(hp) + [b_c], lambda hp: [psb[B(hp, 0)]])
            add("act", lambda hp: (lambda e: e.activation(out=Ht[hp][:, 8:12, :], in_=psbf[B(hp, 0)][:, 0:512].rearrange("p (j c) -> p j c", c=128), func=AF.Copy)),
                lambda hp: [psb[B(hp, 0)]], lambda hp: [bH[hp][8], bH[hp][9], bH[hp][10], bH[hp][11]])
            seg()
            for h in (0, 1):
                hs = slice(64 * h, 64 * h + 64)
                xb = 2 + h
                mb = 4 + h
                add("pe", lambda hp, hs=hs, h=h: (lambda e: e.matmul(ps[B(hp, 2 + h)][:, 0:256], lhsT=Ht[hp][hs, 5, :],
                                                                    rhs=Ht[hp][hs, 2:4, :], start=True, stop=True)),
                    lambda hp: [bH[hp][5], bH[hp][2], bH[hp][3]], lambda hp, h=h: [psb[B(hp, 2 + h)]])
                add("pe", lambda hp, hs=hs, h=h: (lambda e: e.matmul(ps[B(hp, 2 + h)][:, 256:512], lhsT=Ht[hp][hs, 4, :],
                                                                    rhs=Ht[hp][hs, 2:4, :], start=True, stop=True)),
                    lambda hp: [bH[hp][4], bH[hp][2], bH[hp][3]], lambda hp, h=h: [psb[B(hp, 2 + h)]])
                add("pe", lambda hp, hs=hs, h=h: (lambda e: e.matmul(ps[B(hp, h)][:, 0:128], lhsT=Ht[hp][hs, 2, :],
                                                                    rhs=Ht[hp][hs, 5, :], start=True, stop=True)),
                    lambda hp: [bH[hp][5], bH[hp][2]], lambda hp, h=h: [psb[B(hp, h)]])
                xs0 = 12 + 4 * h
                add("dve", lambda hp, h=h, xs0=xs0: (lambda e: e.tensor_tensor(out=H4(hp, xs0), in0=ps[B(hp, 2 + h)][:], in1=mask4[:], op=ALU.mult)),
                    lambda hp, h=h: [psb[B(hp, 2 + h)], b_c], lambda hp, xs0=xs0: [bH[hp][xs0 + i] for i in range(4)])
                add("dve", lambda hp, h=h: (lambda e: e.tensor_tensor(out=H(hp, 28 + h), in0=ps[B(hp, h)][:, 0:128], in1=maskLT[:], op=ALU.mult)),
                    lambda hp, h=h: [psb[B(hp, h)], b_c], lambda hp, h=h: [bH[hp][28 + h]])
            seg("ew")
            TallV = lambda hp: Ht[hp][:, 20:24, :]
            bTall = lambda hp: [bH[hp][20 + i] for i in range(4)]
            XL = lambda hp, h, lvl: H(hp, 42 + 14 * h + lvl)
            bXL = lambda hp, h: [bH[hp][42 + 14 * h + i] for i in range(7)]
            add("dve", lambda hp: (lambda e: e.tensor_tensor(out=Ht[hp][:, 35:63:14, :], in0=Ht[hp][:, 12:20:4, :],
                                                           in1=mlev[:, 0, 0:128].unsqueeze(1).to_broadcast([128, 2, 128]), op=ALU.mult)),
                lambda hp: [bH[hp][12], bH[hp][16], b_c], lambda hp: [bH[hp][35], bH[hp][49]])
            add("dve", lambda hp: (lambda e: e.tensor_tensor(
                out=Ht[hp][:, 42:70, :].rearrange("p (h l) c -> p h l c", l=14)[:, :, 0:7, :],
                in0=Ht[hp][:, 28:30, :].unsqueeze(2).to_broadcast([128, 2, 7, 128]),
                in1=mlev[:, :, 128:256].unsqueeze(1).to_broadcast([128, 2, 7, 128]), op=ALU.mult)),
                lambda hp: [bH[hp][28], bH[hp][29], b_c], lambda hp: bXL(hp, 0) + bXL(hp, 1))
            add("dve", lambda hp: (lambda e: e.tensor_tensor(out=TallV(hp), in0=Ht[hp][:, 35:63:7, :], in1=ident4[:], op=ALU.add)),
                lambda hp: [b_c, bH[hp][35], bH[hp][49]] + bXL(hp, 0) + bXL(hp, 1), lambda hp: bTall(hp))
            for lvl in range(1, 7):
                seg("inv")
                for h in (0, 1):
                    add("pe", lambda hp, h=h, lvl=lvl: (lambda e: e.matmul(ps[B(hp, 2)][:, h * 128:(h + 1) * 128], lhsT=XL(hp, h, lvl), rhs=H(hp, 20 + 2 * h), start=True, stop=True)),
                        lambda hp, h=h: bXL(hp, h) + [bH[hp][20 + 2 * h]], lambda hp: [psb[B(hp, 2)]])
                add("act", lambda hp: (lambda e: e.activation(out=Ht[hp][:, 24:26, :], in_=ps[B(hp, 2)][:, 0:256].rearrange("p (j c) -> p j c", c=128), func=AF.Copy)),
                    lambda hp: [psb[B(hp, 2)]], lambda hp: [bH[hp][24], bH[hp][25]])
                seg("inv")
                add("pe", lambda hp: (lambda e: e.matmul(ps[B(hp, 3)][:], lhsT=identB[:], rhs=Ht[hp][:, 20:24, :], start=True, stop=False)),
                    lambda hp: bTall(hp) + [b_c], lambda hp: [psb[B(hp, 3)]])
                for h in (0, 1):
                    add("pe", lambda hp, h=h: (lambda e: e.matmul(ps[B(hp, 3)][:, (2 * h) * 128:(2 * h + 1) * 128], lhsT=H(hp, 21 + 2 * h), rhs=H(hp, 24 + h), start=False, stop=False)),
                        lambda hp, h=h: [bH[hp][21 + 2 * h], bH[hp][24 + h]], lambda hp: [psb[B(hp, 3)]])
                    add("pe", lambda hp, h=h: (lambda e: e.matmul(ps[B(hp, 3)][:, (2 * h + 1) * 128:(2 * h + 2) * 128], lhsT=H(hp, 24 + h), rhs=H(hp, 21 + 2 * h), start=False, stop=(h == 1))),
                        lambda hp, h=h: [bH[hp][21 + 2 * h], bH[hp][24 + h]], lambda hp: [psb[B(hp, 3)]])
                if lvl % 2 == 1:
                    add("dve", lambda hp: (lambda e: e.tensor_copy(out=TallV(hp), in_=ps[B(hp, 3)][:].rearrange("p (j c) -> p j c", c=128))),
                        lambda hp: [psb[B(hp, 3)]], lambda hp: bTall(hp))
                else:
                    add("act", lambda hp: (lambda e: e.activation(out=TallV(hp), in_=ps[B(hp, 3)][:].rearrange("p (j c) -> p j c", c=128), func=AF.Copy)),
                        lambda hp: [psb[B(hp, 3)]], lambda hp: bTall(hp))
            seg()
            TTs = lambda hp, h: H(hp, 20 + 2 * h)
            bTT = lambda hp, h: bH[hp][20 + 2 * h]
            for h in (0, 1):
                hs = slice(64 * h, 64 * h + 64)
                add("pe", lambda hp, h=h, hs=hs: (lambda e: e.matmul(ps[B(hp, 0)][:, 64 * h:64 * h + 64], lhsT=H(hp, 14 + 4 * h), rhs=Ht[hp][:, 9, hs], start=True, stop=True)),
                    lambda hp, h=h: [bH[hp][14 + 4 * h], bH[hp][9]], lambda hp: [psb[B(hp, 0)]])
            add("act", lambda hp: (lambda e: e.activation(out=H(hp, 32), in_=ps[B(hp, 0)][:, 0:128], func=AF.Copy)),
                lambda hp: [psb[B(hp, 0)]], lambda hp: [bH[hp][32]])
            seg()
            for h in (0, 1):
                hs = slice(64 * h, 64 * h + 64)
                add("pe", lambda hp, h=h, hs=hs: (lambda e: e.matmul(ps[B(hp, 0)][hs, 256:384], lhsT=Ht[hp][:, 8, hs], rhs=TTs(hp, h), start=True, stop=True)),
                    lambda hp, h=h: [bTT(hp, h), bH[hp][8]], lambda hp: [psb[B(hp, 0)]])
            add("act", lambda hp: (lambda e: e.activation(out=H(hp, 33), in_=ps[B(hp, 0)][:, 256:384], func=AF.Copy)),
                lambda hp: [psb[B(hp, 0)]], lambda hp: [bH[hp][33]])
            seg("pm")
            add("pe", lambda hp: (lambda e: e.matmul(ps[B(hp, 1)][:, 0:128], lhsT=H(hp, 33), rhs=Sbd[:, hp, :], start=True, stop=False)),
                lambda hp: [bH[hp][33], b_S[hp]], lambda hp: [psb[B(hp, 1)]])
            for h in (0, 1):
                hs = slice(64 * h, 64 * h + 64)
                add("pe", lambda hp, h=h, hs=hs: (lambda e: e.matmul(ps[B(hp, 1)][:, 64 * h:64 * h + 64], lhsT=TTs(hp, h), rhs=Ht[hp][:, 32, hs], start=False, stop=(h == 1))),
                    lambda hp, h=h: [bTT(hp, h), bH[hp][32]], lambda hp: [psb[B(hp, 1)]])
            add("dve", lambda hp: (lambda e: e.tensor_copy(out=H(hp, 34), in_=ps[B(hp, 1)][:, 0:128])),
                lambda hp: [psb[B(hp, 1)]], lambda hp: [bH[hp][34]])
            add("pe", lambda hp: (lambda e: e.matmul(ps[B(hp, 1)][:, 128:256], lhsT=Sbd[:, hp, :], rhs=H(hp, 3), start=True, stop=False)),
                lambda hp: [b_S[hp], bH[hp][3]], lambda hp: [psb[B(hp, 1)]])
            for h in (0, 1):
                hs = slice(64 * h, 64 * h + 64)
                add("pe", lambda hp, h=h, hs=hs: (lambda e: e.matmul(ps[B(hp, 1)][hs, 128:256], lhsT=Ht[hp][:, 34, hs], rhs=H(hp, 13 + 4 * h), start=False, stop=False)),
                    lambda hp, h=h: [bH[hp][34], bH[hp][13 + 4 * h]], lambda hp: [psb[B(hp, 1)]])
                add("pe", lambda hp, h=h, hs=hs: (lambda e: e.matmul(ps[B(hp, 1)][hs, 128:256], lhsT=Ht[hp][:, 9, hs], rhs=H(hp, 15 + 4 * h), start=False, stop=True)),
                    lambda hp, h=h: [bH[hp][9], bH[hp][15 + 4 * h]], lambda hp: [psb[B(hp, 1)]])
            add("pe", lambda hp: (lambda e: e.matmul(ps[B(hp, 1)][:, 256:384], lhsT=H(hp, 10), rhs=H(hp, 9), start=True, stop=False)),
                lambda hp: [bH[hp][10], bH[hp][9]], lambda hp: [psb[B(hp, 1)]])
            add("pe", lambda hp: (lambda e: e.matmul(ps[B(hp, 1)][:, 256:384], lhsT=H(hp, 11), rhs=H(hp, 34), start=False, stop=True)),
                lambda hp: [bH[hp][11], bH[hp][34]], lambda hp: [psb[B(hp, 1)]])
            add("act", lambda hp: (lambda e: e.activation(out=F(hp, 13), in_=ps[B(hp, 1)][:, 128:256], func=AF.Copy)),
                lambda hp: [psb[B(hp, 1)]], lambda hp: [bF[hp][13]])
            add("act", lambda hp: (lambda e: e.activation(out=F(hp, 14), in_=ps[B(hp, 1)][:, 128:256], func=AF.Square)),
                lambda hp: [psb[B(hp, 1)]], lambda hp: [bF[hp][14]])
            for h in (0, 1):
                hs = slice(64 * h, 64 * h + 64)
                add("dve", lambda hp, h=h, hs=hs: (lambda e: e.scalar_tensor_tensor(out=Sbd[hs, hp, hs], in0=Sbd[hs, hp, hs], scalar=Ft[hp][hs, 4, 127:128],
                                                                                 in1=ps[B(hp, 1)][hs, 256 + 64 * h:256 + 64 * h + 64], op0=ALU.mult, op1=ALU.add)),
                    lambda hp: [psb[B(hp, 1)], bF[hp][4]], lambda hp: [b_S[hp]])
            seg()
            add("pe", lambda hp: (lambda e: e.matmul(ps[B(hp, 1)][:, 0:256], lhsT=bonesF[:], rhs=Ft[hp][:, 13:15, :], start=True, stop=True)),
                lambda hp: [bF[hp][13], bF[hp][14], b_c], lambda hp: [psb[B(hp, 1)]])
            add("act", lambda hp: (lambda e: e.activation(out=F(hp, 1), in_=ps[B(hp, 1)][:, 0:128], func=AF.Square)),
                lambda hp: [psb[B(hp, 1)]], lambda hp: [bF[hp][1]])
            add("dve", lambda hp: (lambda e: e.tensor_tensor(out=F(hp, 5), in0=F(hp, 13), in1=ps[B(hp, 1)][:, 0:128], op=ALU.subtract)),
                lambda hp: [psb[B(hp, 1)], bF[hp][13]], lambda hp: [bF[hp][5]])
            add("dve", lambda hp: (lambda e: e.tensor_tensor(out=F(hp, 2), in0=ps[B(hp, 1)][:, 128:256], in1=F(hp, 1), op=ALU.subtract)),
                lambda hp: [psb[B(hp, 1)], bF[hp][1]], lambda hp: [bF[hp][2]])
            seg("ew")
            add("act", lambda hp: (lambda e: e.activation(out=F(hp, 2), in_=F(hp, 2), func=AF.Ln, bias=gneps[:], scale=1.0)),
                lambda hp: [b_c], lambda hp: [bF[hp][2]])
            add("act", lambda hp: (lambda e: e.activation(out=F(hp, 8), in_=F(hp, 2), func=AF.Exp, scale=-0.5)),
                lambda hp: [bF[hp][2]], lambda hp: [bF[hp][8]])
            add("dve", lambda hp: (lambda e: e.tensor_tensor(out=F(hp, 6), in0=F(hp, 5), in1=F(hp, 8), op=ALU.mult)),
                lambda hp: [bF[hp][5], bF[hp][8]], lambda hp: [bF[hp][6]])
            add("dve", lambda hp: (lambda e: e.scalar_tensor_tensor(out=F(hp, 9), in0=F(hp, 6), scalar=pcol("lnx_w", hp), in1=F(hp, 15), op0=ALU.mult, op1=ALU.add)),
                lambda hp: [bF[hp][6], bF[hp][15], b_par], lambda hp: [bF[hp][9]])
            seg()
            add("pe", lambda hp: (lambda e: e.matmul(ps[B(hp, 0)][:, 0:128], lhsT=g2bf[:, cs(hp)], rhs=sgl_g[s2][:, ts], start=True, stop=True)),
                lambda hp: [bg[3], b_c], lambda hp: [psb[B(hp, 0)]])
            add("dve", lambda hp: (lambda e: e.scalar_tensor_tensor(out=ya_g[s2][:, hp, ts], in0=F(hp, 9), scalar=pcol("lnx_b", hp), in1=ps[B(hp, 0)][:, 0:128], op0=ALU.add, op1=ALU.mult)),
                lambda hp: [psb[B(hp, 0)], bF[hp][9], b_par], lambda hp: [b_ya[s2]])

            bounds = sorted(set(segs + [0, len(steps)]))
            nseg = len(bounds) - 1

            def emit_one(eng, fn, rd, wr, hp):
                rec = _Rec()
                fn(hp)(rec)
                P.op(eng, lambda e, c=rec: getattr(e, c.name)(*c.args, **c.kwargs), reads=rd(hp), writes=wr(hp))

            def emit_seg(si, pairs):
                kind = segkind.get(bounds[si], "ps")
                ops = steps[bounds[si]:bounds[si + 1]]
                if kind in ("ew", "pm"):
                    dicts = {hp: {} for hp in pairs}
                    npp = len(pairs)
                    for dwave in range(len(ops) + npp - 1):
                        for pi, hp in enumerate(pairs):
                            st = dwave - pi
                            if 0 <= st < len(ops):
                                (eng, fn, rd, wr) = ops[st]
                                curh[0] = dicts[hp] if kind == "pm" else {}
                                emit_one(eng, fn, rd, wr, hp)
                else:
                    for hp in pairs:
                        curh[0] = {}
                        for (eng, fn, rd, wr) in ops:
                            emit_one(eng, fn, rd, wr, hp)

            def post():
                if n % TPG == TPG - 1:
                    P.dma("sp", YAv[:, :, gi * GT:(gi + 1) * GT], ya_g[s2][:], b_ya[s2], reads=[b_ya[s2]])
                if dbg and n == NTL - 1:
                    P.barrier()
                    DBGF = nc.dram_tensor("DBGF", [128, NF, 128], F32, kind="ExternalOutput").ap()
                    DBGH = nc.dram_tensor("DBGH", [128, NH, 128], BF16, kind="ExternalOutput").ap()
                    DBGS = nc.dram_tensor("DBGS", [128, NP, 128], BF16, kind="ExternalOutput").ap()
                    bd = P.buf("dbgd")
                    P.dma("sp", DBGF[:, :, :], Ft[0][:], bd)
                    P.dma("sp", DBGH[:, :, :], Ht[0][:], bd)
                    P.dma("sp", DBGS[:, :, :], Sbd[:], bd)

            kinds = [segkind.get(bounds[k], 'ps') for k in range(nseg)]
            return dict(pre=pre, post=post, nseg=nseg, emit_seg=emit_seg, kinds=kinds)

        NTL = int(os.environ.get('KNT', S // 128))
        descs = {}

        def get_desc(n):
            if n not in descs:
                descs[n] = tile_body(n)
            return descs[n]

        nseg0 = get_desc(0)["nseg"]
        LAG = 0
        total = NTL * nseg0
        G0, G1 = (0, 1, 2, 3, 4, 5), ()
        for i in range(total + LAG):
            if i < total:
                n, si = divmod(i, nseg0)
                dsc = get_desc(n)
                kinds = dsc["kinds"]
                if kinds[si] == "inv":
                    if si == 0 or kinds[si - 1] != "inv":
                        sj = si
                        while sj < nseg0 and kinds[sj] == "inv":
                            sj += 1
                        ninv = sj - si
                        for dwave in range(ninv + len(G0) - 1):
                            for pi, hp in enumerate(G0):
                                st = dwave - pi
                                if 0 <= st < ninv:
                                    dsc["emit_seg"](si + st, (hp,))
                else:
                    dsc["emit_seg"](si, G0)
            j = i - LAG
            if j >= 0:
                n, si = divmod(j, nseg0)
                dsc = get_desc(n)
                if si == 0:
                    dsc["pre"]()
                if G1:
                    dsc["emit_seg"](si, G1)
                if si == nseg0 - 1:
                    dsc["post"]()
                    if n - 1 in descs:
                        del descs[n - 1]

        P.barrier()
```
